# Optimizing a Trainium2 kernel written in Bass

```python
import math
import jax, jax.numpy as jnp
from jax import lax
import numpy as np

D_MODEL = 1024
BATCH = 8
SEQ = 2048
DEPTH = 1

D_MIX = D_MODEL
CONV_WIDTH = D_MIX // 2
CONV_GROUPS = 8
CONV_KERNEL = 31
GDN_WIDTH = D_MIX - CONV_WIDTH
GDN_HEAD_DIM = 128
GDN_HEADS = GDN_WIDTH // GDN_HEAD_DIM
GDN_SHORT_CONV = 4
GDN_CHUNK = 64
N_IN = 2 * CONV_WIDTH + 4 * GDN_WIDTH + 2 * GDN_HEADS
D_FF = -(-8 * D_MODEL // (3 * 256)) * 256
N_MOD = 6
EPS = 1e-6

kernel_name = "hybrid_conformer_gdn_adaln_block"


def rms_norm(x, w):
    xf = x.astype(jnp.float32)
    y = xf * lax.rsqrt(jnp.mean(xf * xf, axis=-1, keepdims=True) + EPS)
    return (y * w.astype(jnp.float32)).astype(x.dtype)


def modulate(h, shift, scale):
    return h * (1 + scale[:, None, :]) + shift[:, None, :]


def causal_depthwise_conv(x, w):
    k_len, ch = w.shape
    return lax.conv_general_dilated(
        x, w[:, None, :].astype(x.dtype), window_strides=(1,), padding=((k_len - 1, 0),),
        dimension_numbers=("NWC", "WIO", "NWC"), feature_group_count=ch)


def l2norm(t):
    return t * lax.rsqrt(jnp.sum(t * t, axis=-1, keepdims=True) + EPS)


def conformer_conv_group(a, gate, w_dw, b_dw, gn_w, gn_b):
    u = a * jax.nn.sigmoid(gate)
    u = causal_depthwise_conv(u, w_dw) + b_dw
    bsz, seq, ch = u.shape
    uf = u.astype(jnp.float32).reshape(bsz, seq, CONV_GROUPS, ch // CONV_GROUPS)
    mu = jnp.mean(uf, axis=-1, keepdims=True)
    var = jnp.mean(jnp.square(uf - mu), axis=-1, keepdims=True)
    un = ((uf - mu) * lax.rsqrt(var + EPS)).reshape(bsz, seq, ch)
    un = un * gn_w.astype(jnp.float32) + gn_b.astype(jnp.float32)
    return jax.nn.silu(un).astype(a.dtype)


def chunk_gated_delta_rule(q, k, v, g, beta):
    bsz, seq, nh, dk = q.shape
    dv = v.shape[-1]
    cl = GDN_CHUNK
    nc = seq // cl
    q = q * (dk ** -0.5)

    def to_chunks(t):
        return t.reshape(bsz, nc, cl, nh, -1).transpose(1, 0, 3, 2, 4)

    qc, kc, vc = to_chunks(q), to_chunks(k), to_chunks(v)
    gc = g.reshape(bsz, nc, cl, nh).transpose(1, 0, 3, 2)
    bc = beta.reshape(bsz, nc, cl, nh).transpose(1, 0, 3, 2)
    gcum = jnp.cumsum(gc, axis=-1)
    causal = jnp.tril(jnp.ones((cl, cl), dtype=bool))
    strict = jnp.tril(jnp.ones((cl, cl), dtype=bool), -1)
    diff = gcum[..., :, None] - gcum[..., None, :]
    decay = jnp.where(causal, jnp.exp(jnp.where(causal, diff, 0.0)), 0.0)

    k_beta = kc * bc[..., None]
    a_mat = jnp.where(strict, jnp.einsum("nbhcd,nbhed->nbhce", k_beta, kc) * decay, 0.0)
    eye = jnp.eye(cl, dtype=q.dtype)
    t_mat = lax.linalg.triangular_solve(eye + a_mat, jnp.broadcast_to(eye, a_mat.shape),
                                        left_side=True, lower=True, unit_diagonal=True)
    u = jnp.einsum("nbhce,nbhev->nbhcv", t_mat, vc * bc[..., None])
    w = jnp.einsum("nbhce,nbhek->nbhck", t_mat, k_beta * jnp.exp(gcum)[..., None])
    qk = jnp.where(causal, jnp.einsum("nbhcd,nbhed->nbhce", qc, kc) * decay, 0.0)
    q_dec = qc * jnp.exp(gcum)[..., None]
    k_dec = kc * jnp.exp(gcum[..., -1:] - gcum)[..., None]
    chunk_decay = jnp.exp(gcum[..., -1])

    def step(state, inp):
        q_i, k_i, w_i, u_i, qk_i, cd_i = inp
        v_new = u_i - jnp.einsum("bhck,bhkv->bhcv", w_i, state)
        o_i = jnp.einsum("bhck,bhkv->bhcv", q_i, state) + jnp.einsum("bhce,bhev->bhcv", qk_i, v_new)
        state = state * cd_i[..., None, None] + jnp.einsum("bhck,bhcv->bhkv", k_i, v_new)
        return state, o_i

    state0 = jnp.zeros((bsz, nh, dk, dv), dtype=q.dtype)
    _, o = lax.scan(step, state0, (q_dec, k_dec, w, u, qk, chunk_decay))
    return o.transpose(1, 0, 3, 2, 4).reshape(bsz, seq, nh, dv)


def gated_deltanet_group(q, k, v, z, beta_logit, alpha_logit, w_sc, a_log, dt_bias, norm_w):
    bsz, seq, _ = q.shape
    out_dtype = q.dtype
    qkv = jax.nn.silu(causal_depthwise_conv(jnp.concatenate([q, k, v], axis=-1), w_sc))
    q, k, v = jnp.split(qkv.astype(jnp.float32), 3, axis=-1)
    shp = (bsz, seq, GDN_HEADS, GDN_HEAD_DIM)
    q = l2norm(q.reshape(shp))
    k = l2norm(k.reshape(shp))
    v = v.reshape(shp)
    beta = jax.nn.sigmoid(beta_logit.astype(jnp.float32))
    g = -jnp.exp(a_log.astype(jnp.float32)) * jax.nn.softplus(
        alpha_logit.astype(jnp.float32) + dt_bias.astype(jnp.float32))
    o = chunk_gated_delta_rule(q, k, v, g, beta)
    o = o * lax.rsqrt(jnp.mean(o * o, axis=-1, keepdims=True) + EPS) * norm_w.astype(jnp.float32)
    o = o * jax.nn.silu(z.astype(jnp.float32).reshape(shp))
    return o.reshape(bsz, seq, GDN_WIDTH).astype(out_dtype)


def setup_inputs(seed: int = 0) -> dict:
    key = jax.random.key(seed)
    ks = jax.random.split(key, 24)
    f32 = jnp.float32
    nrm = lambda k, shp, s: jax.random.normal(k, shp, f32) * s
    dt = jnp.exp(jax.random.uniform(ks[12], (DEPTH, GDN_HEADS), f32,
                                    minval=math.log(1e-3), maxval=math.log(0.1)))
    return {
        "x": nrm(ks[0], (BATCH, SEQ, D_MODEL), 1.0),
        "c": nrm(ks[1], (BATCH, D_MODEL), 1.0),
        "w_ada": nrm(ks[2], (DEPTH, D_MODEL, N_MOD * D_MODEL), 0.5 * D_MODEL ** -0.5),
        "b_ada": nrm(ks[3], (DEPTH, N_MOD * D_MODEL), 0.02),
        "norm_mix_w": 1.0 + nrm(ks[4], (DEPTH, D_MODEL), 0.02),
        "w_in": nrm(ks[5], (DEPTH, D_MODEL, N_IN), D_MODEL ** -0.5),
        "conv_w": nrm(ks[6], (DEPTH, CONV_KERNEL, CONV_WIDTH), CONV_KERNEL ** -0.5),
        "conv_b": nrm(ks[7], (DEPTH, CONV_WIDTH), 0.02),
        "conv_gn_w": 1.0 + nrm(ks[8], (DEPTH, CONV_WIDTH), 0.02),
        "conv_gn_b": nrm(ks[9], (DEPTH, CONV_WIDTH), 0.02),
        "gdn_conv_w": nrm(ks[10], (DEPTH, GDN_SHORT_CONV, 3 * GDN_WIDTH), GDN_SHORT_CONV ** -0.5),
        "gdn_a_log": jnp.log(jax.random.uniform(ks[11], (DEPTH, GDN_HEADS), f32, minval=1.0, maxval=16.0)),
        "gdn_dt_bias": dt + jnp.log(-jnp.expm1(-dt)),
        "gdn_norm_w": 1.0 + nrm(ks[13], (DEPTH, GDN_HEAD_DIM), 0.02),
        "w_out": nrm(ks[14], (DEPTH, D_MIX, D_MODEL), D_MIX ** -0.5),
        "norm_ffn_w": 1.0 + nrm(ks[15], (DEPTH, D_MODEL), 0.02),
        "w_ffn_in": nrm(ks[16], (DEPTH, D_MODEL, 2 * D_FF), D_MODEL ** -0.5),
        "w_ffn_out": nrm(ks[17], (DEPTH, D_FF, D_MODEL), D_FF ** -0.5),
        "norm_final_w": 1.0 + nrm(ks[18], (D_MODEL,), 0.02),
    }


def reference(x, c, w_ada, b_ada, norm_mix_w, w_in, conv_w, conv_b, conv_gn_w, conv_gn_b,
              gdn_conv_w, gdn_a_log, gdn_dt_bias, gdn_norm_w, w_out, norm_ffn_w,
              w_ffn_in, w_ffn_out, norm_final_w):
    split_at = list(np.cumsum([CONV_WIDTH, CONV_WIDTH, GDN_WIDTH, GDN_WIDTH, GDN_WIDTH, GDN_WIDTH, GDN_HEADS]))
    c_act = jax.nn.silu(c)
    for layer in range(DEPTH):
        mod = c_act @ w_ada[layer] + b_ada[layer]
        sh1, sc1, gt1, sh2, sc2, gt2 = jnp.split(mod, N_MOD, axis=-1)

        h = modulate(rms_norm(x, norm_mix_w[layer]), sh1, sc1)
        p = h @ w_in[layer]
        a_in, a_gate, q, k, v, z, beta_logit, alpha_logit = jnp.split(p, split_at, axis=-1)
        out_a = conformer_conv_group(a_in, a_gate, conv_w[layer], conv_b[layer],
                                     conv_gn_w[layer], conv_gn_b[layer])
        out_b = gated_deltanet_group(q, k, v, z, beta_logit, alpha_logit, gdn_conv_w[layer],
                                     gdn_a_log[layer], gdn_dt_bias[layer], gdn_norm_w[layer])
        mix = jnp.concatenate([out_a, out_b], axis=-1) @ w_out[layer]
        x = x + gt1[:, None, :] * mix

        h = modulate(rms_norm(x, norm_ffn_w[layer]), sh2, sc2)
        f_gate, f_up = jnp.split(h @ w_ffn_in[layer], 2, axis=-1)
        ffn = (jax.nn.silu(f_gate) * f_up) @ w_ffn_out[layer]
        x = x + gt2[:, None, :] * ffn
    return rms_norm(x, norm_final_w)
```

```python
import contextlib
import numpy as np
import concourse.bass as bass
import concourse.mybir as mybir
from concourse.bass_utils import run_bass_kernel_spmd

F32 = mybir.dt.float32
BF16 = mybir.dt.bfloat16
AF = mybir.ActivationFunctionType
ALU = mybir.AluOpType

D = 1024
S = 2048
TB = 1024
NB = S // TB
NT = TB // 128
NS = TB // 512
N_IN = 3080
D_FF = 2816
EPS = 1e-6
ENGS = ("pe", "act", "dve", "pool", "sp")


class Res:
    __slots__ = ("name", "w", "r", "sem", "cnt", "excl")

    def __init__(self, name, excl=False):
        self.name = name
        self.excl = excl
        self.w = None
        self.r = []
        self.sem = None
        self.cnt = 0


class Prog:
    def __init__(self, nc):
        self.nc = nc
        self.ops = {e: [] for e in ENGS}
        self.dma_res = []
        self.bar = {e: [] for e in ENGS}

    def _deps(self, eng, reads, writes):
        deps = []
        for r in reads:
            if r.w is not None:
                deps.append(r.w)
            if r.excl:
                deps.extend(x for x in r.r if x[1] != eng)
        for r in writes:
            if r.w is not None:
                deps.append(r.w)
            deps.extend(r.r)
        if self.bar[eng]:
            deps.extend(self.bar[eng])
            self.bar[eng] = []
        return [d for d in deps if not (d[0] == "e" and d[1] == "pe" and eng == "pe")]

    def op(self, eng, fn, reads=(), writes=()):
        deps = self._deps(eng, reads, writes)
        idx = len(self.ops[eng])
        self.ops[eng].append(dict(fn=fn, deps=deps, sig=False, dma=None))
        ref = ("e", eng, idx)
        for r in reads:
            r.r.append(ref)
        for r in writes:
            r.w = ref
            r.r = []
        return ref

    def dma(self, eng, fns, key, reads=(), writes=()):
        deps = self._deps(eng, reads, writes)
        if key.sem is None:
            self.dma_res.append(key)
            key.sem = True
        for i, fn in enumerate(fns):
            key.cnt += 16
            self.ops[eng].append(dict(fn=fn, deps=deps if i == 0 else [], sig=False, dma=key))
        ref = ("d", key, key.cnt)
        for r in reads:
            r.r.append(ref)
        for r in writes:
            r.w = ref
            r.r = []
        return ref

    def snapshot(self):
        last = []
        for e in ENGS:
            for i in range(len(self.ops[e]) - 1, -1, -1):
                if self.ops[e][i]["dma"] is None:
                    last.append(("e", e, i))
                    break
        return last

    def apply(self, snap):
        for e in ENGS:
            self.bar[e] = self.bar[e] + list(snap)

    def barrier(self):
        last = []
        for e in ENGS:
            for i in range(len(self.ops[e]) - 1, -1, -1):
                if self.ops[e][i]["dma"] is None:
                    last.append(("e", e, i))
                    break
        for k in self.dma_res:
            last.append(("d", k, k.cnt))
        for e in ENGS:
            self.bar[e] = list(last)

    def emit(self, final_refs):
        nc = self.nc
        allops = self.ops
        for e in ENGS:
            for o in allops[e]:
                for d in o["deps"]:
                    if d[0] == "e":
                        allops[d[1]][d[2]]["sig"] = True
        counts = {}
        for e in ENGS:
            c = 0
            lst = []
            for o in allops[e]:
                if o["sig"]:
                    c += 1
                lst.append(c)
            counts[e] = lst
        with contextlib.ExitStack() as st:
            esem = {e: st.enter_context(nc.semaphore("s_" + e)) for e in ENGS}
            for k in self.dma_res:
                k.sem = st.enter_context(nc.semaphore("d_" + k.name))
            block = st.enter_context(nc.Block())

            def run(e, engine):
                waited = {}

                def do_waits(deps):
                    need = {}
                    for d in deps:
                        if d[0] == "e":
                            s_, v = esem[d[1]], counts[d[1]][d[2]]
                        else:
                            s_, v = d[1].sem, d[2]
                        if v > need.get(s_, 0):
                            need[s_] = v
                    for s_, v in need.items():
                        if waited.get(s_, 0) >= v:
                            continue
                        waited[s_] = v
                        engine.wait_ge(s_, v)

                for o in allops[e]:
                    do_waits(o["deps"])
                    ins = o["fn"](engine)
                    if o["dma"] is not None:
                        ins.then_inc(o["dma"].sem, 16)
                    elif o["sig"]:
                        ins.then_inc(esem[e], 1)
                if e == "sp":
                    do_waits(final_refs)

            @block.tensor
            def _(eng):
                run("pe", eng)

            @block.scalar
            def _(eng):
                run("act", eng)

            @block.vector
            def _(eng):
                run("dve", eng)

            @block.gpsimd
            def _(eng):
                run("pool", eng)

            @block.sync
            def _(eng):
                run("sp", eng)


def L(name, *a, **k):
    return lambda e: getattr(e, name)(*a, **k)


class _Stop(Exception):
    pass


def build(debug=None, stage=99):
    nc = bass.Bass("TRN2", target_bir_lowering=False)
    dt_in = lambda name, shape: nc.dram_tensor(name, shape, F32, kind="ExternalInput").ap()
    x_d = dt_in("x", [S, D])
    prow_d = dt_in("prow", [84, 128])
    cw31_d = dt_in("cw31", [124, 128])
    cw4_d = dt_in("cw4", [48, 128])
    brow_d = dt_in("brow", [1, 6144])
    nfw_d = dt_in("nfw", [1, D])
    gnw_d = dt_in("gnw", [1, 128])
    alog_d = dt_in("alog", [1, 4])
    dtb_d = dt_in("dtb", [1, 4])
    wada_d = dt_in("w_ada", [D, 6 * D])
    win_d = dt_in("w_in", [D, N_IN])
    wout_d = dt_in("w_out", [D, D])
    wfi_d = dt_in("w_ffn_in", [D, 2 * D_FF])
    wfo_d = dt_in("w_ffn_out", [D_FF, D])
    y_d = nc.dram_tensor("y", [S, D], F32, kind="ExternalOutput").ap()
    P = Prog(nc)
    with contextlib.ExitStack() as st:
        def sb(name, shape, dt=F32):
            return st.enter_context(nc.sbuf_tensor(name, shape, dt))

        X = sb("X", [128, NT, D], F32)
        hT = sb("hT", [128, 8, TB], BF16)
        NW = 3
        wring = [sb(f"wr{i}", [128, 4096], BF16) for i in range(NW)]
        A1N = 22 * 1024 + 256
        arena1 = sb("arena1", [128, A1N], BF16)
        R2N = 12 * 4 * 128 + 31 * 128 + NT * 512 + NT * 16 + 1024 + 10 * 512 + 14 * 512
        region2 = sb("region2", [128, R2N], BF16)
        ident_bf = sb("ident_bf", [128, 128], BF16)
        nident_bf = sb("nident_bf", [128, 128], BF16)
        ones_bf = sb("ones_bf", [128, 128], BF16)
        su_bf = sb("su_bf", [128, 128], BF16)
        gm_bf = sb("gm_bf", [128, 128], BF16)
        mk_bf = sb("mk_bf", [128, 5, 128], BF16)
        ident32 = sb("ident32", [128, 128], F32)
        ones32 = sb("ones32", [128, 128], F32)
        nones32 = sb("nones32", [128, 128], F32)
        tri32 = sb("tri32", [128, 128], F32)
        nm32 = sb("nm32", [128, 128], F32)
        imp32 = sb("imp32", [128, 128], F32)
        pcol = sb("pcol", [128, 84], F32)
        cw31c = sb("cw31c", [128, 124], F32)
        cw4c = sb("cw4c", [128, 48], F32)
        mcol = sb("mcol", [128, 48], F32)
        bcen = sb("bcen", [128, 8], F32)
        cact_bf = sb("cact_bf", [128, 8], BF16)
        cactbc = sb("cactbc", [128, 8, 128], BF16)
        gt_bc = sb("gt_bc", [128, 2, D], F32)
        nfw_bc = sb("nfw_bc", [128, D], F32)
        gnw_bc = sb("gnw_bc", [128, 128], F32)
        abc = sb("abc", [128, 4], F32)
        dtb_bc = sb("dtb_bc", [128, 4], F32)
        S32 = sb("S32", [128, 4, 128], F32)
        Sbf = sb("Sbf", [128, 4, 128], BF16)
        rawtail = sb("rawtail", [128, 12, 4], BF16)
        utail = sb("utail", [128, 4, 32], BF16)
        wlg = sb("wlg", [128, 8, 8], BF16)
        tmpf = [sb(f"tmpf{i}", [128, 512], F32) for i in range(2)]
        tmpb = [sb(f"tmpb{i}", [128, 512], BF16) for i in range(2)]
        xnb = [sb(f"xnb{i}", [128, D], BF16) for i in range(2)]
        ncol = [sb(f"ncol{i}", [128, 4], F32) for i in range(2)]
        scb = sb("scb", [128, 16, NT * 4], F32)
        junk = sb("junk", [128, 128], BF16)
        Rjunk = Res("junk")
        banks = [st.enter_context(nc.psum_tensor(f"bk{i}", [128, 512], F32)) for i in range(8)]
        RB = [Res(f"bk{i}", excl=True) for i in range(8)]

        def bkbf(i):
            return banks[i][:].bitcast(BF16)

        RX = [Res(f"X{t}") for t in range(NT)]
        RHa = [Res(f"hTa{t}") for t in range(NT)]
        RHb = [Res(f"hTb{t}") for t in range(NT)]
        RW = [Res(f"wr{i}") for i in range(NW)]
        Rc = Res("consts")
        Rp = Res("params")
        RS = [Res(f"S{h}") for h in range(4)]
        RSB = [Res(f"Sb{h}") for h in range(4)]
        Rtf = [Res(f"tmpf{i}") for i in range(2)]
        Rtb = [Res(f"tmpb{i}") for i in range(2)]
        Rxn = [Res(f"xnb{i}") for i in range(2)]
        Rnc = [Res(f"ncol{i}") for i in range(2)]
        Rsc = Res("sc")
        Rms = [Res("ms0"), Res("ms1")]
        Rtail = Res("tails")
        Rwlg = Res("wlg")

        o = 0
        def carve(n):
            nonlocal o
            v = arena1[:, o:o + n]
            o += n
            return v
        UW = 32 + TB
        u_v = carve(4 * UW).rearrange("p (c t) -> p c t", c=4)
        RAWW = 4 + TB
        raw_v = [carve(RAWW) for _ in range(2)]
        tmpg = [raw_v[i][:, 0:1024].bitcast(F32) for i in range(2)]
        qfm = carve(4 * TB).rearrange("p (c t) -> p c t", c=4)
        kfm = carve(4 * TB).rearrange("p (c t) -> p c t", c=4)
        ktm = carve(NT * 512).rearrange("p (t f) -> p t f", t=NT)
        vtm = carve(NT * 512).rearrange("p (t f) -> p t f", t=NT)
        assert o <= A1N, o
        actT = arena1[:, 0:22 * TB].rearrange("p (c t) -> p c t", c=22)
        RU = [Res(f"u{c}") for c in range(4)]
        RRAW = [Res(f"raw{i}") for i in range(2)]
        Rtg = RRAW
        RQ = [[Res(f"q{h}_{t}") for t in range(NT)] for h in range(4)]
        RK = [[Res(f"k{h}_{t}") for t in range(NT)] for h in range(4)]
        RKT = [Res(f"kt{t}") for t in range(NT)]
        RVT = [Res(f"vt{t}") for t in range(NT)]
        RACT = [Res(f"act{j}") for j in range(22)]

        o2 = 0
        def carve2(n):
            nonlocal o2
            v = region2[:, o2:o2 + n]
            o2 += n
            return v
        W4 = carve2(12 * 4 * 128).rearrange("p (f k c) -> p f k c", f=12, k=4)
        Wc = carve2(31 * 128).rearrange("p (k c) -> p k c", k=31)
        zs = carve2(NT * 512).rearrange("p (t f) -> p t f", t=NT)
        lg = carve2(NT * 8 * 2).bitcast(F32).rearrange("p (t f) -> p t f", t=NT)
        def c2h():
            return carve2(512).rearrange("p (h t) -> p h t", h=4)

        class Chain:
            pass
        CH = [Chain(), Chain()]
        oa = [0]
        def carveA(n):
            v = arena1[:, oa[0]:oa[0] + n]
            oa[0] += n
            return v
        ow = [0]
        def carveW(n):
            v = region2[:, ow[0]:ow[0] + n]
            ow[0] += n
            return v
        for ci, cv in enumerate([carve2, carveW]):
            B = CH[ci]
            B.gtri = cv(1024).bitcast(F32).rearrange("p (h t) -> p h t", h=4)
            for nm in ["dti", "egb", "A0", "A0T", "A1", "AT1", "qsq", "ksq", "Zb", "ZTb"]:
                setattr(B, nm, cv(512).rearrange("p (h t) -> p h t", h=4))
            B.R = {nm: Res(f"c{ci}{nm}") for nm in ["gtri", "dti", "egb", "A0", "A0T", "A1", "AT1", "qsq", "ksq", "Zb", "ZTb"]}
            B.bx, B.by = (0, 3) if ci == 0 else (4, 5)
        assert ow[0] <= 12 * 4 * 128
        Zfin = [c2h() for _ in range(4)]
        qkdt = [c2h() for _ in range(4)]
        RZF = [Res(f"zfin{i}") for i in range(4)]
        RQK = [Res(f"qkdt{i}") for i in range(4)]
        Vr = c2h()
        wz = c2h()
        rhs2 = c2h()
        Ybf = c2h()
        Ykbf = c2h()
        outb = c2h()
        assert o2 <= R2N, o2
        wfo = region2[:, 0:22 * 1024].rearrange("p (c n) -> p c n", c=22)
        RW4 = [[Res(f"W4_{f}_{k}") for k in range(4)] for f in range(12)]
        RWc = Res("Wc")
        RWck = [Res(f"Wc{k}") for k in range(31)]
        RZ = [Res(f"zs{t}") for t in range(NT)]
        RLG = [Res(f"lg{t}") for t in range(NT)]
        Rg = {k: Res(k) for k in ["Vr", "wz", "Y", "Yk", "outb"]}
        Rrhs2 = [Res(f"rhs2{h}") for h in range(4)]
        RWFO = [Res(f"wfo{i}") for i in range(6)]

        specs = []
        def win_group(g):
            src = win_d.rearrange("(c p) n -> p c n", p=128)[:, :, g * 512:(g + 1) * 512]
            return lambda slot: [(slot[:, 0:4096].rearrange("p (c n) -> p c n", c=8), src)]
        def wout_group(i):
            src = wout_d.rearrange("(c p) n -> p c n", p=128)[:, 4 * i:4 * i + 4, :]
            return lambda slot: [(slot[:, 0:4096].rearrange("p (c n) -> p c n", c=4), src)]
        def wfi_unit(u_):
            v = wfi_d.rearrange("(c p) n -> p c n", p=128)
            return lambda slot: [(slot[:, 0:2048].rearrange("p (c n) -> p c n", c=8), v[:, :, u_ * 256:(u_ + 1) * 256]),
                                 (slot[:, 2048:4096].rearrange("p (c n) -> p c n", c=8), v[:, :, D_FF + u_ * 256:D_FF + (u_ + 1) * 256])]
        in_order = [5, 0, 1, 2, 3, 4]
        def wada_group(pc, half):
            src = wada_d.rearrange("(c p) n -> p c n", p=128)[:, :, pc * D + half * 512:pc * D + (half + 1) * 512]
            return lambda slot: [(slot[:, 0:4096].rearrange("p (c n) -> p c n", c=8), src)]
        LATE = [(3, "col", 2, 24), (4, "col", 3, 32), (2, "gt", 0, 0), (5, "gt", 1, 0)]
        for blk in range(NB):
            for gi_, g in enumerate(in_order):
                specs.append(win_group(g))
                if blk == 0 and gi_ == 5:
                    for pc, _k, _i, _b in LATE:
                        for half in range(2):
                            specs.append(wada_group(pc, half))
            specs.append(wout_group(0))
            specs.append(wout_group(1))
            for u_ in range(11):
                specs.append(wfi_unit(u_))
        wstate = dict(issued=0, released=0, next=0)

        def wpump():
            while wstate["issued"] < len(specs) and wstate["issued"] - NW < wstate["released"]:
                j = wstate["issued"]
                slot = wring[j % NW]
                pairs = specs[j](slot)
                P.dma("pool", [L("dma_start", out=d_, in_=s_) for d_, s_ in pairs], RW[j % NW], writes=[RW[j % NW]])
                wstate["issued"] += 1

        def wnext():
            i = wstate["next"]
            wstate["next"] += 1
            wpump()
            assert wstate["issued"] > i
            return wring[i % NW], RW[i % NW]

        def wrel(n=1):
            wstate["released"] += n
            wpump()

        prow = arena1[:, 0:256].bitcast(F32)
        cw31r = arena1[:, 256:512].bitcast(F32)
        cw4r = arena1[:, 512:768].bitcast(F32)
        Rst = Res("stage")
        Rst2 = Res("stage2")
        Rst3 = Res("stage3")
        P.dma("sp", [L("dma_start", out=prow[0:84, :], in_=prow_d)], Rst, writes=[Rst])
        P.dma("sp", [L("dma_start", out=cw31r[0:124, :], in_=cw31_d)], Rst2, writes=[Rst2])
        P.dma("sp", [L("dma_start", out=cw4r[0:48, :], in_=cw4_d)], Rst3, writes=[Rst3])
        Rbc = Res("bcasts")
        Rgt = Res("gt")
        Rm2 = Res("mod2")
        P.dma("sp", [L("dma_start", out=nfw_bc[:], in_=nfw_d.partition_broadcast(128)),
                     L("dma_start", out=gnw_bc[:], in_=gnw_d.partition_broadcast(128)),
                     L("dma_start", out=abc[:], in_=alog_d.partition_broadcast(128)),
                     L("dma_start", out=dtb_bc[:], in_=dtb_d.partition_broadcast(128)),
                     L("dma_start", out=gt_bc[:, 0, :], in_=brow_d[:, 2 * D:3 * D].partition_broadcast(128)),
                     L("dma_start", out=gt_bc[:, 1, :], in_=brow_d[:, 5 * D:6 * D].partition_broadcast(128))],
              Rbc, writes=[Rbc])
        wad = [region2[:, 0:8192].rearrange("p (c n) -> p c n", c=8), region2[:, 8192:16384].rearrange("p (c n) -> p c n", c=8)]
        Rwad = [Res("wad0"), Res("wad1")]
        wada_v = wada_d.rearrange("(c p) n -> p c n", p=128)
        for i in range(2):
            P.dma("pool", [L("dma_start", out=wad[i], in_=wada_v[:, :, i * D:(i + 1) * D])], Rwad[i], writes=[Rwad[i]])
        for t in range(NT):
            P.dma("sp", [L("dma_start", out=X[:, t, :], in_=x_d[t * 128:(t + 1) * 128, :])], RX[t], reads=(Rwad if t >= 2 else []), writes=[RX[t]])
        def mk_mask(t, cmp, fill, base_val=1.0, step=-1, cm=1, eng="pool"):
            P.op(eng, L("memset", t[:], base_val), writes=[Rc])
            P.op(eng, L("affine_select", out=t[:], in_=t[:], pattern=[[step, 128]], compare_op=cmp, fill=fill, base=0,
                        channel_multiplier=cm), reads=[Rc], writes=[Rc])
        mk_mask(ident32, ALU.is_equal, 0.0)
        mk_mask(tri32, ALU.is_ge, 0.0, step=1, cm=-1)
        mk_mask(nm32, ALU.is_ge, -1e30, base_val=0.0, step=1, cm=-1)
        P.op("pool", L("memset", ones32[:], 1.0), writes=[Rc])
        P.op("pool", L("memset", nones32[:], -1.0), writes=[Rc])
        P.op("pool", L("memset", ones_bf[:], 1.0), writes=[Rc])
        P.op("pool", L("tensor_copy", ident_bf[:], ident32[:]), reads=[Rc], writes=[Rc])
        P.op("pool", L("tensor_scalar", nident_bf[:], ident32[:], -1.0, None, op0=ALU.mult), reads=[Rc], writes=[Rc])
        P.op("pool", L("tensor_tensor", out=su_bf[:], in0=tri32[:], in1=ident32[:], op=ALU.subtract), reads=[Rc], writes=[Rc])
        P.op("pool", L("memset", imp32[:], 0.0), writes=[Rc])
        P.op("pool", L("memset", imp32[0:64, 0:64], 1.0 / 64), writes=[Rc])
        P.op("pool", L("memset", imp32[64:128, 64:128], 1.0 / 64), writes=[Rc])
        P.op("pool", L("tensor_copy", gm_bf[:], imp32[:]), reads=[Rc], writes=[Rc])
        P.op("pool", L("tensor_tensor", out=imp32[:], in0=ident32[:], in1=imp32[:], op=ALU.subtract), reads=[Rc], writes=[Rc])
        P.op("pool", L("memset", S32[:], 0.0), writes=RS)
        P.op("pool", L("memset", Sbf[:], 0.0), writes=RSB)
        P.op("pool", L("memset", rawtail[:], 0.0), writes=[Rtail])
        P.op("pool", L("memset", utail[:], 0.0), writes=[Rtail])

        bdf = [arena1[:, 2048 + i * 256:2048 + (i + 1) * 256].bitcast(F32) for i in range(4)]
        for i, s_ in enumerate([8, 16, 32, 64]):
            v = bdf[i].rearrange("p (a b) -> p a b", b=s_)
            P.op("pool", L("memset", bdf[i], 1.0), writes=[Rc])
            P.op("pool", L("affine_select", out=v, in_=v, pattern=[[-s_, 128 // s_], [0, s_]], compare_op=ALU.is_ge, fill=0.0, base=0,
                           channel_multiplier=1), reads=[Rc], writes=[Rc])
            P.op("pool", L("affine_select", out=v, in_=v, pattern=[[s_, 128 // s_], [0, s_]], compare_op=ALU.is_ge, fill=0.0, base=s_ - 1,
                           channel_multiplier=-1), reads=[Rc], writes=[Rc])
        P.op("pool", L("tensor_copy", mk_bf[:, 0, :], bdf[0]), reads=[Rc], writes=[Rc])
        for i in range(3):
            P.op("pool", L("tensor_tensor", out=mk_bf[:, 1 + i, :], in0=bdf[i + 1], in1=bdf[i], op=ALU.subtract), reads=[Rc], writes=[Rc])
        P.op("pool", L("tensor_tensor", out=mk_bf[:, 4, :], in0=ones32[:], in1=bdf[3], op=ALU.subtract), reads=[Rc], writes=[Rc])
        P.dma("pool", [L("dma_start", out=wlg[:], in_=win_d.rearrange("(c p) n -> p c n", p=128)[:, :, 3072:3080])],
              Rwlg, writes=[Rwlg])
        wpump()
        P.op("pe", L("transpose", banks[0][:, 0:84], prow[0:84, :], ident32[0:84, 0:84]), reads=[Rst, Rc], writes=[RB[0]])
        P.op("pe", L("transpose", banks[0][:, 128:252], cw31r[0:124, :], ident32[0:124, 0:124]), reads=[Rst2, Rc], writes=[RB[0]])
        P.op("pe", L("transpose", banks[0][:, 256:304], cw4r[0:48, :], ident32[0:48, 0:48]), reads=[Rst3, Rc], writes=[RB[0]])
        P.op("dve", L("tensor_copy", pcol[:], banks[0][:, 0:84]), reads=[RB[0]], writes=[Rp])
        P.op("dve", L("tensor_scalar", cw31c[:], banks[0][:, 128:252], 0.5, None, op0=ALU.mult), reads=[RB[0]], writes=[Rp])
        P.op("dve", L("tensor_copy", cw4c[:], banks[0][:, 256:304]), reads=[RB[0]], writes=[Rp])
        P.op("act", L("activation", cact_bf[:], pcol[:, 48:56], AF.Silu), reads=[Rp], writes=[Rp])
        P.op("dve", L("tensor_copy", cactbc[:], cact_bf[:].unsqueeze(2).to_broadcast([128, 8, 128])), reads=[Rp], writes=[Rp])
        P.op("act", L("activation", abc[:], abc[:], AF.Exp), reads=[Rbc], writes=[Rbc])
        P.op("dve", L("tensor_scalar", abc[:], abc[:], -1.0, None, op0=ALU.mult), reads=[Rbc], writes=[Rbc])
        P.op("pe", L("matmul", banks[1][:, 0:4], lhsT=imp32[:], rhs=pcol[:, 72:76], start=True, stop=True), reads=[Rc, Rp], writes=[RB[1]])
        P.op("dve", L("tensor_copy", bcen[:, 0:4], banks[1][:, 0:4]), reads=[RB[1]], writes=[Rp])
        P.op("dve", L("tensor_scalar", bcen[:, 4:8], pcol[:, 80:84], -1.0, None, op0=ALU.mult), reads=[Rp], writes=[Rp])

        for i in range(2):
            for fb in range(8):
                for cch in range(8):
                    P.op("pe", L("matmul", banks[2][:, i * 8 + fb:i * 8 + fb + 1], lhsT=wad[i][:, cch, fb * 128:(fb + 1) * 128],
                                 rhs=cact_bf[:, cch:cch + 1], start=(cch == 0), stop=(cch == 7)),
                         reads=[Rwad[i], Rp], writes=[RB[2]])
        for i, boff in enumerate([0, 8]):
            P.op("dve", L("tensor_tensor", out=mcol[:, i * 8:(i + 1) * 8], in0=banks[2][:, i * 8:(i + 1) * 8],
                          in1=pcol[:, boff:boff + 8], op=ALU.add), reads=[RB[2], Rp], writes=[Rp])
        P.op("dve", L("scalar_tensor_tensor", out=mcol[:, 32:40], in0=mcol[:, 8:16], scalar=1.0, in1=pcol[:, 56:64],
                      op0=ALU.add, op1=ALU.mult), reads=[Rp], writes=[Rp])
        snap_setup = P.snapshot()

        dbg_list = []

        def norm_A1(t, par):
            P.op("act", L("activation", xnb[par][:], X[:, t, :], AF.Square, accum_out=ncol[par][:, 0:1]),
                 reads=[RX[t]], writes=[Rxn[par], Rnc[par]])
            P.op("act", L("activation", ncol[par][:, 1:2], ncol[par][:, 0:1], AF.Ln, bias=EPS, scale=1.0 / D),
                 reads=[Rnc[par]], writes=[Rnc[par]])
            P.op("act", L("activation", ncol[par][:, 2:3], ncol[par][:, 1:2], AF.Exp, scale=-0.5),
                 reads=[Rnc[par]], writes=[Rnc[par]])
            P.op("dve", L("tensor_scalar", xnb[par][:], X[:, t, :], ncol[par][:, 2:3], None, op0=ALU.mult),
                 reads=[RX[t], Rnc[par]], writes=[Rxn[par]])

        def norm_A2(t, par, bks=(6, 7)):
            bk = bks[par]
            pv = bkbf(bk)
            for cch in range(8):
                P.op("pe", L("transpose", pv[:, cch * 128:(cch + 1) * 128], xnb[par][:, cch * 128:(cch + 1) * 128], ident_bf[:]),
                     reads=[Rxn[par], Rc], writes=[RB[bk]])

        def norm_B(t, par, acol, shcol, bks=(6, 7)):
            bk = bks[par]
            pv = bkbf(bk)
            for cch in range(4):
                P.op("act", L("activation", hT[:, cch, t * 128:(t + 1) * 128], pv[:, cch * 128:(cch + 1) * 128], AF.Identity,
                              bias=mcol[:, shcol + cch:shcol + cch + 1], scale=mcol[:, acol + cch:acol + cch + 1]),
                     reads=[RB[bk], Rp, Rm2], writes=[RHa[t]])
            for cch in range(4, 8):
                P.op("dve", L("tensor_scalar", hT[:, cch, t * 128:(t + 1) * 128], pv[:, cch * 128:(cch + 1) * 128],
                              mcol[:, acol + cch:acol + cch + 1], mcol[:, shcol + cch:shcol + cch + 1], op0=ALU.mult, op1=ALU.add),
                     reads=[RB[bk], Rp, Rm2], writes=[RHb[t]])

        def norm_stream(acol, shcol, bks=(6, 7)):
            p1 = None
            p2 = None
            while True:
                t = yield
                if t is not None:
                    norm_A1(t, t % 2)
                if p1 is not None:
                    norm_A2(p1, p1 % 2, bks)
                if p2 is not None:
                    norm_B(p2, p2 % 2, acol, shcol, bks)
                p2 = p1
                p1 = t

        def norm_all(acol, shcol):
            ns = norm_stream(acol, shcol)
            next(ns)
            for t in range(NT):
                ns.send(t)
            ns.send(None)
            ns.send(None)

        bkrr = dict(i=0)

        def nbank(lst=(0, 1, 2, 3)):
            b = lst[bkrr["i"] % len(lst)]
            bkrr["i"] += 1
            return b

        evrr = dict(i=0)

        def chk(n):
            if stage == n:
                raise _Stop()

        carry = dict(ns=None, pend=[])

        def body():
          for blk in range(NB):
            tok0 = blk * TB
            chk(1)
            if blk == 0:
                norm_all(32, 0)
                P.apply(snap_setup)
            chk(2)
            wsl, wres = wnext()
            wv = wsl[:, 0:4096].rearrange("p (c n) -> p c n", c=8)
            for t in range(NT):
                if carry["pend"] and t >= 4:
                    carry["ns"].send(carry["pend"].pop(0))
                    if t == 4:
                        carry["ns"].send(carry["pend"].pop(0))
                bk = nbank()
                for cch in range(8):
                    P.op("pe", L("matmul", banks[bk][:, :], lhsT=hT[:, cch, t * 128:(t + 1) * 128], rhs=wv[:, cch, :],
                                 start=(cch == 0), stop=(cch == 7)), reads=[wres, RHa[t], RHb[t]], writes=[RB[bk]])
                P.op("act", L("activation", zs[:, t, :], banks[bk][:, :], AF.Silu), reads=[RB[bk]], writes=[RZ[t]])
                bk = nbank()
                for cch in range(8):
                    P.op("pe", L("matmul", banks[bk][:, 0:8], lhsT=hT[:, cch, t * 128:(t + 1) * 128], rhs=wlg[:, cch, :],
                                 start=(cch == 0), stop=(cch == 7)), reads=[Rwlg, RHa[t], RHb[t]], writes=[RB[bk]])
                P.op("dve", L("tensor_copy", lg[:, t, :], banks[bk][:, 0:8]), reads=[RB[bk]], writes=[RLG[t]])
            wrel(1)
            w4_list = [(f, k) for f in range(12) for k in range(4)]

            def w4_build(i0, i1):
                for (f, k) in w4_list[i0:i1]:
                    if (f * 4 + k) % 2 == 0:
                        P.op("dve", L("tensor_scalar", W4[:, f, k, :], ident32[:], cw4c[:, k * 12 + f:k * 12 + f + 1], None, op0=ALU.mult),
                             reads=[Rc, Rp], writes=[RW4[f][k]])
                    else:
                        P.op("act", L("activation", W4[:, f, k, :], ident32[:], AF.Copy, scale=cw4c[:, k * 12 + f:k * 12 + f + 1]),
                             reads=[Rc, Rp], writes=[RW4[f][k]])

            wa, wares = wnext()
            wg, wgres = wnext()
            wav = wa[:, 0:4096].rearrange("p (c n) -> p c n", c=8)
            wgv = wg[:, 0:4096].rearrange("p (c n) -> p c n", c=8)
            for cb in range(4):
                P.op("pool", L("tensor_copy", u_v[:, cb, 0:32], utail[:, cb, :]), reads=[Rtail], writes=[RU[cb]])
                for s in range(NS):
                    bka = nbank()
                    for cch in range(8):
                        P.op("pe", L("matmul", banks[bka][:, :], lhsT=wav[:, cch, cb * 128:(cb + 1) * 128],
                                     rhs=hT[:, cch, s * 512:(s + 1) * 512], start=(cch == 0), stop=(cch == 7)),
                             reads=[wares] + (RHa if cch < 4 else RHb)[4 * s:4 * s + 4], writes=[RB[bka]])
                    bkg = nbank()
                    for cch in range(8):
                        P.op("pe", L("matmul", banks[bkg][:, :], lhsT=wgv[:, cch, cb * 128:(cb + 1) * 128],
                                     rhs=hT[:, cch, s * 512:(s + 1) * 512], start=(cch == 0), stop=(cch == 7)),
                             reads=[wgres] + (RHa if cch < 4 else RHb)[4 * s:4 * s + 4], writes=[RB[bkg]])
                    par = s % 2
                    P.op("act", L("activation", tmpf[par][:], banks[bkg][:, :], AF.Tanh, scale=0.5), reads=[RB[bkg]], writes=[Rtf[par]])
                    P.op("dve", L("scalar_tensor_tensor", out=u_v[:, cb, 32 + s * 512:32 + (s + 1) * 512], in0=tmpf[par][:], scalar=1.0,
                                  in1=banks[bka][:, :], op0=ALU.add, op1=ALU.mult), reads=[Rtf[par], RB[bka]], writes=[RU[cb]])
                    w4_build((cb * NS + s) * 6, (cb * NS + s + 1) * 6)
                P.op("pool", L("tensor_copy", utail[:, cb, :], u_v[:, cb, TB:TB + 32]), reads=[RU[cb]], writes=[Rtail])
            wrel(2)
            chk(4)
            Q = lambda q: scb[:, q, :]
            Q3 = lambda q: scb[:, q, :].rearrange("p (t h) -> p t h", h=4)
            S1 = lambda q, t, h: scb[:, q, t * 4 + h:t * 4 + h + 1]
            S4 = lambda q, t: scb[:, q, t * 4:(t + 1) * 4]
            f4 = lambda v: v[:].rearrange("p h t -> p (h t)")
            bc4 = lambda m: m.unsqueeze(1).to_broadcast([128, 4, 128])

            def gdn_scalars():
                P.op("dve", L("tensor_tensor", out=Q3(0), in0=lg[:, :, 4:8], in1=dtb_bc[:].unsqueeze(1).to_broadcast([128, NT, 4]), op=ALU.add),
                     reads=RLG + [Rbc], writes=[Rsc])
                P.op("act", L("activation", Q(0), Q(0), AF.Exp), reads=[Rsc], writes=[Rsc])
                P.op("act", L("activation", Q(0), Q(0), AF.Ln, bias=1.0), reads=[Rsc], writes=[Rsc])
                P.op("dve", L("tensor_tensor", out=Q3(0), in0=Q3(0), in1=abc[:].unsqueeze(1).to_broadcast([128, NT, 4]), op=ALU.mult), reads=[Rsc, Rbc], writes=[Rsc])
                P.op("act", L("activation", Q3(1), lg[:, :, 0:4], AF.Exp, scale=-1.0), reads=RLG + [Rsc], writes=[Rsc])
                P.op("dve", L("tensor_scalar", Q(1), Q(1), 1.0, None, op0=ALU.add), reads=[Rsc], writes=[Rsc])
                P.op("dve", L("reciprocal", Q(1), Q(1)), reads=[Rsc], writes=[Rsc])
                for t in range(NT):
                    sq = tmpf[t % 2][:].bitcast(BF16)
                    qsq_v = sq[:, 0:512].rearrange("p (h t) -> p h t", h=4)
                    ksq_v = sq[:, 512:1024].rearrange("p (h t) -> p h t", h=4)
                    s_ = t // 4
                    tsl = slice(t * 128, (t + 1) * 128)
                    P.op("dve", L("tensor_tensor", out=qsq_v, in0=qfm[:, :, tsl], in1=qfm[:, :, tsl], op=ALU.mult),
                         reads=[RQ[h][t] for h in range(4)], writes=[Rtf[t % 2]])
                    P.op("dve", L("tensor_tensor", out=ksq_v, in0=kfm[:, :, tsl], in1=kfm[:, :, tsl], op=ALU.mult),
                         reads=[RK[h][t] for h in range(4)], writes=[Rtf[t % 2]])
                    for h in range(4):
                        P.op("pe", L("matmul", banks[4][:, t * 4 + h:t * 4 + h + 1], lhsT=qsq_v[:, h, :], rhs=ones_bf[:, 0:1], start=True, stop=True),
                             reads=[Rtf[t % 2], Rc], writes=[RB[4]])
                        P.op("pe", L("matmul", banks[4][:, 32 + t * 4 + h:32 + t * 4 + h + 1], lhsT=ksq_v[:, h, :], rhs=ones_bf[:, 0:1], start=True, stop=True),
                             reads=[Rtf[t % 2], Rc], writes=[RB[4]])
                P.op("pe", L("matmul", banks[4][:, 64:96], lhsT=tri32[:], rhs=Q(0), start=True, stop=True), reads=[Rc, Rsc], writes=[RB[4]])
                P.op("pe", L("matmul", banks[4][:, 96:128], lhsT=ones32[:], rhs=Q(0), start=True, stop=True), reads=[Rc, Rsc], writes=[RB[4]])
                P.op("act", L("activation", Q(13), banks[4][:, 32:64], AF.Ln, bias=EPS), reads=[RB[4], Rsc], writes=[Rsc])
                P.op("act", L("activation", Q(2), Q(13), AF.Exp, scale=-0.5), reads=[Rsc], writes=[Rsc])
                P.op("act", L("activation", Q(3), Q(13), AF.Exp, scale=0.5), reads=[Rsc], writes=[Rsc])
                P.op("dve", L("tensor_tensor", out=Q(4), in0=Q(2), in1=Q(2), op=ALU.mult), reads=[Rsc], writes=[Rsc])
                P.op("dve", L("tensor_tensor", out=Q(4), in0=Q(4), in1=Q(1), op=ALU.mult), reads=[Rsc], writes=[Rsc])
                P.op("dve", L("tensor_copy", scb[:, 5:7, :], banks[4][:, 64:128].rearrange("p (a b) -> p a b", a=2)), reads=[RB[4], Rsc], writes=[Rsc])
                P.op("act", L("activation", Q(7), Q(5), AF.Exp), reads=[Rsc], writes=[Rsc])
                P.op("dve", L("tensor_scalar", Q(8), Q(7), -1.0, None, op0=ALU.mult), reads=[Rsc], writes=[Rsc])
                P.op("dve", L("tensor_tensor", out=Q(9), in0=Q(6), in1=Q(5), op=ALU.subtract), reads=[Rsc], writes=[Rsc])
                P.op("act", L("activation", Q(9), Q(9), AF.Exp), reads=[Rsc], writes=[Rsc])
                P.op("dve", L("tensor_tensor", out=Q(10), in0=Q(9), in1=Q(4), op=ALU.mult), reads=[Rsc], writes=[Rsc])
                P.op("act", L("activation", Q(11), Q(6), AF.Exp), reads=[Rsc], writes=[Rsc])
                P.op("dve", L("tensor_scalar", Q(12), banks[4][:, 0:32], EPS, EPS * 128.0, op0=ALU.add, op1=ALU.mult), reads=[RB[4], Rsc], writes=[Rsc])

            kinds = ["q", "k", "v"]
            wcur = {}

            def qkv_A(fidx):
                gi, h = fidx // 4, fidx % 4
                if h == 0:
                    wsl, wres = wnext()
                    wcur["v"] = wsl[:, 0:4096].rearrange("p (c n) -> p c n", c=8)
                    wcur["r"] = wres
                wv, wres = wcur["v"], wcur["r"]
                rs = fidx % 2
                raw = raw_v[rs]
                P.op("pool", L("tensor_copy", raw[:, 0:4], rawtail[:, fidx, :]), reads=[Rtail], writes=[RRAW[rs]])
                for s in range(NS):
                    bk = nbank()
                    for cch in range(8):
                        P.op("pe", L("matmul", banks[bk][:, :], lhsT=wv[:, cch, h * 128:(h + 1) * 128],
                                     rhs=hT[:, cch, s * 512:(s + 1) * 512], start=(cch == 0), stop=(cch == 7)),
                             reads=[wres] + (RHa if cch < 4 else RHb)[4 * s:4 * s + 4], writes=[RB[bk]])
                    if evrr["i"] % 2 == 0:
                        P.op("act", L("copy", raw[:, 4 + s * 512:4 + (s + 1) * 512], banks[bk][:, :]), reads=[RB[bk]], writes=[RRAW[rs]])
                    else:
                        P.op("dve", L("tensor_copy", raw[:, 4 + s * 512:4 + (s + 1) * 512], banks[bk][:, :]), reads=[RB[bk]], writes=[RRAW[rs]])
                    evrr["i"] += 1
                P.op("pool", L("tensor_copy", rawtail[:, fidx, :], raw[:, TB:TB + 4]), reads=[RRAW[rs]], writes=[Rtail])
                if h == 3:
                    wrel(1)

            def qkv_B(fidx):
                gi, h = fidx // 4, fidx % 4
                kind = kinds[gi]
                rs = fidx % 2
                raw = raw_v[rs]
                if kind in ("q", "k"):
                    dst = qfm if kind == "q" else kfm
                    RR = RQ if kind == "q" else RK
                    for s in range(NS):
                        bk = nbank()
                        for k in range(4):
                            P.op("pe", L("matmul", banks[bk][:, :], lhsT=W4[:, fidx, k, :],
                                         rhs=raw[:, s * 512 + 1 + k:s * 512 + 1 + k + 512], start=(k == 0), stop=(k == 3)),
                                 reads=[RW4[fidx][k], RRAW[rs]], writes=[RB[bk]])
                        P.op("act", L("activation", dst[:, h, s * 512:(s + 1) * 512], banks[bk][:, :], AF.Silu),
                             reads=[RB[bk]], writes=RR[h][4 * s:4 * s + 4])
                if kind in ("k", "v"):
                    dst = ktm if kind == "k" else vtm
                    RR = RKT if kind == "k" else RVT
                    for t4 in range(NT // 4):
                        bk = nbank()
                        for tt in range(4):
                            t = t4 * 4 + tt
                            for k in range(4):
                                P.op("pe", L("matmul", banks[bk][:, tt * 128:(tt + 1) * 128],
                                             lhsT=raw[:, t * 128 + 1 + k:t * 128 + 1 + k + 128], rhs=W4[:, fidx, k, :],
                                             start=(k == 0), stop=(k == 3)), reads=[RW4[fidx][k], RRAW[rs]], writes=[RB[bk]])
                        P.op("act", L("activation", dst[:, t4 * 4:t4 * 4 + 4, h * 128:(h + 1) * 128],
                                      banks[bk][:, :].rearrange("p (t f) -> p t f", t=4), AF.Silu),
                             reads=[RB[bk]], writes=RR[t4 * 4:t4 * 4 + 4])
                if fidx == 7:
                    gdn_scalars()

            qkv_A(0)
            for fidx in range(12):
                if fidx + 1 < 12:
                    qkv_A(fidx + 1)
                qkv_B(fidx)
            snap_conv4 = P.snapshot()
            chk(3)
            def build_wc(cb, k0=0, k1=31):
                for k in range(k0, k1):
                    if k % 2 == 0:
                        P.op("dve", L("tensor_scalar", Wc[:, k, :], imp32[:], cw31c[:, k * 4 + cb:k * 4 + cb + 1], None, op0=ALU.mult),
                             reads=[Rc, Rp], writes=[RWck[k]])
                    else:
                        P.op("act", L("activation", Wc[:, k, :], imp32[:], AF.Copy, scale=cw31c[:, k * 4 + cb:k * 4 + cb + 1]),
                             reads=[Rc, Rp], writes=[RWck[k]])

            def gn(cb, s, bk, bk2):
                par = s % 2
                P.op("act", L("activation", tmpb[par][:], banks[bk][:, :], AF.Square, bias=bcen[:, cb:cb + 1]),
                     reads=[RB[bk], Rp], writes=[Rtb[par]])
                P.op("act", L("activation", tmpf[par][:], banks[bk][:, :], AF.Identity, bias=bcen[:, cb:cb + 1]),
                     reads=[RB[bk], Rp], writes=[Rtf[par]])
                yield
                P.op("pe", L("matmul", banks[bk2][:, :], lhsT=gm_bf[:], rhs=tmpb[par][:], start=True, stop=True),
                     reads=[Rc, Rtb[par]], writes=[RB[bk2]])
                yield
                P.op("act", L("activation", tmpg[par], banks[bk2][:, :], AF.Ln, bias=EPS), reads=[RB[bk2]], writes=[Rtg[par]])
                P.op("act", L("activation", tmpg[par], tmpg[par], AF.Exp, scale=-0.5), reads=[Rtg[par]], writes=[Rtg[par]])
                yield
                P.op("dve", L("scalar_tensor_tensor", out=hT[:, cb, s * 512:(s + 1) * 512], in0=tmpf[par][:], scalar=pcol[:, 76 + cb:77 + cb],
                              in1=tmpg[par], op0=ALU.mult, op1=ALU.mult), reads=[Rtf[par], Rtg[par], Rp], writes=RHa[4 * s:4 * s + 4])
                yield

            def conv31gen():
                cbanks = [1, 7]
                ci = 0
                for k0 in range(0, 31, 8):
                    build_wc(0, k0, min(31, k0 + 8))
                    yield
                for cb in range(4):
                    for s in range(NS):
                        bk = cbanks[ci % 2]
                        bk2 = cbanks[(ci + 1) % 2]
                        ci += 1
                        for k in range(31):
                            P.op("pe", L("matmul", banks[bk][:, :], lhsT=Wc[:, k, :], rhs=u_v[:, cb, s * 512 + 2 + k:s * 512 + 2 + k + 512],
                                         start=(k == 0), stop=(k == 30)), reads=[RWck[k], RU[cb]], writes=[RB[bk]])
                            if k % 4 == 3:
                                yield
                        yield
                        if s == NS - 1 and cb + 1 < 4:
                            for k0 in range(0, 31, 8):
                                build_wc(cb + 1, k0, min(31, k0 + 8))
                                yield
                        yield from gn(cb, s, bk, bk2)
            chk(5)
            P.apply(snap_conv4)
            chk(51)

            def prep(t):
                B = CH[t % 2]
                R = B.R
                o4 = t % 4
                s_ = t // 4
                tsl = slice(t * 128, (t + 1) * 128)
                bx, by = B.bx, B.by
                HS = [slice(h * 128, (h + 1) * 128) for h in range(4)]
                for h in range(4):
                    P.op("act", L("activation", B.gtri[:, h, :], tri32[:], AF.Copy, scale=S1(0, t, h)), reads=[Rc, Rsc], writes=[R["gtri"]])
                yield
                for h in range(4):
                    hs = HS[h]
                    P.op("pe", L("matmul", banks[bx][:, hs], lhsT=ones32[:], rhs=B.gtri[:, h, :], start=True, stop=False), reads=[Rc, R["gtri"]], writes=[RB[bx]])
                    P.op("pe", L("matmul", banks[bx][:, hs], lhsT=B.gtri[:, h, :], rhs=nones32[:], start=False, stop=False), reads=[Rc, R["gtri"]], writes=[RB[bx]])
                    P.op("pe", L("matmul", banks[bx][:, hs], lhsT=ident32[:], rhs=nm32[:], start=False, stop=True), reads=[Rc], writes=[RB[bx]])
                    P.op("pe", L("matmul", banks[by][:, hs], lhsT=ones32[:], rhs=B.gtri[:, h, :], start=True, stop=True), reads=[Rc, R["gtri"]], writes=[RB[by]])
                    if h < 3:
                        yield
                P.op("act", L("activation", f4(B.dti), banks[bx][:, :], AF.Exp), reads=[RB[bx]], writes=[R["dti"]])
                P.op("act", L("activation", f4(B.egb), banks[by][:, :], AF.Exp), reads=[RB[by]], writes=[R["egb"]])
                yield
                for h in range(4):
                    P.op("pe", L("matmul", banks[bx][:, HS[h]], lhsT=kfm[:, h, tsl], rhs=kfm[:, h, tsl], start=True, stop=True), reads=[RK[h][t]], writes=[RB[bx]])
                for h in range(4):
                    P.op("pe", L("matmul", banks[by][:, HS[h]], lhsT=kfm[:, h, tsl], rhs=qfm[:, h, tsl], start=True, stop=True), reads=[RK[h][t], RQ[h][t]], writes=[RB[by]])
                for h in range(4):
                    P.op("dve", L("scalar_tensor_tensor", out=B.A0[:, h, :], in0=banks[bx][:, HS[h]], scalar=S1(4, t, h), in1=B.dti[:, h, :],
                                  op0=ALU.mult, op1=ALU.mult), reads=[RB[bx], Rsc, R["dti"]], writes=[R["A0"]])
                P.op("dve", L("tensor_tensor", out=qkdt[o4][:].rearrange("p h t -> p (h t)"), in0=banks[by][:, :], in1=f4(B.dti), op=ALU.mult),
                     reads=[RB[by], R["dti"]], writes=[RQK[o4]])
                P.op("dve", L("tensor_tensor", out=B.A0[:], in0=B.A0[:], in1=bc4(su_bf[:]), op=ALU.mult), reads=[R["A0"], Rc], writes=[R["A0"]])
                P.op("dve", L("tensor_tensor", out=qfm[:, :, tsl], in0=qfm[:, :, tsl], in1=B.egb[:], op=ALU.mult),
                     reads=[RQ[h][t] for h in range(4)] + [R["egb"]], writes=[RQ[h][t] for h in range(4)])
                yield
                pvy = bkbf(by)
                for h in range(4):
                    P.op("pe", L("transpose", pvy[:, h * 128:(h + 1) * 128], B.A0[:, h, :], ident_bf[:]), reads=[R["A0"], Rc], writes=[RB[by]])
                P.op("act", L("copy", f4(B.A0T), pvy[:, 0:512]), reads=[RB[by]], writes=[R["A0T"]])
                yield
                P.op("dve", L("tensor_tensor", out=B.A1[:], in0=B.A0[:], in1=bc4(mk_bf[:, 0, :]), op=ALU.mult), reads=[R["A0"], Rc], writes=[R["A1"]])
                P.op("dve", L("tensor_tensor", out=B.AT1[:], in0=B.A0T[:], in1=bc4(mk_bf[:, 0, :]), op=ALU.mult), reads=[R["A0T"], Rc], writes=[R["AT1"]])
                P.op("dve", L("tensor_tensor", out=B.Zb[:], in0=bc4(ident_bf[:]), in1=B.A1[:], op=ALU.subtract), reads=[R["A1"], Rc], writes=[R["Zb"]])
                P.op("dve", L("tensor_tensor", out=B.ZTb[:], in0=bc4(ident_bf[:]), in1=B.AT1[:], op=ALU.subtract), reads=[R["AT1"], Rc], writes=[R["ZTb"]])
                yield
                for h in range(4):
                    P.op("pe", L("matmul", banks[bx][:, HS[h]], lhsT=B.AT1[:, h, :], rhs=B.A1[:, h, :], start=True, stop=True), reads=[R["A1"], R["AT1"]], writes=[RB[bx]])
                for h in range(4):
                    P.op("pe", L("matmul", banks[by][:, HS[h]], lhsT=B.A1[:, h, :], rhs=B.AT1[:, h, :], start=True, stop=True), reads=[R["A1"], R["AT1"]], writes=[RB[by]])
                P.op("dve", L("tensor_copy", f4(B.qsq), banks[bx][:, :]), reads=[RB[bx]], writes=[R["qsq"]])
                P.op("act", L("copy", f4(B.ksq), banks[by][:, :]), reads=[RB[by]], writes=[R["ksq"]])
                yield

                def zup(lz, rlz, rz, rrz, lzt, rlzt, rzt, rrzt, final=False):
                    for h in range(4):
                        P.op("pe", L("matmul", banks[bx][:, HS[h]], lhsT=ident_bf[:], rhs=B.Zb[:, h, :], start=True, stop=False), reads=[Rc, R["Zb"]], writes=[RB[bx]])
                        P.op("pe", L("matmul", banks[bx][:, HS[h]], lhsT=lz[:, h, :], rhs=rz[:, h, :], start=False, stop=True), reads=[rlz, rrz], writes=[RB[bx]])
                    if not final:
                        for h in range(4):
                            P.op("pe", L("matmul", banks[by][:, HS[h]], lhsT=ident_bf[:], rhs=B.ZTb[:, h, :], start=True, stop=False), reads=[Rc, R["ZTb"]], writes=[RB[by]])
                            P.op("pe", L("matmul", banks[by][:, HS[h]], lhsT=lzt[:, h, :], rhs=rzt[:, h, :], start=False, stop=True), reads=[rlzt, rrzt], writes=[RB[by]])
                        P.op("dve", L("tensor_copy", f4(B.Zb), banks[bx][:, :]), reads=[RB[bx]], writes=[R["Zb"]])
                        P.op("act", L("copy", f4(B.ZTb), banks[by][:, :]), reads=[RB[by]], writes=[R["ZTb"]])
                    else:
                        P.op("dve", L("tensor_copy", f4(Zfin[o4]), banks[bx][:, :]), reads=[RB[bx]], writes=[RZF[o4]])

                zup(B.ksq, R["ksq"], B.Zb, R["Zb"], B.Zb, R["Zb"], B.ksq, R["ksq"])
                yield
                for h in range(4):
                    P.op("pe", L("matmul", banks[by][:, HS[h]], lhsT=B.qsq[:, h, :], rhs=B.ksq[:, h, :], start=True, stop=True), reads=[R["qsq"], R["ksq"]], writes=[RB[by]])
                P.op("act", L("copy", f4(B.ksq), banks[by][:, :]), reads=[RB[by]], writes=[R["ksq"]])
                yield
                zup(B.ksq, R["ksq"], B.Zb, R["Zb"], B.Zb, R["Zb"], B.ksq, R["ksq"])

                def aoff(lvl):
                    P.op("pool", L("tensor_tensor", out=B.AT1[:], in0=B.A0T[:], in1=bc4(mk_bf[:, 1 + lvl, :]), op=ALU.mult), reads=[R["A0T"], Rc], writes=[R["AT1"]])
                aoff(0)
                yield
                for lvl in range(4):
                    for h in range(4):
                        P.op("pe", L("matmul", banks[bx][:, HS[h]], lhsT=B.AT1[:, h, :], rhs=B.Zb[:, h, :], start=True, stop=True), reads=[R["AT1"], R["Zb"]], writes=[RB[bx]])
                    P.op("act", L("activation", f4(B.A1), banks[bx][:, :], AF.Identity, scale=-1.0), reads=[RB[bx]], writes=[R["A1"]])
                    yield
                    zup(B.ZTb, R["ZTb"], B.A1, R["A1"], B.A1, R["A1"], B.ZTb, R["ZTb"], final=(lvl == 3))
                    if lvl < 3:
                        aoff(lvl + 1)
                    yield

            def rec(t):
                o4 = t % 4
                s_ = t // 4
                tsl = slice(t * 128, (t + 1) * 128)
                HS = [slice(h * 128, (h + 1) * 128) for h in range(4)]
                P.op("pool", L("tensor_tensor", out=Vr[:], in0=vtm[:, t, :].rearrange("p (h f) -> p h f", h=4),
                               in1=S4(3, t).unsqueeze(2).to_broadcast([128, 4, 128]), op=ALU.mult), reads=[RVT[t], Rsc], writes=[Rg["Vr"]])
                P.op("pool", L("tensor_tensor", out=wz[:], in0=zs[:, t, :].rearrange("p (h f) -> p h f", h=4),
                               in1=gnw_bc[:].unsqueeze(1).to_broadcast([128, 4, 128]), op=ALU.mult), reads=[RZ[t], Rbc], writes=[Rg["wz"]])
                for h in range(4):
                    P.op("pe", L("matmul", banks[2][:, HS[h]], lhsT=kfm[:, h, tsl], rhs=Sbf[:, h, :], start=True, stop=True),
                         reads=[RK[h][t], RSB[h]], writes=[RB[2]])
                yield
                for h in range(4):
                    P.op("dve", L("scalar_tensor_tensor", out=rhs2[:, h, :], in0=banks[2][:, HS[h]], scalar=S1(8, t, h), in1=Vr[:, h, :],
                                  op0=ALU.mult, op1=ALU.add), reads=[RB[2], Rsc, Rg["Vr"]], writes=[Rrhs2[h]])
                yield
                for h in range(4):
                    P.op("pe", L("matmul", banks[2][:, HS[h]], lhsT=Zfin[o4][:, h, :], rhs=rhs2[:, h, :], start=True, stop=True),
                         reads=[RZF[o4], Rrhs2[h]], writes=[RB[2]])
                yield
                yp3 = banks[2][:, :].rearrange("p (h t) -> p h t", h=4)
                P.op("dve", L("tensor_tensor", out=Ykbf[:], in0=yp3, in1=S4(10, t).unsqueeze(2).to_broadcast([128, 4, 128]), op=ALU.mult),
                     reads=[RB[2], Rsc], writes=[Rg["Yk"]])
                P.op("dve", L("tensor_tensor", out=Ybf[:], in0=yp3, in1=S4(4, t).unsqueeze(2).to_broadcast([128, 4, 128]), op=ALU.mult),
                     reads=[RB[2], Rsc], writes=[Rg["Y"]])
                yield
                for h in range(4):
                    P.op("pe", L("matmul", banks[2][:, HS[h]], lhsT=ktm[:, t, HS[h]], rhs=Ykbf[:, h, :], start=True, stop=True),
                         reads=[RKT[t], Rg["Yk"]], writes=[RB[2]])
                for h in range(4):
                    P.op("pe", L("matmul", banks[6][:, HS[h]], lhsT=qfm[:, h, tsl], rhs=Sbf[:, h, :], start=True, stop=False),
                         reads=[RQ[h][t], RSB[h]], writes=[RB[6]])
                    P.op("pe", L("matmul", banks[6][:, HS[h]], lhsT=qkdt[o4][:, h, :], rhs=Ybf[:, h, :], start=False, stop=True),
                         reads=[RQK[o4], Rg["Y"]], writes=[RB[6]])
                yield
                for h in range(4):
                    P.op("dve", L("scalar_tensor_tensor", out=S32[:, h, :], in0=S32[:, h, :], scalar=S1(11, t, h), in1=banks[2][:, HS[h]],
                                  op0=ALU.mult, op1=ALU.add), reads=[RB[2], Rsc, RS[h]], writes=[RS[h]])
                for h in range(4):
                    P.op("act", L("copy", Sbf[:, h, :], S32[:, h, :]), reads=[RS[h]], writes=[RSB[h]])
                for h in range(4):
                    P.op("act", L("activation", junk[:], banks[6][:, HS[h]], AF.Square, scale=128.0 ** -0.5, accum_out=S1(14, t, h)),
                         reads=[RB[6]], writes=[Rjunk, Rms[t % 2]])
                yield
                P.op("dve", L("tensor_tensor", out=S4(15, t), in0=S4(14, t), in1=S4(12, t), op=ALU.add), reads=[Rsc, Rms[t % 2]], writes=[Rms[t % 2]])
                yield
                P.op("act", L("activation", S4(15, t), S4(15, t), AF.Ln), reads=[Rms[t % 2]], writes=[Rms[t % 2]])
                P.op("act", L("activation", S4(15, t), S4(15, t), AF.Exp, scale=-0.5), reads=[Rms[t % 2]], writes=[Rms[t % 2]])
                yield
                for h in range(4):
                    P.op("dve", L("scalar_tensor_tensor", out=outb[:, h, :], in0=banks[6][:, HS[h]], scalar=S1(15, t, h), in1=wz[:, h, :],
                                  op0=ALU.mult, op1=ALU.mult), reads=[RB[6], Rms[t % 2], Rg["wz"]], writes=[Rg["outb"]])
                yield
                pv2 = bkbf(2)
                for h in range(4):
                    P.op("pe", L("transpose", pv2[:, h * 128:(h + 1) * 128], outb[:, h, :], ident_bf[:]), reads=[Rg["outb"], Rc], writes=[RB[2]])
                P.op("act", L("copy", hT[:, 4:8, tsl], pv2[:, 0:512].rearrange("p (h t) -> p h t", h=4)), reads=[RB[2]], writes=[RHb[t]])
                yield

            def late_gen(i):
                pc, kind_, idx, boff = LATE[i // 2]
                half = i % 2
                lsl, lres = wnext()
                lv = lsl[:, 0:4096].rearrange("p (c n) -> p c n", c=8)
                bk = 6
                if kind_ == "col":
                    for fb4 in range(4):
                        for cch in range(8):
                            P.op("pe", L("matmul", banks[bk][:, fb4:fb4 + 1], lhsT=lv[:, cch, fb4 * 128:(fb4 + 1) * 128],
                                         rhs=cact_bf[:, cch:cch + 1], start=(cch == 0), stop=(cch == 7)),
                                 reads=[lres, Rp], writes=[RB[bk]])
                    P.op("dve", L("tensor_tensor", out=mcol[:, idx * 8 + half * 4:idx * 8 + half * 4 + 4], in0=banks[bk][:, 0:4],
                                  in1=pcol[:, boff + half * 4:boff + half * 4 + 4], op=ALU.add), reads=[RB[bk], Rp, Rm2], writes=[Rm2])
                else:
                    for cch in range(8):
                        P.op("pe", L("matmul", banks[bk][:, :], lhsT=cactbc[:, cch, :], rhs=lv[:, cch, :],
                                     start=(cch == 0), stop=(cch == 7)), reads=[lres, Rp], writes=[RB[bk]])
                    P.op("dve", L("tensor_tensor", out=gt_bc[:, idx, half * 512:(half + 1) * 512], in0=banks[bk][:, :],
                                  in1=gt_bc[:, idx, half * 512:(half + 1) * 512], op=ALU.add), reads=[RB[bk], Rbc, Rgt], writes=[Rgt])
                wrel(1)
                if i == 3:
                    P.op("dve", L("scalar_tensor_tensor", out=mcol[:, 40:48], in0=mcol[:, 24:32], scalar=1.0, in1=pcol[:, 64:72],
                                  op0=ALU.add, op1=ALU.mult), reads=[Rp, Rm2], writes=[Rm2])
                yield

            rr_round = [0]

            def rr(gens, bg=None, period=2):
                gens = [g for g in gens if g is not None]
                while gens:
                    for g in list(gens):
                        try:
                            next(g)
                        except StopIteration:
                            gens.remove(g)
                    rr_round[0] += 1
                    if bg is not None:
                        try:
                            next(bg)
                        except StopIteration:
                            bg = None

            def seq(*gs):
                for g in gs:
                    yield from g

            def pair(g0, g1):
                gens = [g0, g1]
                while gens:
                    for g in list(gens):
                        try:
                            next(g)
                        except StopIteration:
                            gens.remove(g)
                    yield

            NU = NT // 2
            cg = conv31gen()
            late_after = {0: [0], 1: [1, 2], 2: [3], 3: [4, 5], 4: [6], 5: [7]} if blk == 0 else {}

            def recs(ts):
                for t_ in ts:
                    yield from rec(t_)
                    for li in late_after.get(t_, []):
                        yield from late_gen(li)

            rr([pair(prep(0), prep(1))], cg)
            for j in range(NU - 1):
                rr([recs([2 * j, 2 * j + 1]), pair(prep(2 * j + 2), prep(2 * j + 3))], cg)
            for _ in cg:
                pass
            for cb in range(4):
                P.op("act", L("activation", hT[:, cb, :], hT[:, cb, :], AF.Silu, bias=pcol[:, 80 + cb:81 + cb]), reads=RHa + [Rp], writes=RHa)
            chk(6)
            ns2 = norm_stream(40, 16, bks=(1, 7))
            next(ns2)
            w0, w0r = wnext()
            w1, w1r = wnext()
            wo = [w0[:, 0:4096].rearrange("p (c n) -> p c n", c=4), w1[:, 0:4096].rearrange("p (c n) -> p c n", c=4)]
            wor = [w0r, w1r]
            obk = dict(i=0)

            def outproj_gen(tiles):
                for t in tiles:
                    if t >= 2:
                        ns2.send(t - 2)
                    for nh in range(2):
                        bk = (0, 3, 4, 5)[obk["i"] % 4]
                        obk["i"] += 1
                        for kc in range(8):
                            P.op("pe", L("matmul", banks[bk][:, :], lhsT=hT[:, kc, t * 128:(t + 1) * 128], rhs=wo[kc // 4][:, kc % 4, nh * 512:(nh + 1) * 512],
                                         start=(kc == 0), stop=(kc == 7)), reads=[(RHa if kc < 4 else RHb)[t], wor[kc // 4]], writes=[RB[bk]])
                        par = nh
                        P.op("dve", L("tensor_tensor", out=tmpf[par][:], in0=banks[bk][:, :], in1=gt_bc[:, 0, nh * 512:(nh + 1) * 512], op=ALU.mult),
                             reads=[RB[bk], Rbc, Rgt], writes=[Rtf[par]])
                        P.op("dve", L("tensor_tensor", out=X[:, t, nh * 512:(nh + 1) * 512], in0=X[:, t, nh * 512:(nh + 1) * 512], in1=tmpf[par][:], op=ALU.add),
                             reads=[Rtf[par], RX[t]], writes=[RX[t]])
                        yield

            rr([recs([NT - 2, NT - 1]), outproj_gen(range(NT - 2))])
            for _ in outproj_gen(range(NT - 2, NT)):
                pass
            snap_gdn = P.snapshot()
            ns2.send(NT - 2)
            ns2.send(NT - 1)
            ns2.send(None)
            ns2.send(None)
            wrel(2)
            P.apply(snap_gdn)
            chk(7)
            wfo_v = wfo_d.rearrange("(c p) n -> p c n", p=128)
            def wfo_load(i):
                P.dma("pool", [L("dma_start", out=wfo[:, 4 * i:min(22, 4 * i + 4), :], in_=wfo_v[:, 4 * i:min(22, 4 * i + 4), :])],
                      RWFO[i], writes=[RWFO[i]])
            wfo_at = {0: 0, 2: 1, 4: 2, 6: 3, 8: 4, 9: 5}
            for u_ in range(11):
                wsl_, wr_ = wnext()
                wgv = wsl_[:, 0:2048].rearrange("p (c n) -> p c n", c=8)
                wuv = wsl_[:, 2048:4096].rearrange("p (c n) -> p c n", c=8)
                for fb in range(2):
                    j = u_ * 2 + fb
                    for s in range(NS):
                        bkg = nbank()
                        for cch in range(8):
                            P.op("pe", L("matmul", banks[bkg][:, :], lhsT=wgv[:, cch, fb * 128:(fb + 1) * 128], rhs=hT[:, cch, s * 512:(s + 1) * 512],
                                         start=(cch == 0), stop=(cch == 7)), reads=[wr_] + (RHa if cch < 4 else RHb)[4 * s:4 * s + 4], writes=[RB[bkg]])
                        bku = nbank()
                        for cch in range(8):
                            P.op("pe", L("matmul", banks[bku][:, :], lhsT=wuv[:, cch, fb * 128:(fb + 1) * 128], rhs=hT[:, cch, s * 512:(s + 1) * 512],
                                         start=(cch == 0), stop=(cch == 7)), reads=[wr_] + (RHa if cch < 4 else RHb)[4 * s:4 * s + 4], writes=[RB[bku]])
                        par = s % 2
                        P.op("act", L("activation", tmpb[par][:], banks[bkg][:, :], AF.Silu), reads=[RB[bkg]], writes=[Rtb[par]])
                        P.op("dve", L("tensor_tensor", out=actT[:, j, s * 512:(s + 1) * 512], in0=banks[bku][:, :], in1=tmpb[par][:], op=ALU.mult),
                             reads=[RB[bku], Rtb[par]], writes=[RACT[j]])
                wrel(1)
                if u_ in wfo_at:
                    wfo_load(wfo_at[u_])
            chk(8)
            ns1 = norm_stream(32, 0)
            next(ns1)
            for t in range(NT):
                if blk + 1 < NB and t >= 2:
                    ns1.send(t - 2)
                for nh in range(2):
                    bk = nbank()
                    for kc in range(22):
                        P.op("pe", L("matmul", banks[bk][:, :], lhsT=actT[:, kc, t * 128:(t + 1) * 128], rhs=wfo[:, kc, nh * 512:(nh + 1) * 512],
                                     start=(kc == 0), stop=(kc == 21)), reads=[RACT[kc], RWFO[kc // 4]], writes=[RB[bk]])
                    if t == NT - 1 and nh == 1:
                        snap_ffn = P.snapshot()
                    par = nh
                    P.op("dve", L("tensor_tensor", out=tmpf[par][:], in0=banks[bk][:, :], in1=gt_bc[:, 1, nh * 512:(nh + 1) * 512], op=ALU.mult),
                         reads=[RB[bk], Rbc, Rgt], writes=[Rtf[par]])
                    P.op("dve", L("tensor_tensor", out=X[:, t, nh * 512:(nh + 1) * 512], in0=X[:, t, nh * 512:(nh + 1) * 512], in1=tmpf[par][:], op=ALU.add),
                         reads=[Rtf[par], RX[t]], writes=[RX[t]])
                par = t % 2
                P.op("act", L("activation", tmpf[par][:].bitcast(BF16), X[:, t, :], AF.Square, accum_out=ncol[par][:, 0:1]), reads=[RX[t]], writes=[Rtf[par], Rnc[par]])
                P.op("act", L("activation", ncol[par][:, 1:2], ncol[par][:, 0:1], AF.Ln, bias=EPS, scale=1.0 / D), reads=[Rnc[par]], writes=[Rnc[par]])
                P.op("act", L("activation", ncol[par][:, 2:3], ncol[par][:, 1:2], AF.Exp, scale=-0.5), reads=[Rnc[par]], writes=[Rnc[par]])
                P.op("dve", L("scalar_tensor_tensor", out=X[:, t, :], in0=X[:, t, :], scalar=ncol[par][:, 2:3], in1=nfw_bc[:],
                              op0=ALU.mult, op1=ALU.mult), reads=[RX[t], Rnc[par], Rbc], writes=[RX[t]])
                P.dma("sp", [L("dma_start", out=y_d[tok0 + t * 128: tok0 + (t + 1) * 128, :], in_=X[:, t, :])], RX[t], reads=[RX[t]])
                if blk + 1 < NB:
                    ntok = tok0 + TB
                    P.dma("sp", [L("dma_start", out=X[:, t, :], in_=x_d[ntok + t * 128: ntok + (t + 1) * 128, :])], RX[t], writes=[RX[t]])
            if blk + 1 < NB:
                carry["ns"] = ns1
                carry["pend"] = [NT - 2, NT - 1, None, None]
            P.apply(snap_ffn)
        try:
            body()
        except _Stop:
            pass
        finals = [("d", RX[t], RX[t].cnt) for t in range(NT) if RX[t].cnt]
        if debug:
            P.barrier()
            dumps = dict(mk_bf=mk_bf, pcol=pcol, mcol=mcol, gt_bc=gt_bc, cw31c=cw31c, cw4c=cw4c, bcen=bcen, tri32=tri32, nm32=nm32, imp32=imp32,
                         su_bf=su_bf, gm_bf=gm_bf, cact_bf=cact_bf, abc=abc, hT=hT, qfm=qfm, kfm=kfm, ktm=ktm, vtm=vtm, zs=zs, lg=lg,
                         u_v=u_v, S32=S32, X=X, scb=scb, outb=outb, Ybf=Ybf)
            for name in debug:
                t_ = dumps[name]
                ap_ = t_ if isinstance(t_, bass.AP) else t_[:]
                dd = nc.dram_tensor("dbg_" + name, list(ap_.shape), ap_.dtype, kind="ExternalOutput").ap()
                rr = Res("dbg_" + name)
                finals.append(P.dma("sp", [L("dma_start", out=dd, in_=ap_)], rr))
        P.emit(finals)
    return nc


_NC_CACHE = {}


def kernel(x, c, w_ada, b_ada, norm_mix_w, w_in, conv_w, conv_b, conv_gn_w, conv_gn_b,
           gdn_conv_w, gdn_a_log, gdn_dt_bias, gdn_norm_w, w_out, norm_ffn_w,
           w_ffn_in, w_ffn_out, norm_final_w):
    f = lambda a: np.ascontiguousarray(np.asarray(a, dtype=np.float32))
    x = f(x); c = f(c)
    if "nc" not in _NC_CACHE:
        _NC_CACHE["nc"] = build()
    nc = _NC_CACHE["nc"]
    cw31 = f(conv_w)[0].reshape(31, 4, 128).reshape(124, 128)
    cw4 = f(gdn_conv_w)[0].reshape(4, 12, 128).reshape(48, 128)
    shared = {
        "cw31": cw31, "cw4": cw4,
        "brow": f(b_ada).reshape(1, 6144),
        "nfw": f(norm_final_w).reshape(1, D),
        "gnw": f(gdn_norm_w).reshape(1, 128),
        "alog": f(gdn_a_log).reshape(1, 4),
        "dtb": f(gdn_dt_bias).reshape(1, 4),
        "w_ada": f(w_ada)[0], "w_in": f(w_in)[0], "w_out": f(w_out)[0],
        "w_ffn_in": f(w_ffn_in)[0], "w_ffn_out": f(w_ffn_out)[0],
    }
    in_maps = []
    for b in range(8):
        prow = np.concatenate([f(b_ada).reshape(48, 128), c[b].reshape(8, 128), f(norm_mix_w).reshape(8, 128),
                               f(norm_ffn_w).reshape(8, 128), f(conv_b).reshape(4, 128), f(conv_gn_w).reshape(4, 128),
                               f(conv_gn_b).reshape(4, 128)], axis=0)
        m = dict(shared)
        m["x"] = x[b]
        m["prow"] = np.ascontiguousarray(prow)
        in_maps.append(m)
    res = run_bass_kernel_spmd(nc, in_maps, core_ids=list(range(8)))
    return np.stack([r["y"] for r in res.results], axis=0).astype(np.float32)
```

```python
import contextlib
import numpy as np
import concourse.bass as bass
import concourse.mybir as mybir
from concourse.bass_utils import run_bass_kernel_spmd

F32 = mybir.dt.float32
BF16 = mybir.dt.bfloat16
AF = mybir.ActivationFunctionType
ALU = mybir.AluOpType

D = 1024
S = 2048
TB = 1024
NB = S // TB
NT = TB // 128
NS = TB // 512
N_IN = 3080
D_FF = 2816
EPS = 1e-6
ENGS = ("pe", "act", "dve", "pool", "sp")


class Res:
    __slots__ = ("name", "w", "r", "sem", "cnt", "excl")

    def __init__(self, name, excl=False):
        self.name = name
        self.excl = excl
        self.w = None
        self.r = []
        self.sem = None
        self.cnt = 0


class Prog:
    def __init__(self, nc):
        self.nc = nc
        self.ops = {e: [] for e in ENGS}
        self.dma_res = []
        self.bar = {e: [] for e in ENGS}

    def _deps(self, eng, reads, writes):
        deps = []
        for r in reads:
            if r.w is not None:
                deps.append(r.w)
            if r.excl:
                deps.extend(x for x in r.r if x[1] != eng)
        for r in writes:
            if r.w is not None:
                deps.append(r.w)
            deps.extend(r.r)
        if self.bar[eng]:
            deps.extend(self.bar[eng])
            self.bar[eng] = []
        return [d for d in deps if not (d[0] == "e" and d[1] == "pe" and eng == "pe")]

    def op(self, eng, fn, reads=(), writes=()):
        deps = self._deps(eng, reads, writes)
        idx = len(self.ops[eng])
        self.ops[eng].append(dict(fn=fn, deps=deps, sig=False, dma=None))
        ref = ("e", eng, idx)
        for r in reads:
            r.r.append(ref)
        for r in writes:
            r.w = ref
            r.r = []
        return ref

    def dma(self, eng, fns, key, reads=(), writes=()):
        deps = self._deps(eng, reads, writes)
        if key.sem is None:
            self.dma_res.append(key)
            key.sem = True
        for i, fn in enumerate(fns):
            key.cnt += 16
            self.ops[eng].append(dict(fn=fn, deps=deps if i == 0 else [], sig=False, dma=key))
        ref = ("d", key, key.cnt)
        for r in reads:
            r.r.append(ref)
        for r in writes:
            r.w = ref
            r.r = []
        return ref

    def snapshot(self):
        last = []
        for e in ENGS:
            for i in range(len(self.ops[e]) - 1, -1, -1):
                if self.ops[e][i]["dma"] is None:
                    last.append(("e", e, i))
                    break
        return last

    def apply(self, snap):
        for e in ENGS:
            self.bar[e] = self.bar[e] + list(snap)

    def barrier(self):
        last = []
        for e in ENGS:
            for i in range(len(self.ops[e]) - 1, -1, -1):
                if self.ops[e][i]["dma"] is None:
                    last.append(("e", e, i))
                    break
        for k in self.dma_res:
            last.append(("d", k, k.cnt))
        for e in ENGS:
            self.bar[e] = list(last)

    def emit(self, final_refs):
        nc = self.nc
        allops = self.ops
        for e in ENGS:
            for o in allops[e]:
                for d in o["deps"]:
                    if d[0] == "e":
                        allops[d[1]][d[2]]["sig"] = True
        counts = {}
        for e in ENGS:
            c = 0
            lst = []
            for o in allops[e]:
                if o["sig"]:
                    c += 1
                lst.append(c)
            counts[e] = lst
        with contextlib.ExitStack() as st:
            esem = {e: st.enter_context(nc.semaphore("s_" + e)) for e in ENGS}
            for k in self.dma_res:
                k.sem = st.enter_context(nc.semaphore("d_" + k.name))
            block = st.enter_context(nc.Block())

            def run(e, engine):
                waited = {}

                def do_waits(deps):
                    need = {}
                    for d in deps:
                        if d[0] == "e":
                            s_, v = esem[d[1]], counts[d[1]][d[2]]
                        else:
                            s_, v = d[1].sem, d[2]
                        if v > need.get(s_, 0):
                            need[s_] = v
                    for s_, v in need.items():
                        if waited.get(s_, 0) >= v:
                            continue
                        waited[s_] = v
                        engine.wait_ge(s_, v)

                for o in allops[e]:
                    do_waits(o["deps"])
                    ins = o["fn"](engine)
                    if o["dma"] is not None:
                        ins.then_inc(o["dma"].sem, 16)
                    elif o["sig"]:
                        ins.then_inc(esem[e], 1)
                if e == "sp":
                    do_waits(final_refs)

            @block.tensor
            def _(eng):
                run("pe", eng)

            @block.scalar
            def _(eng):
                run("act", eng)

            @block.vector
            def _(eng):
                run("dve", eng)

            @block.gpsimd
            def _(eng):
                run("pool", eng)

            @block.sync
            def _(eng):
                run("sp", eng)


def L(name, *a, **k):
    return lambda e: getattr(e, name)(*a, **k)


class _Stop(Exception):
    pass


def build(debug=None, stage=99):
    nc = bass.Bass("TRN2", target_bir_lowering=False)
    dt_in = lambda name, shape: nc.dram_tensor(name, shape, F32, kind="ExternalInput").ap()
    x_d = dt_in("x", [S, D])
    prow_d = dt_in("prow", [84, 128])
    cw31_d = dt_in("cw31", [124, 128])
    cw4_d = dt_in("cw4", [48, 128])
    brow_d = dt_in("brow", [1, 6144])
    nfw_d = dt_in("nfw", [1, D])
    gnw_d = dt_in("gnw", [1, 128])
    alog_d = dt_in("alog", [1, 4])
    dtb_d = dt_in("dtb", [1, 4])
    wada_d = dt_in("w_ada", [D, 6 * D])
    win_d = dt_in("w_in", [D, N_IN])
    wout_d = dt_in("w_out", [D, D])
    wfi_d = dt_in("w_ffn_in", [D, 2 * D_FF])
    wfo_d = dt_in("w_ffn_out", [D_FF, D])
    y_d = nc.dram_tensor("y", [S, D], F32, kind="ExternalOutput").ap()
    P = Prog(nc)
    with contextlib.ExitStack() as st:
        def sb(name, shape, dt=F32):
            return st.enter_context(nc.sbuf_tensor(name, shape, dt))

        X = sb("X", [128, NT, D], F32)
        hT = sb("hT", [128, 8, TB], BF16)
        NW = 3
        wring = [sb(f"wr{i}", [128, 4096], BF16) for i in range(NW)]
        A1N = 22 * 1024 + 256
        arena1 = sb("arena1", [128, A1N], BF16)
        R2N = 12 * 4 * 128 + 31 * 128 + NT * 512 + NT * 16 + 1024 + 10 * 512 + 14 * 512
        region2 = sb("region2", [128, R2N], BF16)
        ident_bf = sb("ident_bf", [128, 128], BF16)
        nident_bf = sb("nident_bf", [128, 128], BF16)
        ones_bf = sb("ones_bf", [128, 128], BF16)
        su_bf = sb("su_bf", [128, 128], BF16)
        gm_bf = sb("gm_bf", [128, 128], BF16)
        mk_bf = sb("mk_bf", [128, 5, 128], BF16)
        ident32 = sb("ident32", [128, 128], F32)
        ones32 = sb("ones32", [128, 128], F32)
        nones32 = sb("nones32", [128, 128], F32)
        tri32 = sb("tri32", [128, 128], F32)
        nm32 = sb("nm32", [128, 128], F32)
        imp32 = sb("imp32", [128, 128], F32)
        pcol = sb("pcol", [128, 84], F32)
        cw31c = sb("cw31c", [128, 124], F32)
        cw4c = sb("cw4c", [128, 48], F32)
        mcol = sb("mcol", [128, 48], F32)
        bcen = sb("bcen", [128, 8], F32)
        cact_bf = sb("cact_bf", [128, 8], BF16)
        cactbc = sb("cactbc", [128, 8, 128], BF16)
        gt_bc = sb("gt_bc", [128, 2, D], F32)
        nfw_bc = sb("nfw_bc", [128, D], F32)
        gnw_bc = sb("gnw_bc", [128, 128], F32)
        abc = sb("abc", [128, 4], F32)
        dtb_bc = sb("dtb_bc", [128, 4], F32)
        S32 = sb("S32", [128, 4, 128], F32)
        Sbf = sb("Sbf", [128, 4, 128], BF16)
        rawtail = sb("rawtail", [128, 12, 4], BF16)
        utail = sb("utail", [128, 4, 32], BF16)
        wlg = sb("wlg", [128, 8, 8], BF16)
        tmpf = [sb(f"tmpf{i}", [128, 512], F32) for i in range(2)]
        tmpb = [sb(f"tmpb{i}", [128, 512], BF16) for i in range(2)]
        xnb = [sb(f"xnb{i}", [128, D], BF16) for i in range(2)]
        ncol = [sb(f"ncol{i}", [128, 4], F32) for i in range(2)]
        scb = sb("scb", [128, 16, NT * 4], F32)
        junk = sb("junk", [128, 128], BF16)
        Rjunk = Res("junk")
        banks = [st.enter_context(nc.psum_tensor(f"bk{i}", [128, 512], F32)) for i in range(8)]
        RB = [Res(f"bk{i}", excl=True) for i in range(8)]

        def bkbf(i):
            return banks[i][:].bitcast(BF16)

        RX = [Res(f"X{t}") for t in range(NT)]
        RHa = [Res(f"hTa{t}") for t in range(NT)]
        RHb = [Res(f"hTb{t}") for t in range(NT)]
        RW = [Res(f"wr{i}") for i in range(NW)]
        Rc = Res("consts")
        Rp = Res("params")
        RS = [Res(f"S{h}") for h in range(4)]
        RSB = [Res(f"Sb{h}") for h in range(4)]
        Rtf = [Res(f"tmpf{i}") for i in range(2)]
        Rtb = [Res(f"tmpb{i}") for i in range(2)]
        Rxn = [Res(f"xnb{i}") for i in range(2)]
        Rnc = [Res(f"ncol{i}") for i in range(2)]
        Rsc = Res("sc")
        Rms = [Res("ms0"), Res("ms1")]
        Rtail = Res("tails")
        Rwlg = Res("wlg")

        o = 0
        def carve(n):
            nonlocal o
            v = arena1[:, o:o + n]
            o += n
            return v
        UW = 32 + TB
        u_v = carve(4 * UW).rearrange("p (c t) -> p c t", c=4)
        RAWW = 4 + TB
        raw_v = [carve(RAWW) for _ in range(2)]
        tmpg = [raw_v[i][:, 0:1024].bitcast(F32) for i in range(2)]
        qfm = carve(4 * TB).rearrange("p (c t) -> p c t", c=4)
        kfm = carve(4 * TB).rearrange("p (c t) -> p c t", c=4)
        ktm = carve(NT * 512).rearrange("p (t f) -> p t f", t=NT)
        vtm = carve(NT * 512).rearrange("p (t f) -> p t f", t=NT)
        assert o <= A1N, o
        actT = arena1[:, 0:22 * TB].rearrange("p (c t) -> p c t", c=22)
        RU = [Res(f"u{c}") for c in range(4)]
        RRAW = [Res(f"raw{i}") for i in range(2)]
        Rtg = RRAW
        RQ = [[Res(f"q{h}_{t}") for t in range(NT)] for h in range(4)]
        RK = [[Res(f"k{h}_{t}") for t in range(NT)] for h in range(4)]
        RKT = [Res(f"kt{t}") for t in range(NT)]
        RVT = [Res(f"vt{t}") for t in range(NT)]
        RACT = [Res(f"act{j}") for j in range(22)]

        o2 = 0
        def carve2(n):
            nonlocal o2
            v = region2[:, o2:o2 + n]
            o2 += n
            return v
        W4 = carve2(12 * 4 * 128).rearrange("p (f k c) -> p f k c", f=12, k=4)
        Wc = carve2(31 * 128).rearrange("p (k c) -> p k c", k=31)
        zs = carve2(NT * 512).rearrange("p (t f) -> p t f", t=NT)
        lg = carve2(NT * 8 * 2).bitcast(F32).rearrange("p (t f) -> p t f", t=NT)
        def c2h():
            return carve2(512).rearrange("p (h t) -> p h t", h=4)

        class Chain:
            pass
        CH = [Chain(), Chain()]
        oa = [0]
        def carveA(n):
            v = arena1[:, oa[0]:oa[0] + n]
            oa[0] += n
            return v
        ow = [0]
        def carveW(n):
            v = region2[:, ow[0]:ow[0] + n]
            ow[0] += n
            return v
        for ci, cv in enumerate([carve2, carveW]):
            B = CH[ci]
            B.gtri = cv(1024).bitcast(F32).rearrange("p (h t) -> p h t", h=4)
            for nm in ["dti", "egb", "A0", "A0T", "A1", "AT1", "qsq", "ksq", "Zb", "ZTb"]:
                setattr(B, nm, cv(512).rearrange("p (h t) -> p h t", h=4))
            B.R = {nm: Res(f"c{ci}{nm}") for nm in ["gtri", "dti", "egb", "A0", "A0T", "A1", "AT1", "qsq", "ksq", "Zb", "ZTb"]}
            B.bx, B.by = (0, 3) if ci == 0 else (4, 5)
        assert ow[0] <= 12 * 4 * 128
        Zfin = [c2h() for _ in range(4)]
        qkdt = [c2h() for _ in range(4)]
        RZF = [Res(f"zfin{i}") for i in range(4)]
        RQK = [Res(f"qkdt{i}") for i in range(4)]
        Vr = c2h()
        wz = c2h()
        rhs2 = c2h()
        Ybf = c2h()
        Ykbf = c2h()
        outb = c2h()
        assert o2 <= R2N, o2
        wfo = region2[:, 0:22 * 1024].rearrange("p (c n) -> p c n", c=22)
        RW4 = [[Res(f"W4_{f}_{k}") for k in range(4)] for f in range(12)]
        RWc = Res("Wc")
        RWck = [Res(f"Wc{k}") for k in range(31)]
        RZ = [Res(f"zs{t}") for t in range(NT)]
        RLG = [Res(f"lg{t}") for t in range(NT)]
        Rg = {k: Res(k) for k in ["Vr", "wz", "Y", "Yk", "outb"]}
        Rrhs2 = [Res(f"rhs2{h}") for h in range(4)]
        RWFO = [Res(f"wfo{i}") for i in range(6)]

        specs = []
        def win_group(g):
            src = win_d.rearrange("(c p) n -> p c n", p=128)[:, :, g * 512:(g + 1) * 512]
            return lambda slot: [(slot[:, 0:4096].rearrange("p (c n) -> p c n", c=8), src)]
        def wout_group(i):
            src = wout_d.rearrange("(c p) n -> p c n", p=128)[:, 4 * i:4 * i + 4, :]
            return lambda slot: [(slot[:, 0:4096].rearrange("p (c n) -> p c n", c=4), src)]
        def wfi_unit(u_):
            v = wfi_d.rearrange("(c p) n -> p c n", p=128)
            return lambda slot: [(slot[:, 0:2048].rearrange("p (c n) -> p c n", c=8), v[:, :, u_ * 256:(u_ + 1) * 256]),
                                 (slot[:, 2048:4096].rearrange("p (c n) -> p c n", c=8), v[:, :, D_FF + u_ * 256:D_FF + (u_ + 1) * 256])]
        in_order = [5, 0, 1, 2, 3, 4]
        def wada_group(pc, half):
            src = wada_d.rearrange("(c p) n -> p c n", p=128)[:, :, pc * D + half * 512:pc * D + (half + 1) * 512]
            return lambda slot: [(slot[:, 0:4096].rearrange("p (c n) -> p c n", c=8), src)]
        LATE = [(3, "col", 2, 24), (4, "col", 3, 32), (2, "gt", 0, 0), (5, "gt", 1, 0)]
        for blk in range(NB):
            for gi_, g in enumerate(in_order):
                specs.append(win_group(g))
                if blk == 0 and gi_ == 5:
                    for pc, _k, _i, _b in LATE:
                        for half in range(2):
                            specs.append(wada_group(pc, half))
            specs.append(wout_group(0))
            specs.append(wout_group(1))
            for u_ in range(11):
                specs.append(wfi_unit(u_))
        wstate = dict(issued=0, released=0, next=0)

        def wpump():
            while wstate["issued"] < len(specs) and wstate["issued"] - NW < wstate["released"]:
                j = wstate["issued"]
                slot = wring[j % NW]
                pairs = specs[j](slot)
                P.dma("pool", [L("dma_start", out=d_, in_=s_) for d_, s_ in pairs], RW[j % NW], writes=[RW[j % NW]])
                wstate["issued"] += 1

        def wnext():
            i = wstate["next"]
            wstate["next"] += 1
            wpump()
            assert wstate["issued"] > i
            return wring[i % NW], RW[i % NW]

        def wrel(n=1):
            wstate["released"] += n
            wpump()

        prow = arena1[:, 0:256].bitcast(F32)
        cw31r = arena1[:, 256:512].bitcast(F32)
        cw4r = arena1[:, 512:768].bitcast(F32)
        Rst = Res("stage")
        Rst2 = Res("stage2")
        Rst3 = Res("stage3")
        P.dma("sp", [L("dma_start", out=prow[0:84, :], in_=prow_d)], Rst, writes=[Rst])
        P.dma("sp", [L("dma_start", out=cw31r[0:124, :], in_=cw31_d)], Rst2, writes=[Rst2])
        P.dma("sp", [L("dma_start", out=cw4r[0:48, :], in_=cw4_d)], Rst3, writes=[Rst3])
        Rbc = Res("bcasts")
        Rgt = Res("gt")
        Rm2 = Res("mod2")
        P.dma("sp", [L("dma_start", out=nfw_bc[:], in_=nfw_d.partition_broadcast(128)),
                     L("dma_start", out=gnw_bc[:], in_=gnw_d.partition_broadcast(128)),
                     L("dma_start", out=abc[:], in_=alog_d.partition_broadcast(128)),
                     L("dma_start", out=dtb_bc[:], in_=dtb_d.partition_broadcast(128)),
                     L("dma_start", out=gt_bc[:, 0, :], in_=brow_d[:, 2 * D:3 * D].partition_broadcast(128)),
                     L("dma_start", out=gt_bc[:, 1, :], in_=brow_d[:, 5 * D:6 * D].partition_broadcast(128))],
              Rbc, writes=[Rbc])
        wad = [region2[:, 0:8192].rearrange("p (c n) -> p c n", c=8), region2[:, 8192:16384].rearrange("p (c n) -> p c n", c=8)]
        Rwad = [Res("wad0"), Res("wad1")]
        wada_v = wada_d.rearrange("(c p) n -> p c n", p=128)
        for i in range(2):
            P.dma("pool", [L("dma_start", out=wad[i], in_=wada_v[:, :, i * D:(i + 1) * D])], Rwad[i], writes=[Rwad[i]])
        for t in range(NT):
            P.dma("sp", [L("dma_start", out=X[:, t, :], in_=x_d[t * 128:(t + 1) * 128, :])], RX[t], reads=(Rwad if t >= 2 else []), writes=[RX[t]])
        def mk_mask(t, cmp, fill, base_val=1.0, step=-1, cm=1, eng="pool"):
            P.op(eng, L("memset", t[:], base_val), writes=[Rc])
            P.op(eng, L("affine_select", out=t[:], in_=t[:], pattern=[[step, 128]], compare_op=cmp, fill=fill, base=0,
                        channel_multiplier=cm), reads=[Rc], writes=[Rc])
        mk_mask(ident32, ALU.is_equal, 0.0)
        mk_mask(tri32, ALU.is_ge, 0.0, step=1, cm=-1)
        mk_mask(nm32, ALU.is_ge, -1e30, base_val=0.0, step=1, cm=-1)
        P.op("pool", L("memset", ones32[:], 1.0), writes=[Rc])
        P.op("pool", L("memset", nones32[:], -1.0), writes=[Rc])
        P.op("pool", L("memset", ones_bf[:], 1.0), writes=[Rc])
        P.op("pool", L("tensor_copy", ident_bf[:], ident32[:]), reads=[Rc], writes=[Rc])
        P.op("pool", L("tensor_scalar", nident_bf[:], ident32[:], -1.0, None, op0=ALU.mult), reads=[Rc], writes=[Rc])
        P.op("pool", L("tensor_tensor", out=su_bf[:], in0=tri32[:], in1=ident32[:], op=ALU.subtract), reads=[Rc], writes=[Rc])
        P.op("pool", L("memset", imp32[:], 0.0), writes=[Rc])
        P.op("pool", L("memset", imp32[0:64, 0:64], 1.0 / 64), writes=[Rc])
        P.op("pool", L("memset", imp32[64:128, 64:128], 1.0 / 64), writes=[Rc])
        P.op("pool", L("tensor_copy", gm_bf[:], imp32[:]), reads=[Rc], writes=[Rc])
        P.op("pool", L("tensor_tensor", out=imp32[:], in0=ident32[:], in1=imp32[:], op=ALU.subtract), reads=[Rc], writes=[Rc])
        P.op("pool", L("memset", S32[:], 0.0), writes=RS)
        P.op("pool", L("memset", Sbf[:], 0.0), writes=RSB)
        P.op("pool", L("memset", rawtail[:], 0.0), writes=[Rtail])
        P.op("pool", L("memset", utail[:], 0.0), writes=[Rtail])

        bdf = [arena1[:, 2048 + i * 256:2048 + (i + 1) * 256].bitcast(F32) for i in range(4)]
        for i, s_ in enumerate([8, 16, 32, 64]):
            v = bdf[i].rearrange("p (a b) -> p a b", b=s_)
            P.op("pool", L("memset", bdf[i], 1.0), writes=[Rc])
            P.op("pool", L("affine_select", out=v, in_=v, pattern=[[-s_, 128 // s_], [0, s_]], compare_op=ALU.is_ge, fill=0.0, base=0,
                           channel_multiplier=1), reads=[Rc], writes=[Rc])
            P.op("pool", L("affine_select", out=v, in_=v, pattern=[[s_, 128 // s_], [0, s_]], compare_op=ALU.is_ge, fill=0.0, base=s_ - 1,
                           channel_multiplier=-1), reads=[Rc], writes=[Rc])
        P.op("pool", L("tensor_copy", mk_bf[:, 0, :], bdf[0]), reads=[Rc], writes=[Rc])
        for i in range(3):
            P.op("pool", L("tensor_tensor", out=mk_bf[:, 1 + i, :], in0=bdf[i + 1], in1=bdf[i], op=ALU.subtract), reads=[Rc], writes=[Rc])
        P.op("pool", L("tensor_tensor", out=mk_bf[:, 4, :], in0=ones32[:], in1=bdf[3], op=ALU.subtract), reads=[Rc], writes=[Rc])
        P.dma("pool", [L("dma_start", out=wlg[:], in_=win_d.rearrange("(c p) n -> p c n", p=128)[:, :, 3072:3080])],
              Rwlg, writes=[Rwlg])
        wpump()
        P.op("pe", L("transpose", banks[0][:, 0:84], prow[0:84, :], ident32[0:84, 0:84]), reads=[Rst, Rc], writes=[RB[0]])
        P.op("pe", L("transpose", banks[0][:, 128:252], cw31r[0:124, :], ident32[0:124, 0:124]), reads=[Rst2, Rc], writes=[RB[0]])
        P.op("pe", L("transpose", banks[0][:, 256:304], cw4r[0:48, :], ident32[0:48, 0:48]), reads=[Rst3, Rc], writes=[RB[0]])
        P.op("dve", L("tensor_copy", pcol[:], banks[0][:, 0:84]), reads=[RB[0]], writes=[Rp])
        P.op("dve", L("tensor_scalar", cw31c[:], banks[0][:, 128:252], 0.5, None, op0=ALU.mult), reads=[RB[0]], writes=[Rp])
        P.op("dve", L("tensor_copy", cw4c[:], banks[0][:, 256:304]), reads=[RB[0]], writes=[Rp])
        P.op("act", L("activation", cact_bf[:], pcol[:, 48:56], AF.Silu), reads=[Rp], writes=[Rp])
        P.op("dve", L("tensor_copy", cactbc[:], cact_bf[:].unsqueeze(2).to_broadcast([128, 8, 128])), reads=[Rp], writes=[Rp])
        P.op("act", L("activation", abc[:], abc[:], AF.Exp), reads=[Rbc], writes=[Rbc])
        P.op("dve", L("tensor_scalar", abc[:], abc[:], -1.0, None, op0=ALU.mult), reads=[Rbc], writes=[Rbc])
        P.op("pe", L("matmul", banks[1][:, 0:4], lhsT=imp32[:], rhs=pcol[:, 72:76], start=True, stop=True), reads=[Rc, Rp], writes=[RB[1]])
        P.op("dve", L("tensor_copy", bcen[:, 0:4], banks[1][:, 0:4]), reads=[RB[1]], writes=[Rp])
        P.op("dve", L("tensor_scalar", bcen[:, 4:8], pcol[:, 80:84], -1.0, None, op0=ALU.mult), reads=[Rp], writes=[Rp])

        for i in range(2):
            for fb in range(8):
                for cch in range(8):
                    P.op("pe", L("matmul", banks[2][:, i * 8 + fb:i * 8 + fb + 1], lhsT=wad[i][:, cch, fb * 128:(fb + 1) * 128],
                                 rhs=cact_bf[:, cch:cch + 1], start=(cch == 0), stop=(cch == 7)),
                         reads=[Rwad[i], Rp], writes=[RB[2]])
        for i, boff in enumerate([0, 8]):
            P.op("dve", L("tensor_tensor", out=mcol[:, i * 8:(i + 1) * 8], in0=banks[2][:, i * 8:(i + 1) * 8],
                          in1=pcol[:, boff:boff + 8], op=ALU.add), reads=[RB[2], Rp], writes=[Rp])
        P.op("dve", L("scalar_tensor_tensor", out=mcol[:, 32:40], in0=mcol[:, 8:16], scalar=1.0, in1=pcol[:, 56:64],
                      op0=ALU.add, op1=ALU.mult), reads=[Rp], writes=[Rp])
        snap_setup = P.snapshot()

        dbg_list = []

        def norm_A1(t, par):
            P.op("act", L("activation", xnb[par][:], X[:, t, :], AF.Square, accum_out=ncol[par][:, 0:1]),
                 reads=[RX[t]], writes=[Rxn[par], Rnc[par]])
            P.op("act", L("activation", ncol[par][:, 1:2], ncol[par][:, 0:1], AF.Ln, bias=EPS, scale=1.0 / D),
                 reads=[Rnc[par]], writes=[Rnc[par]])
            P.op("act", L("activation", ncol[par][:, 2:3], ncol[par][:, 1:2], AF.Exp, scale=-0.5),
                 reads=[Rnc[par]], writes=[Rnc[par]])
            P.op("dve", L("tensor_scalar", xnb[par][:], X[:, t, :], ncol[par][:, 2:3], None, op0=ALU.mult),
                 reads=[RX[t], Rnc[par]], writes=[Rxn[par]])

        def norm_A2(t, par):
            bk = 6 + par
            pv = bkbf(bk)
            for cch in range(8):
                P.op("pe", L("transpose", pv[:, cch * 128:(cch + 1) * 128], xnb[par][:, cch * 128:(cch + 1) * 128], ident_bf[:]),
                     reads=[Rxn[par], Rc], writes=[RB[bk]])

        def norm_B(t, par, acol, shcol):
            bk = 6 + par
            pv = bkbf(bk)
            for cch in range(4):
                P.op("act", L("activation", hT[:, cch, t * 128:(t + 1) * 128], pv[:, cch * 128:(cch + 1) * 128], AF.Identity,
                              bias=mcol[:, shcol + cch:shcol + cch + 1], scale=mcol[:, acol + cch:acol + cch + 1]),
                     reads=[RB[bk], Rp, Rm2], writes=[RHa[t]])
            for cch in range(4, 8):
                P.op("dve", L("tensor_scalar", hT[:, cch, t * 128:(t + 1) * 128], pv[:, cch * 128:(cch + 1) * 128],
                              mcol[:, acol + cch:acol + cch + 1], mcol[:, shcol + cch:shcol + cch + 1], op0=ALU.mult, op1=ALU.add),
                     reads=[RB[bk], Rp, Rm2], writes=[RHb[t]])

        def norm_stream(acol, shcol):
            p1 = None
            p2 = None
            while True:
                t = yield
                if t is not None:
                    norm_A1(t, t % 2)
                if p1 is not None:
                    norm_A2(p1, p1 % 2)
                if p2 is not None:
                    norm_B(p2, p2 % 2, acol, shcol)
                p2 = p1
                p1 = t

        def norm_all(acol, shcol):
            ns = norm_stream(acol, shcol)
            next(ns)
            for t in range(NT):
                ns.send(t)
            ns.send(None)
            ns.send(None)

        bkrr = dict(i=0)

        def nbank(lst=(0, 1, 2, 3)):
            b = lst[bkrr["i"] % len(lst)]
            bkrr["i"] += 1
            return b

        evrr = dict(i=0)

        def chk(n):
            if stage == n:
                raise _Stop()

        carry = dict(ns=None, pend=[])

        def body():
          for blk in range(NB):
            tok0 = blk * TB
            chk(1)
            if blk == 0:
                norm_all(32, 0)
                P.apply(snap_setup)
            chk(2)
            wsl, wres = wnext()
            wv = wsl[:, 0:4096].rearrange("p (c n) -> p c n", c=8)
            for t in range(NT):
                if carry["pend"] and t >= 4:
                    carry["ns"].send(carry["pend"].pop(0))
                    if t == 4:
                        carry["ns"].send(carry["pend"].pop(0))
                bk = nbank()
                for cch in range(8):
                    P.op("pe", L("matmul", banks[bk][:, :], lhsT=hT[:, cch, t * 128:(t + 1) * 128], rhs=wv[:, cch, :],
                                 start=(cch == 0), stop=(cch == 7)), reads=[wres, RHa[t], RHb[t]], writes=[RB[bk]])
                P.op("act", L("activation", zs[:, t, :], banks[bk][:, :], AF.Silu), reads=[RB[bk]], writes=[RZ[t]])
                bk = nbank()
                for cch in range(8):
                    P.op("pe", L("matmul", banks[bk][:, 0:8], lhsT=hT[:, cch, t * 128:(t + 1) * 128], rhs=wlg[:, cch, :],
                                 start=(cch == 0), stop=(cch == 7)), reads=[Rwlg, RHa[t], RHb[t]], writes=[RB[bk]])
                P.op("dve", L("tensor_copy", lg[:, t, :], banks[bk][:, 0:8]), reads=[RB[bk]], writes=[RLG[t]])
            wrel(1)
            w4_list = [(f, k) for f in range(12) for k in range(4)]

            def w4_build(i0, i1):
                for (f, k) in w4_list[i0:i1]:
                    if (f * 4 + k) % 2 == 0:
                        P.op("dve", L("tensor_scalar", W4[:, f, k, :], ident32[:], cw4c[:, k * 12 + f:k * 12 + f + 1], None, op0=ALU.mult),
                             reads=[Rc, Rp], writes=[RW4[f][k]])
                    else:
                        P.op("act", L("activation", W4[:, f, k, :], ident32[:], AF.Copy, scale=cw4c[:, k * 12 + f:k * 12 + f + 1]),
                             reads=[Rc, Rp], writes=[RW4[f][k]])

            wa, wares = wnext()
            wg, wgres = wnext()
            wav = wa[:, 0:4096].rearrange("p (c n) -> p c n", c=8)
            wgv = wg[:, 0:4096].rearrange("p (c n) -> p c n", c=8)
            for cb in range(4):
                P.op("pool", L("tensor_copy", u_v[:, cb, 0:32], utail[:, cb, :]), reads=[Rtail], writes=[RU[cb]])
                for s in range(NS):
                    bka = nbank()
                    for cch in range(8):
                        P.op("pe", L("matmul", banks[bka][:, :], lhsT=wav[:, cch, cb * 128:(cb + 1) * 128],
                                     rhs=hT[:, cch, s * 512:(s + 1) * 512], start=(cch == 0), stop=(cch == 7)),
                             reads=[wares] + (RHa if cch < 4 else RHb)[4 * s:4 * s + 4], writes=[RB[bka]])
                    bkg = nbank()
                    for cch in range(8):
                        P.op("pe", L("matmul", banks[bkg][:, :], lhsT=wgv[:, cch, cb * 128:(cb + 1) * 128],
                                     rhs=hT[:, cch, s * 512:(s + 1) * 512], start=(cch == 0), stop=(cch == 7)),
                             reads=[wgres] + (RHa if cch < 4 else RHb)[4 * s:4 * s + 4], writes=[RB[bkg]])
                    par = s % 2
                    P.op("act", L("activation", tmpf[par][:], banks[bkg][:, :], AF.Tanh, scale=0.5), reads=[RB[bkg]], writes=[Rtf[par]])
                    P.op("dve", L("scalar_tensor_tensor", out=u_v[:, cb, 32 + s * 512:32 + (s + 1) * 512], in0=tmpf[par][:], scalar=1.0,
                                  in1=banks[bka][:, :], op0=ALU.add, op1=ALU.mult), reads=[Rtf[par], RB[bka]], writes=[RU[cb]])
                    w4_build((cb * NS + s) * 6, (cb * NS + s + 1) * 6)
                P.op("pool", L("tensor_copy", utail[:, cb, :], u_v[:, cb, TB:TB + 32]), reads=[RU[cb]], writes=[Rtail])
            wrel(2)
            chk(4)
            Q = lambda q: scb[:, q, :]
            Q3 = lambda q: scb[:, q, :].rearrange("p (t h) -> p t h", h=4)
            S1 = lambda q, t, h: scb[:, q, t * 4 + h:t * 4 + h + 1]
            S4 = lambda q, t: scb[:, q, t * 4:(t + 1) * 4]
            f4 = lambda v: v[:].rearrange("p h t -> p (h t)")
            bc4 = lambda m: m.unsqueeze(1).to_broadcast([128, 4, 128])

            def gdn_scalars():
                P.op("dve", L("tensor_tensor", out=Q3(0), in0=lg[:, :, 4:8], in1=dtb_bc[:].unsqueeze(1).to_broadcast([128, NT, 4]), op=ALU.add),
                     reads=RLG + [Rbc], writes=[Rsc])
                P.op("act", L("activation", Q(0), Q(0), AF.Exp), reads=[Rsc], writes=[Rsc])
                P.op("act", L("activation", Q(0), Q(0), AF.Ln, bias=1.0), reads=[Rsc], writes=[Rsc])
                P.op("dve", L("tensor_tensor", out=Q3(0), in0=Q3(0), in1=abc[:].unsqueeze(1).to_broadcast([128, NT, 4]), op=ALU.mult), reads=[Rsc, Rbc], writes=[Rsc])
                P.op("act", L("activation", Q3(1), lg[:, :, 0:4], AF.Exp, scale=-1.0), reads=RLG + [Rsc], writes=[Rsc])
                P.op("dve", L("tensor_scalar", Q(1), Q(1), 1.0, None, op0=ALU.add), reads=[Rsc], writes=[Rsc])
                P.op("dve", L("reciprocal", Q(1), Q(1)), reads=[Rsc], writes=[Rsc])
                for t in range(NT):
                    sq = tmpf[t % 2][:].bitcast(BF16)
                    qsq_v = sq[:, 0:512].rearrange("p (h t) -> p h t", h=4)
                    ksq_v = sq[:, 512:1024].rearrange("p (h t) -> p h t", h=4)
                    s_ = t // 4
                    tsl = slice(t * 128, (t + 1) * 128)
                    P.op("dve", L("tensor_tensor", out=qsq_v, in0=qfm[:, :, tsl], in1=qfm[:, :, tsl], op=ALU.mult),
                         reads=[RQ[h][t] for h in range(4)], writes=[Rtf[t % 2]])
                    P.op("dve", L("tensor_tensor", out=ksq_v, in0=kfm[:, :, tsl], in1=kfm[:, :, tsl], op=ALU.mult),
                         reads=[RK[h][t] for h in range(4)], writes=[Rtf[t % 2]])
                    for h in range(4):
                        P.op("pe", L("matmul", banks[4][:, t * 4 + h:t * 4 + h + 1], lhsT=qsq_v[:, h, :], rhs=ones_bf[:, 0:1], start=True, stop=True),
                             reads=[Rtf[t % 2], Rc], writes=[RB[4]])
                        P.op("pe", L("matmul", banks[4][:, 32 + t * 4 + h:32 + t * 4 + h + 1], lhsT=ksq_v[:, h, :], rhs=ones_bf[:, 0:1], start=True, stop=True),
                             reads=[Rtf[t % 2], Rc], writes=[RB[4]])
                P.op("pe", L("matmul", banks[4][:, 64:96], lhsT=tri32[:], rhs=Q(0), start=True, stop=True), reads=[Rc, Rsc], writes=[RB[4]])
                P.op("pe", L("matmul", banks[4][:, 96:128], lhsT=ones32[:], rhs=Q(0), start=True, stop=True), reads=[Rc, Rsc], writes=[RB[4]])
                P.op("act", L("activation", Q(13), banks[4][:, 32:64], AF.Ln, bias=EPS), reads=[RB[4], Rsc], writes=[Rsc])
                P.op("act", L("activation", Q(2), Q(13), AF.Exp, scale=-0.5), reads=[Rsc], writes=[Rsc])
                P.op("act", L("activation", Q(3), Q(13), AF.Exp, scale=0.5), reads=[Rsc], writes=[Rsc])
                P.op("dve", L("tensor_tensor", out=Q(4), in0=Q(2), in1=Q(2), op=ALU.mult), reads=[Rsc], writes=[Rsc])
                P.op("dve", L("tensor_tensor", out=Q(4), in0=Q(4), in1=Q(1), op=ALU.mult), reads=[Rsc], writes=[Rsc])
                P.op("dve", L("tensor_copy", scb[:, 5:7, :], banks[4][:, 64:128].rearrange("p (a b) -> p a b", a=2)), reads=[RB[4], Rsc], writes=[Rsc])
                P.op("act", L("activation", Q(7), Q(5), AF.Exp), reads=[Rsc], writes=[Rsc])
                P.op("dve", L("tensor_scalar", Q(8), Q(7), -1.0, None, op0=ALU.mult), reads=[Rsc], writes=[Rsc])
                P.op("dve", L("tensor_tensor", out=Q(9), in0=Q(6), in1=Q(5), op=ALU.subtract), reads=[Rsc], writes=[Rsc])
                P.op("act", L("activation", Q(9), Q(9), AF.Exp), reads=[Rsc], writes=[Rsc])
                P.op("dve", L("tensor_tensor", out=Q(10), in0=Q(9), in1=Q(4), op=ALU.mult), reads=[Rsc], writes=[Rsc])
                P.op("act", L("activation", Q(11), Q(6), AF.Exp), reads=[Rsc], writes=[Rsc])
                P.op("dve", L("tensor_scalar", Q(12), banks[4][:, 0:32], EPS, EPS * 128.0, op0=ALU.add, op1=ALU.mult), reads=[RB[4], Rsc], writes=[Rsc])

            kinds = ["q", "k", "v"]
            wcur = {}

            def qkv_A(fidx):
                gi, h = fidx // 4, fidx % 4
                if h == 0:
                    wsl, wres = wnext()
                    wcur["v"] = wsl[:, 0:4096].rearrange("p (c n) -> p c n", c=8)
                    wcur["r"] = wres
                wv, wres = wcur["v"], wcur["r"]
                rs = fidx % 2
                raw = raw_v[rs]
                P.op("pool", L("tensor_copy", raw[:, 0:4], rawtail[:, fidx, :]), reads=[Rtail], writes=[RRAW[rs]])
                for s in range(NS):
                    bk = nbank()
                    for cch in range(8):
                        P.op("pe", L("matmul", banks[bk][:, :], lhsT=wv[:, cch, h * 128:(h + 1) * 128],
                                     rhs=hT[:, cch, s * 512:(s + 1) * 512], start=(cch == 0), stop=(cch == 7)),
                             reads=[wres] + (RHa if cch < 4 else RHb)[4 * s:4 * s + 4], writes=[RB[bk]])
                    if evrr["i"] % 2 == 0:
                        P.op("act", L("copy", raw[:, 4 + s * 512:4 + (s + 1) * 512], banks[bk][:, :]), reads=[RB[bk]], writes=[RRAW[rs]])
                    else:
                        P.op("dve", L("tensor_copy", raw[:, 4 + s * 512:4 + (s + 1) * 512], banks[bk][:, :]), reads=[RB[bk]], writes=[RRAW[rs]])
                    evrr["i"] += 1
                P.op("pool", L("tensor_copy", rawtail[:, fidx, :], raw[:, TB:TB + 4]), reads=[RRAW[rs]], writes=[Rtail])
                if h == 3:
                    wrel(1)

            def qkv_B(fidx):
                gi, h = fidx // 4, fidx % 4
                kind = kinds[gi]
                rs = fidx % 2
                raw = raw_v[rs]
                if kind in ("q", "k"):
                    dst = qfm if kind == "q" else kfm
                    RR = RQ if kind == "q" else RK
                    for s in range(NS):
                        bk = nbank()
                        for k in range(4):
                            P.op("pe", L("matmul", banks[bk][:, :], lhsT=W4[:, fidx, k, :],
                                         rhs=raw[:, s * 512 + 1 + k:s * 512 + 1 + k + 512], start=(k == 0), stop=(k == 3)),
                                 reads=[RW4[fidx][k], RRAW[rs]], writes=[RB[bk]])
                        P.op("act", L("activation", dst[:, h, s * 512:(s + 1) * 512], banks[bk][:, :], AF.Silu),
                             reads=[RB[bk]], writes=RR[h][4 * s:4 * s + 4])
                if kind in ("k", "v"):
                    dst = ktm if kind == "k" else vtm
                    RR = RKT if kind == "k" else RVT
                    for t4 in range(NT // 4):
                        bk = nbank()
                        for tt in range(4):
                            t = t4 * 4 + tt
                            for k in range(4):
                                P.op("pe", L("matmul", banks[bk][:, tt * 128:(tt + 1) * 128],
                                             lhsT=raw[:, t * 128 + 1 + k:t * 128 + 1 + k + 128], rhs=W4[:, fidx, k, :],
                                             start=(k == 0), stop=(k == 3)), reads=[RW4[fidx][k], RRAW[rs]], writes=[RB[bk]])
                        P.op("act", L("activation", dst[:, t4 * 4:t4 * 4 + 4, h * 128:(h + 1) * 128],
                                      banks[bk][:, :].rearrange("p (t f) -> p t f", t=4), AF.Silu),
                             reads=[RB[bk]], writes=RR[t4 * 4:t4 * 4 + 4])
                if fidx == 7:
                    gdn_scalars()

            qkv_A(0)
            for fidx in range(12):
                if fidx + 1 < 12:
                    qkv_A(fidx + 1)
                qkv_B(fidx)
            snap_conv4 = P.snapshot()
            chk(3)
            def build_wc(cb, k0=0, k1=31):
                for k in range(k0, k1):
                    if k % 2 == 0:
                        P.op("dve", L("tensor_scalar", Wc[:, k, :], imp32[:], cw31c[:, k * 4 + cb:k * 4 + cb + 1], None, op0=ALU.mult),
                             reads=[Rc, Rp], writes=[RWck[k]])
                    else:
                        P.op("act", L("activation", Wc[:, k, :], imp32[:], AF.Copy, scale=cw31c[:, k * 4 + cb:k * 4 + cb + 1]),
                             reads=[Rc, Rp], writes=[RWck[k]])

            def gn(cb, s, bk, bk2):
                par = s % 2
                P.op("act", L("activation", tmpb[par][:], banks[bk][:, :], AF.Square, bias=bcen[:, cb:cb + 1]),
                     reads=[RB[bk], Rp], writes=[Rtb[par]])
                P.op("act", L("activation", tmpf[par][:], banks[bk][:, :], AF.Identity, bias=bcen[:, cb:cb + 1]),
                     reads=[RB[bk], Rp], writes=[Rtf[par]])
                yield
                P.op("pe", L("matmul", banks[bk2][:, :], lhsT=gm_bf[:], rhs=tmpb[par][:], start=True, stop=True),
                     reads=[Rc, Rtb[par]], writes=[RB[bk2]])
                yield
                P.op("act", L("activation", tmpg[par], banks[bk2][:, :], AF.Ln, bias=EPS), reads=[RB[bk2]], writes=[Rtg[par]])
                P.op("act", L("activation", tmpg[par], tmpg[par], AF.Exp, scale=-0.5), reads=[Rtg[par]], writes=[Rtg[par]])
                yield
                P.op("dve", L("scalar_tensor_tensor", out=hT[:, cb, s * 512:(s + 1) * 512], in0=tmpf[par][:], scalar=pcol[:, 76 + cb:77 + cb],
                              in1=tmpg[par], op0=ALU.mult, op1=ALU.mult), reads=[Rtf[par], Rtg[par], Rp], writes=RHa[4 * s:4 * s + 4])
                yield

            def conv31gen():
                cbanks = [1, 7]
                ci = 0
                for k0 in range(0, 31, 8):
                    build_wc(0, k0, min(31, k0 + 8))
                    yield
                for cb in range(4):
                    for s in range(NS):
                        bk = cbanks[ci % 2]
                        bk2 = cbanks[(ci + 1) % 2]
                        ci += 1
                        for k in range(31):
                            P.op("pe", L("matmul", banks[bk][:, :], lhsT=Wc[:, k, :], rhs=u_v[:, cb, s * 512 + 2 + k:s * 512 + 2 + k + 512],
                                         start=(k == 0), stop=(k == 30)), reads=[RWck[k], RU[cb]], writes=[RB[bk]])
                            if k % 4 == 3:
                                yield
                        yield
                        if s == NS - 1 and cb + 1 < 4:
                            for k0 in range(0, 31, 8):
                                build_wc(cb + 1, k0, min(31, k0 + 8))
                                yield
                        yield from gn(cb, s, bk, bk2)
            chk(5)
            P.apply(snap_conv4)
            chk(51)

            def prep(t):
                B = CH[t % 2]
                R = B.R
                o4 = t % 4
                s_ = t // 4
                tsl = slice(t * 128, (t + 1) * 128)
                bx, by = B.bx, B.by
                HS = [slice(h * 128, (h + 1) * 128) for h in range(4)]
                for h in range(4):
                    P.op("act", L("activation", B.gtri[:, h, :], tri32[:], AF.Copy, scale=S1(0, t, h)), reads=[Rc, Rsc], writes=[R["gtri"]])
                yield
                for h in range(4):
                    hs = HS[h]
                    P.op("pe", L("matmul", banks[bx][:, hs], lhsT=ones32[:], rhs=B.gtri[:, h, :], start=True, stop=False), reads=[Rc, R["gtri"]], writes=[RB[bx]])
                    P.op("pe", L("matmul", banks[bx][:, hs], lhsT=B.gtri[:, h, :], rhs=nones32[:], start=False, stop=False), reads=[Rc, R["gtri"]], writes=[RB[bx]])
                    P.op("pe", L("matmul", banks[bx][:, hs], lhsT=ident32[:], rhs=nm32[:], start=False, stop=True), reads=[Rc], writes=[RB[bx]])
                    P.op("pe", L("matmul", banks[by][:, hs], lhsT=ones32[:], rhs=B.gtri[:, h, :], start=True, stop=True), reads=[Rc, R["gtri"]], writes=[RB[by]])
                    if h < 3:
                        yield
                P.op("act", L("activation", f4(B.dti), banks[bx][:, :], AF.Exp), reads=[RB[bx]], writes=[R["dti"]])
                P.op("act", L("activation", f4(B.egb), banks[by][:, :], AF.Exp), reads=[RB[by]], writes=[R["egb"]])
                yield
                for h in range(4):
                    P.op("pe", L("matmul", banks[bx][:, HS[h]], lhsT=kfm[:, h, tsl], rhs=kfm[:, h, tsl], start=True, stop=True), reads=[RK[h][t]], writes=[RB[bx]])
                for h in range(4):
                    P.op("pe", L("matmul", banks[by][:, HS[h]], lhsT=kfm[:, h, tsl], rhs=qfm[:, h, tsl], start=True, stop=True), reads=[RK[h][t], RQ[h][t]], writes=[RB[by]])
                for h in range(4):
                    P.op("dve", L("scalar_tensor_tensor", out=B.A0[:, h, :], in0=banks[bx][:, HS[h]], scalar=S1(4, t, h), in1=B.dti[:, h, :],
                                  op0=ALU.mult, op1=ALU.mult), reads=[RB[bx], Rsc, R["dti"]], writes=[R["A0"]])
                P.op("dve", L("tensor_tensor", out=B.A0[:], in0=B.A0[:], in1=bc4(su_bf[:]), op=ALU.mult), reads=[R["A0"], Rc], writes=[R["A0"]])
                yield
                P.op("dve", L("tensor_tensor", out=qkdt[o4][:].rearrange("p h t -> p (h t)"), in0=banks[by][:, :], in1=f4(B.dti), op=ALU.mult),
                     reads=[RB[by], R["dti"]], writes=[RQK[o4]])
                P.op("dve", L("tensor_tensor", out=qfm[:, :, tsl], in0=qfm[:, :, tsl], in1=B.egb[:], op=ALU.mult),
                     reads=[RQ[h][t] for h in range(4)] + [R["egb"]], writes=[RQ[h][t] for h in range(4)])
                yield
                pvy = bkbf(by)
                for h in range(4):
                    P.op("pe", L("transpose", pvy[:, h * 128:(h + 1) * 128], B.A0[:, h, :], ident_bf[:]), reads=[R["A0"], Rc], writes=[RB[by]])
                P.op("act", L("copy", f4(B.A0T), pvy[:, 0:512]), reads=[RB[by]], writes=[R["A0T"]])
                yield
                P.op("dve", L("tensor_tensor", out=B.A1[:], in0=B.A0[:], in1=bc4(mk_bf[:, 0, :]), op=ALU.mult), reads=[R["A0"], Rc], writes=[R["A1"]])
                P.op("dve", L("tensor_tensor", out=B.AT1[:], in0=B.A0T[:], in1=bc4(mk_bf[:, 0, :]), op=ALU.mult), reads=[R["A0T"], Rc], writes=[R["AT1"]])
                P.op("dve", L("tensor_tensor", out=B.Zb[:], in0=bc4(ident_bf[:]), in1=B.A1[:], op=ALU.subtract), reads=[R["A1"], Rc], writes=[R["Zb"]])
                P.op("dve", L("tensor_tensor", out=B.ZTb[:], in0=bc4(ident_bf[:]), in1=B.AT1[:], op=ALU.subtract), reads=[R["AT1"], Rc], writes=[R["ZTb"]])
                yield
                for h in range(4):
                    P.op("pe", L("matmul", banks[bx][:, HS[h]], lhsT=B.AT1[:, h, :], rhs=B.A1[:, h, :], start=True, stop=True), reads=[R["A1"], R["AT1"]], writes=[RB[bx]])
                for h in range(4):
                    P.op("pe", L("matmul", banks[by][:, HS[h]], lhsT=B.A1[:, h, :], rhs=B.AT1[:, h, :], start=True, stop=True), reads=[R["A1"], R["AT1"]], writes=[RB[by]])
                P.op("dve", L("tensor_copy", f4(B.qsq), banks[bx][:, :]), reads=[RB[bx]], writes=[R["qsq"]])
                P.op("act", L("copy", f4(B.ksq), banks[by][:, :]), reads=[RB[by]], writes=[R["ksq"]])
                yield

                def zup(lz, rlz, rz, rrz, lzt, rlzt, rzt, rrzt, final=False):
                    for h in range(4):
                        P.op("pe", L("matmul", banks[bx][:, HS[h]], lhsT=ident_bf[:], rhs=B.Zb[:, h, :], start=True, stop=False), reads=[Rc, R["Zb"]], writes=[RB[bx]])
                        P.op("pe", L("matmul", banks[bx][:, HS[h]], lhsT=lz[:, h, :], rhs=rz[:, h, :], start=False, stop=True), reads=[rlz, rrz], writes=[RB[bx]])
                    if not final:
                        for h in range(4):
                            P.op("pe", L("matmul", banks[by][:, HS[h]], lhsT=ident_bf[:], rhs=B.ZTb[:, h, :], start=True, stop=False), reads=[Rc, R["ZTb"]], writes=[RB[by]])
                            P.op("pe", L("matmul", banks[by][:, HS[h]], lhsT=lzt[:, h, :], rhs=rzt[:, h, :], start=False, stop=True), reads=[rlzt, rrzt], writes=[RB[by]])
                        P.op("dve", L("tensor_copy", f4(B.Zb), banks[bx][:, :]), reads=[RB[bx]], writes=[R["Zb"]])
                        P.op("act", L("copy", f4(B.ZTb), banks[by][:, :]), reads=[RB[by]], writes=[R["ZTb"]])
                    else:
                        P.op("dve", L("tensor_copy", f4(Zfin[o4]), banks[bx][:, :]), reads=[RB[bx]], writes=[RZF[o4]])

                zup(B.ksq, R["ksq"], B.Zb, R["Zb"], B.Zb, R["Zb"], B.ksq, R["ksq"])
                yield
                for h in range(4):
                    P.op("pe", L("matmul", banks[by][:, HS[h]], lhsT=B.qsq[:, h, :], rhs=B.ksq[:, h, :], start=True, stop=True), reads=[R["qsq"], R["ksq"]], writes=[RB[by]])
                P.op("act", L("copy", f4(B.ksq), banks[by][:, :]), reads=[RB[by]], writes=[R["ksq"]])
                yield
                zup(B.ksq, R["ksq"], B.Zb, R["Zb"], B.Zb, R["Zb"], B.ksq, R["ksq"])

                def aoff(lvl):
                    P.op("pool", L("tensor_tensor", out=B.AT1[:], in0=B.A0T[:], in1=bc4(mk_bf[:, 1 + lvl, :]), op=ALU.mult), reads=[R["A0T"], Rc], writes=[R["AT1"]])
                aoff(0)
                yield
                for lvl in range(4):
                    for h in range(4):
                        P.op("pe", L("matmul", banks[bx][:, HS[h]], lhsT=B.AT1[:, h, :], rhs=B.Zb[:, h, :], start=True, stop=True), reads=[R["AT1"], R["Zb"]], writes=[RB[bx]])
                    P.op("act", L("activation", f4(B.A1), banks[bx][:, :], AF.Identity, scale=-1.0), reads=[RB[bx]], writes=[R["A1"]])
                    yield
                    zup(B.ZTb, R["ZTb"], B.A1, R["A1"], B.A1, R["A1"], B.ZTb, R["ZTb"], final=(lvl == 3))
                    if lvl < 3:
                        aoff(lvl + 1)
                    yield

            def rec(t):
                o4 = t % 4
                s_ = t // 4
                tsl = slice(t * 128, (t + 1) * 128)
                HS = [slice(h * 128, (h + 1) * 128) for h in range(4)]
                P.op("pool", L("tensor_tensor", out=Vr[:], in0=vtm[:, t, :].rearrange("p (h f) -> p h f", h=4),
                               in1=S4(3, t).unsqueeze(2).to_broadcast([128, 4, 128]), op=ALU.mult), reads=[RVT[t], Rsc], writes=[Rg["Vr"]])
                P.op("pool", L("tensor_tensor", out=wz[:], in0=zs[:, t, :].rearrange("p (h f) -> p h f", h=4),
                               in1=gnw_bc[:].unsqueeze(1).to_broadcast([128, 4, 128]), op=ALU.mult), reads=[RZ[t], Rbc], writes=[Rg["wz"]])
                for h in range(4):
                    P.op("pe", L("matmul", banks[2][:, HS[h]], lhsT=kfm[:, h, tsl], rhs=Sbf[:, h, :], start=True, stop=True),
                         reads=[RK[h][t], RSB[h]], writes=[RB[2]])
                yield
                for h in range(4):
                    P.op("dve", L("scalar_tensor_tensor", out=rhs2[:, h, :], in0=banks[2][:, HS[h]], scalar=S1(8, t, h), in1=Vr[:, h, :],
                                  op0=ALU.mult, op1=ALU.add), reads=[RB[2], Rsc, Rg["Vr"]], writes=[Rrhs2[h]])
                yield
                for h in range(4):
                    P.op("pe", L("matmul", banks[2][:, HS[h]], lhsT=Zfin[o4][:, h, :], rhs=rhs2[:, h, :], start=True, stop=True),
                         reads=[RZF[o4], Rrhs2[h]], writes=[RB[2]])
                yield
                yp3 = banks[2][:, :].rearrange("p (h t) -> p h t", h=4)
                P.op("dve", L("tensor_tensor", out=Ykbf[:], in0=yp3, in1=S4(10, t).unsqueeze(2).to_broadcast([128, 4, 128]), op=ALU.mult),
                     reads=[RB[2], Rsc], writes=[Rg["Yk"]])
                P.op("dve", L("tensor_tensor", out=Ybf[:], in0=yp3, in1=S4(4, t).unsqueeze(2).to_broadcast([128, 4, 128]), op=ALU.mult),
                     reads=[RB[2], Rsc], writes=[Rg["Y"]])
                yield
                for h in range(4):
                    P.op("pe", L("matmul", banks[2][:, HS[h]], lhsT=ktm[:, t, HS[h]], rhs=Ykbf[:, h, :], start=True, stop=True),
                         reads=[RKT[t], Rg["Yk"]], writes=[RB[2]])
                for h in range(4):
                    P.op("pe", L("matmul", banks[6][:, HS[h]], lhsT=qfm[:, h, tsl], rhs=Sbf[:, h, :], start=True, stop=False),
                         reads=[RQ[h][t], RSB[h]], writes=[RB[6]])
                    P.op("pe", L("matmul", banks[6][:, HS[h]], lhsT=qkdt[o4][:, h, :], rhs=Ybf[:, h, :], start=False, stop=True),
                         reads=[RQK[o4], Rg["Y"]], writes=[RB[6]])
                yield
                for h in range(4):
                    P.op("dve", L("scalar_tensor_tensor", out=S32[:, h, :], in0=S32[:, h, :], scalar=S1(11, t, h), in1=banks[2][:, HS[h]],
                                  op0=ALU.mult, op1=ALU.add), reads=[RB[2], Rsc, RS[h]], writes=[RS[h]])
                for h in range(4):
                    P.op("act", L("copy", Sbf[:, h, :], S32[:, h, :]), reads=[RS[h]], writes=[RSB[h]])
                for h in range(4):
                    P.op("act", L("activation", junk[:], banks[6][:, HS[h]], AF.Square, scale=128.0 ** -0.5, accum_out=S1(14, t, h)),
                         reads=[RB[6]], writes=[Rjunk, Rms[t % 2]])
                yield
                P.op("dve", L("tensor_tensor", out=S4(15, t), in0=S4(14, t), in1=S4(12, t), op=ALU.add), reads=[Rsc, Rms[t % 2]], writes=[Rms[t % 2]])
                yield
                P.op("act", L("activation", S4(15, t), S4(15, t), AF.Ln), reads=[Rms[t % 2]], writes=[Rms[t % 2]])
                P.op("act", L("activation", S4(15, t), S4(15, t), AF.Exp, scale=-0.5), reads=[Rms[t % 2]], writes=[Rms[t % 2]])
                yield
                for h in range(4):
                    P.op("dve", L("scalar_tensor_tensor", out=outb[:, h, :], in0=banks[6][:, HS[h]], scalar=S1(15, t, h), in1=wz[:, h, :],
                                  op0=ALU.mult, op1=ALU.mult), reads=[RB[6], Rms[t % 2], Rg["wz"]], writes=[Rg["outb"]])
                yield
                pv2 = bkbf(2)
                for h in range(4):
                    P.op("pe", L("transpose", pv2[:, h * 128:(h + 1) * 128], outb[:, h, :], ident_bf[:]), reads=[Rg["outb"], Rc], writes=[RB[2]])
                P.op("act", L("copy", hT[:, 4:8, tsl], pv2[:, 0:512].rearrange("p (h t) -> p h t", h=4)), reads=[RB[2]], writes=[RHb[t]])
                yield

            def late_gen(i):
                pc, kind_, idx, boff = LATE[i // 2]
                half = i % 2
                lsl, lres = wnext()
                lv = lsl[:, 0:4096].rearrange("p (c n) -> p c n", c=8)
                bk = 6
                if kind_ == "col":
                    for fb4 in range(4):
                        for cch in range(8):
                            P.op("pe", L("matmul", banks[bk][:, fb4:fb4 + 1], lhsT=lv[:, cch, fb4 * 128:(fb4 + 1) * 128],
                                         rhs=cact_bf[:, cch:cch + 1], start=(cch == 0), stop=(cch == 7)),
                                 reads=[lres, Rp], writes=[RB[bk]])
                    P.op("dve", L("tensor_tensor", out=mcol[:, idx * 8 + half * 4:idx * 8 + half * 4 + 4], in0=banks[bk][:, 0:4],
                                  in1=pcol[:, boff + half * 4:boff + half * 4 + 4], op=ALU.add), reads=[RB[bk], Rp, Rm2], writes=[Rm2])
                else:
                    for cch in range(8):
                        P.op("pe", L("matmul", banks[bk][:, :], lhsT=cactbc[:, cch, :], rhs=lv[:, cch, :],
                                     start=(cch == 0), stop=(cch == 7)), reads=[lres, Rp], writes=[RB[bk]])
                    P.op("dve", L("tensor_tensor", out=gt_bc[:, idx, half * 512:(half + 1) * 512], in0=banks[bk][:, :],
                                  in1=gt_bc[:, idx, half * 512:(half + 1) * 512], op=ALU.add), reads=[RB[bk], Rbc, Rgt], writes=[Rgt])
                wrel(1)
                if i == 3:
                    P.op("dve", L("scalar_tensor_tensor", out=mcol[:, 40:48], in0=mcol[:, 24:32], scalar=1.0, in1=pcol[:, 64:72],
                                  op0=ALU.add, op1=ALU.mult), reads=[Rp, Rm2], writes=[Rm2])
                yield

            rr_round = [0]

            def rr(gens, bg=None, period=2):
                gens = [g for g in gens if g is not None]
                while gens:
                    for g in list(gens):
                        try:
                            next(g)
                        except StopIteration:
                            gens.remove(g)
                    rr_round[0] += 1
                    if bg is not None:
                        try:
                            next(bg)
                        except StopIteration:
                            bg = None

            def seq(*gs):
                for g in gs:
                    yield from g

            def pair(g0, g1):
                gens = [g0, g1]
                while gens:
                    for g in list(gens):
                        try:
                            next(g)
                        except StopIteration:
                            gens.remove(g)
                    yield

            NU = NT // 2
            cg = conv31gen()
            rr([pair(prep(0), prep(1))], cg)
            for j in range(NU):
                nxt = pair(prep(2 * j + 2), prep(2 * j + 3)) if j + 1 < NU else None
                if blk == 0:
                    rr([seq(rec(2 * j), late_gen(2 * j), rec(2 * j + 1), late_gen(2 * j + 1)), nxt], cg)
                else:
                    rr([seq(rec(2 * j), rec(2 * j + 1)), nxt], cg)
            for _ in cg:
                pass
            for cb in range(4):
                P.op("act", L("activation", hT[:, cb, :], hT[:, cb, :], AF.Silu, bias=pcol[:, 80 + cb:81 + cb]), reads=RHa + [Rp], writes=RHa)
            snap_gdn = P.snapshot()
            chk(6)
            ns2 = norm_stream(40, 16)
            next(ns2)
            w0, w0r = wnext()
            w1, w1r = wnext()
            wo = [w0[:, 0:4096].rearrange("p (c n) -> p c n", c=4), w1[:, 0:4096].rearrange("p (c n) -> p c n", c=4)]
            wor = [w0r, w1r]
            for t in range(NT):
                if t >= 2:
                    ns2.send(t - 2)
                for nh in range(2):
                    bk = nbank()
                    for kc in range(8):
                        P.op("pe", L("matmul", banks[bk][:, :], lhsT=hT[:, kc, t * 128:(t + 1) * 128], rhs=wo[kc // 4][:, kc % 4, nh * 512:(nh + 1) * 512],
                                     start=(kc == 0), stop=(kc == 7)), reads=[(RHa if kc < 4 else RHb)[t], wor[kc // 4]], writes=[RB[bk]])
                    par = nh
                    P.op("dve", L("tensor_tensor", out=tmpf[par][:], in0=banks[bk][:, :], in1=gt_bc[:, 0, nh * 512:(nh + 1) * 512], op=ALU.mult),
                         reads=[RB[bk], Rbc, Rgt], writes=[Rtf[par]])
                    P.op("dve", L("tensor_tensor", out=X[:, t, nh * 512:(nh + 1) * 512], in0=X[:, t, nh * 512:(nh + 1) * 512], in1=tmpf[par][:], op=ALU.add),
                         reads=[Rtf[par], RX[t]], writes=[RX[t]])
            ns2.send(NT - 2)
            ns2.send(NT - 1)
            ns2.send(None)
            ns2.send(None)
            wrel(2)
            P.apply(snap_gdn)
            chk(7)
            wfo_v = wfo_d.rearrange("(c p) n -> p c n", p=128)
            def wfo_load(i):
                P.dma("pool", [L("dma_start", out=wfo[:, 4 * i:min(22, 4 * i + 4), :], in_=wfo_v[:, 4 * i:min(22, 4 * i + 4), :])],
                      RWFO[i], writes=[RWFO[i]])
            wfo_at = {0: 0, 2: 1, 4: 2, 6: 3, 8: 4, 9: 5}
            for u_ in range(11):
                wsl_, wr_ = wnext()
                wgv = wsl_[:, 0:2048].rearrange("p (c n) -> p c n", c=8)
                wuv = wsl_[:, 2048:4096].rearrange("p (c n) -> p c n", c=8)
                for fb in range(2):
                    j = u_ * 2 + fb
                    for s in range(NS):
                        bkg = nbank()
                        for cch in range(8):
                            P.op("pe", L("matmul", banks[bkg][:, :], lhsT=wgv[:, cch, fb * 128:(fb + 1) * 128], rhs=hT[:, cch, s * 512:(s + 1) * 512],
                                         start=(cch == 0), stop=(cch == 7)), reads=[wr_] + (RHa if cch < 4 else RHb)[4 * s:4 * s + 4], writes=[RB[bkg]])
                        bku = nbank()
                        for cch in range(8):
                            P.op("pe", L("matmul", banks[bku][:, :], lhsT=wuv[:, cch, fb * 128:(fb + 1) * 128], rhs=hT[:, cch, s * 512:(s + 1) * 512],
                                         start=(cch == 0), stop=(cch == 7)), reads=[wr_] + (RHa if cch < 4 else RHb)[4 * s:4 * s + 4], writes=[RB[bku]])
                        par = s % 2
                        P.op("act", L("activation", tmpb[par][:], banks[bkg][:, :], AF.Silu), reads=[RB[bkg]], writes=[Rtb[par]])
                        P.op("dve", L("tensor_tensor", out=actT[:, j, s * 512:(s + 1) * 512], in0=banks[bku][:, :], in1=tmpb[par][:], op=ALU.mult),
                             reads=[RB[bku], Rtb[par]], writes=[RACT[j]])
                wrel(1)
                if u_ in wfo_at:
                    wfo_load(wfo_at[u_])
            chk(8)
            ns1 = norm_stream(32, 0)
            next(ns1)
            for t in range(NT):
                if blk + 1 < NB and t >= 2:
                    ns1.send(t - 2)
                for nh in range(2):
                    bk = nbank()
                    for kc in range(22):
                        P.op("pe", L("matmul", banks[bk][:, :], lhsT=actT[:, kc, t * 128:(t + 1) * 128], rhs=wfo[:, kc, nh * 512:(nh + 1) * 512],
                                     start=(kc == 0), stop=(kc == 21)), reads=[RACT[kc], RWFO[kc // 4]], writes=[RB[bk]])
                    if t == NT - 1 and nh == 1:
                        snap_ffn = P.snapshot()
                    par = nh
                    P.op("dve", L("tensor_tensor", out=tmpf[par][:], in0=banks[bk][:, :], in1=gt_bc[:, 1, nh * 512:(nh + 1) * 512], op=ALU.mult),
                         reads=[RB[bk], Rbc, Rgt], writes=[Rtf[par]])
                    P.op("dve", L("tensor_tensor", out=X[:, t, nh * 512:(nh + 1) * 512], in0=X[:, t, nh * 512:(nh + 1) * 512], in1=tmpf[par][:], op=ALU.add),
                         reads=[Rtf[par], RX[t]], writes=[RX[t]])
                par = t % 2
                P.op("act", L("activation", tmpf[par][:].bitcast(BF16), X[:, t, :], AF.Square, accum_out=ncol[par][:, 0:1]), reads=[RX[t]], writes=[Rtf[par], Rnc[par]])
                P.op("act", L("activation", ncol[par][:, 1:2], ncol[par][:, 0:1], AF.Ln, bias=EPS, scale=1.0 / D), reads=[Rnc[par]], writes=[Rnc[par]])
                P.op("act", L("activation", ncol[par][:, 2:3], ncol[par][:, 1:2], AF.Exp, scale=-0.5), reads=[Rnc[par]], writes=[Rnc[par]])
                P.op("dve", L("scalar_tensor_tensor", out=X[:, t, :], in0=X[:, t, :], scalar=ncol[par][:, 2:3], in1=nfw_bc[:],
                              op0=ALU.mult, op1=ALU.mult), reads=[RX[t], Rnc[par], Rbc], writes=[RX[t]])
                P.dma("sp", [L("dma_start", out=y_d[tok0 + t * 128: tok0 + (t + 1) * 128, :], in_=X[:, t, :])], RX[t], reads=[RX[t]])
                if blk + 1 < NB:
                    ntok = tok0 + TB
                    P.dma("sp", [L("dma_start", out=X[:, t, :], in_=x_d[ntok + t * 128: ntok + (t + 1) * 128, :])], RX[t], writes=[RX[t]])
            if blk + 1 < NB:
                carry["ns"] = ns1
                carry["pend"] = [NT - 2, NT - 1, None, None]
            P.apply(snap_ffn)
        try:
            body()
        except _Stop:
            pass
        finals = [("d", RX[t], RX[t].cnt) for t in range(NT) if RX[t].cnt]
        if debug:
            P.barrier()
            dumps = dict(mk_bf=mk_bf, pcol=pcol, mcol=mcol, gt_bc=gt_bc, cw31c=cw31c, cw4c=cw4c, bcen=bcen, tri32=tri32, nm32=nm32, imp32=imp32,
                         su_bf=su_bf, gm_bf=gm_bf, cact_bf=cact_bf, abc=abc, hT=hT, qfm=qfm, kfm=kfm, ktm=ktm, vtm=vtm, zs=zs, lg=lg,
                         u_v=u_v, S32=S32, X=X, scb=scb, outb=outb, Ybf=Ybf)
            for name in debug:
                t_ = dumps[name]
                ap_ = t_ if isinstance(t_, bass.AP) else t_[:]
                dd = nc.dram_tensor("dbg_" + name, list(ap_.shape), ap_.dtype, kind="ExternalOutput").ap()
                rr = Res("dbg_" + name)
                finals.append(P.dma("sp", [L("dma_start", out=dd, in_=ap_)], rr))
        P.emit(finals)
    return nc


_NC_CACHE = {}


def kernel(x, c, w_ada, b_ada, norm_mix_w, w_in, conv_w, conv_b, conv_gn_w, conv_gn_b,
           gdn_conv_w, gdn_a_log, gdn_dt_bias, gdn_norm_w, w_out, norm_ffn_w,
           w_ffn_in, w_ffn_out, norm_final_w):
    f = lambda a: np.ascontiguousarray(np.asarray(a, dtype=np.float32))
    x = f(x); c = f(c)
    if "nc" not in _NC_CACHE:
        _NC_CACHE["nc"] = build()
    nc = _NC_CACHE["nc"]
    cw31 = f(conv_w)[0].reshape(31, 4, 128).reshape(124, 128)
    cw4 = f(gdn_conv_w)[0].reshape(4, 12, 128).reshape(48, 128)
    shared = {
        "cw31": cw31, "cw4": cw4,
        "brow": f(b_ada).reshape(1, 6144),
        "nfw": f(norm_final_w).reshape(1, D),
        "gnw": f(gdn_norm_w).reshape(1, 128),
        "alog": f(gdn_a_log).reshape(1, 4),
        "dtb": f(gdn_dt_bias).reshape(1, 4),
        "w_ada": f(w_ada)[0], "w_in": f(w_in)[0], "w_out": f(w_out)[0],
        "w_ffn_in": f(w_ffn_in)[0], "w_ffn_out": f(w_ffn_out)[0],
    }
    in_maps = []
    for b in range(8):
        prow = np.concatenate([f(b_ada).reshape(48, 128), c[b].reshape(8, 128), f(norm_mix_w).reshape(8, 128),
                               f(norm_ffn_w).reshape(8, 128), f(conv_b).reshape(4, 128), f(conv_gn_w).reshape(4, 128),
                               f(conv_gn_b).reshape(4, 128)], axis=0)
        m = dict(shared)
        m["x"] = x[b]
        m["prow"] = np.ascontiguousarray(prow)
        in_maps.append(m)
    res = run_bass_kernel_spmd(nc, in_maps, core_ids=list(range(8)))
    return np.stack([r["y"] for r in res.results], axis=0).astype(np.float32)
```

```python
import contextlib
import numpy as np
import concourse.bass as bass
import concourse.mybir as mybir
from concourse.bass_utils import run_bass_kernel_spmd

F32 = mybir.dt.float32
BF16 = mybir.dt.bfloat16
AF = mybir.ActivationFunctionType
ALU = mybir.AluOpType

D = 1024
S = 2048
TB = 1024
NB = S // TB
NT = TB // 128
NS = TB // 512
N_IN = 3080
D_FF = 2816
EPS = 1e-6
ENGS = ("pe", "act", "dve", "pool", "sp")


class Res:
    __slots__ = ("name", "w", "r", "sem", "cnt", "excl")

    def __init__(self, name, excl=False):
        self.name = name
        self.excl = excl
        self.w = None
        self.r = []
        self.sem = None
        self.cnt = 0


class Prog:
    def __init__(self, nc):
        self.nc = nc
        self.ops = {e: [] for e in ENGS}
        self.dma_res = []
        self.bar = {e: [] for e in ENGS}

    def _deps(self, eng, reads, writes):
        deps = []
        for r in reads:
            if r.w is not None:
                deps.append(r.w)
            if r.excl:
                deps.extend(x for x in r.r if x[1] != eng)
        for r in writes:
            if r.w is not None:
                deps.append(r.w)
            deps.extend(r.r)
        if self.bar[eng]:
            deps.extend(self.bar[eng])
            self.bar[eng] = []
        return [d for d in deps if not (d[0] == "e" and d[1] == "pe" and eng == "pe")]

    def op(self, eng, fn, reads=(), writes=()):
        deps = self._deps(eng, reads, writes)
        idx = len(self.ops[eng])
        self.ops[eng].append(dict(fn=fn, deps=deps, sig=False, dma=None))
        ref = ("e", eng, idx)
        for r in reads:
            r.r.append(ref)
        for r in writes:
            r.w = ref
            r.r = []
        return ref

    def dma(self, eng, fns, key, reads=(), writes=()):
        deps = self._deps(eng, reads, writes)
        if key.sem is None:
            self.dma_res.append(key)
            key.sem = True
        for i, fn in enumerate(fns):
            key.cnt += 16
            self.ops[eng].append(dict(fn=fn, deps=deps if i == 0 else [], sig=False, dma=key))
        ref = ("d", key, key.cnt)
        for r in reads:
            r.r.append(ref)
        for r in writes:
            r.w = ref
            r.r = []
        return ref

    def snapshot(self):
        last = []
        for e in ENGS:
            for i in range(len(self.ops[e]) - 1, -1, -1):
                if self.ops[e][i]["dma"] is None:
                    last.append(("e", e, i))
                    break
        return last

    def apply(self, snap):
        for e in ENGS:
            self.bar[e] = self.bar[e] + list(snap)

    def barrier(self):
        last = []
        for e in ENGS:
            for i in range(len(self.ops[e]) - 1, -1, -1):
                if self.ops[e][i]["dma"] is None:
                    last.append(("e", e, i))
                    break
        for k in self.dma_res:
            last.append(("d", k, k.cnt))
        for e in ENGS:
            self.bar[e] = list(last)

    def emit(self, final_refs):
        nc = self.nc
        allops = self.ops
        for e in ENGS:
            for o in allops[e]:
                for d in o["deps"]:
                    if d[0] == "e":
                        allops[d[1]][d[2]]["sig"] = True
        counts = {}
        for e in ENGS:
            c = 0
            lst = []
            for o in allops[e]:
                if o["sig"]:
                    c += 1
                lst.append(c)
            counts[e] = lst
        with contextlib.ExitStack() as st:
            esem = {e: st.enter_context(nc.semaphore("s_" + e)) for e in ENGS}
            for k in self.dma_res:
                k.sem = st.enter_context(nc.semaphore("d_" + k.name))
            block = st.enter_context(nc.Block())

            def run(e, engine):
                waited = {}

                def do_waits(deps):
                    need = {}
                    for d in deps:
                        if d[0] == "e":
                            s_, v = esem[d[1]], counts[d[1]][d[2]]
                        else:
                            s_, v = d[1].sem, d[2]
                        if v > need.get(s_, 0):
                            need[s_] = v
                    for s_, v in need.items():
                        if waited.get(s_, 0) >= v:
                            continue
                        waited[s_] = v
                        engine.wait_ge(s_, v)

                for o in allops[e]:
                    do_waits(o["deps"])
                    ins = o["fn"](engine)
                    if o["dma"] is not None:
                        ins.then_inc(o["dma"].sem, 16)
                    elif o["sig"]:
                        ins.then_inc(esem[e], 1)
                if e == "sp":
                    do_waits(final_refs)

            @block.tensor
            def _(eng):
                run("pe", eng)

            @block.scalar
            def _(eng):
                run("act", eng)

            @block.vector
            def _(eng):
                run("dve", eng)

            @block.gpsimd
            def _(eng):
                run("pool", eng)

            @block.sync
            def _(eng):
                run("sp", eng)


def L(name, *a, **k):
    return lambda e: getattr(e, name)(*a, **k)


class _Stop(Exception):
    pass


def build(debug=None, stage=99):
    nc = bass.Bass("TRN2", target_bir_lowering=False)
    dt_in = lambda name, shape: nc.dram_tensor(name, shape, F32, kind="ExternalInput").ap()
    x_d = dt_in("x", [S, D])
    prow_d = dt_in("prow", [84, 128])
    cw31_d = dt_in("cw31", [124, 128])
    cw4_d = dt_in("cw4", [48, 128])
    brow_d = dt_in("brow", [1, 6144])
    nfw_d = dt_in("nfw", [1, D])
    gnw_d = dt_in("gnw", [1, 128])
    alog_d = dt_in("alog", [1, 4])
    dtb_d = dt_in("dtb", [1, 4])
    wada_d = dt_in("w_ada", [D, 6 * D])
    win_d = dt_in("w_in", [D, N_IN])
    wout_d = dt_in("w_out", [D, D])
    wfi_d = dt_in("w_ffn_in", [D, 2 * D_FF])
    wfo_d = dt_in("w_ffn_out", [D_FF, D])
    y_d = nc.dram_tensor("y", [S, D], F32, kind="ExternalOutput").ap()
    P = Prog(nc)
    with contextlib.ExitStack() as st:
        def sb(name, shape, dt=F32):
            return st.enter_context(nc.sbuf_tensor(name, shape, dt))

        X = sb("X", [128, NT, D], F32)
        hT = sb("hT", [128, 8, TB], BF16)
        NW = 3
        wring = [sb(f"wr{i}", [128, 4096], BF16) for i in range(NW)]
        A1N = 22 * 1024 + 256
        arena1 = sb("arena1", [128, A1N], BF16)
        R2N = 12 * 4 * 128 + 31 * 128 + NT * 512 + NT * 16 + 1024 + 10 * 512 + 14 * 512
        region2 = sb("region2", [128, R2N], BF16)
        ident_bf = sb("ident_bf", [128, 128], BF16)
        nident_bf = sb("nident_bf", [128, 128], BF16)
        ones_bf = sb("ones_bf", [128, 128], BF16)
        su_bf = sb("su_bf", [128, 128], BF16)
        gm_bf = sb("gm_bf", [128, 128], BF16)
        mk_bf = sb("mk_bf", [128, 5, 128], BF16)
        ident32 = sb("ident32", [128, 128], F32)
        ones32 = sb("ones32", [128, 128], F32)
        nones32 = sb("nones32", [128, 128], F32)
        tri32 = sb("tri32", [128, 128], F32)
        nm32 = sb("nm32", [128, 128], F32)
        imp32 = sb("imp32", [128, 128], F32)
        pcol = sb("pcol", [128, 84], F32)
        cw31c = sb("cw31c", [128, 124], F32)
        cw4c = sb("cw4c", [128, 48], F32)
        mcol = sb("mcol", [128, 48], F32)
        bcen = sb("bcen", [128, 8], F32)
        cact_bf = sb("cact_bf", [128, 8], BF16)
        cactbc = sb("cactbc", [128, 8, 128], BF16)
        gt_bc = sb("gt_bc", [128, 2, D], F32)
        nfw_bc = sb("nfw_bc", [128, D], F32)
        gnw_bc = sb("gnw_bc", [128, 128], F32)
        abc = sb("abc", [128, 4], F32)
        dtb_bc = sb("dtb_bc", [128, 4], F32)
        S32 = sb("S32", [128, 4, 128], F32)
        Sbf = sb("Sbf", [128, 4, 128], BF16)
        rawtail = sb("rawtail", [128, 12, 4], BF16)
        utail = sb("utail", [128, 4, 32], BF16)
        wlg = sb("wlg", [128, 8, 8], BF16)
        tmpf = [sb(f"tmpf{i}", [128, 512], F32) for i in range(2)]
        tmpb = [sb(f"tmpb{i}", [128, 512], BF16) for i in range(2)]
        xnb = [sb(f"xnb{i}", [128, D], BF16) for i in range(2)]
        ncol = [sb(f"ncol{i}", [128, 4], F32) for i in range(2)]
        scb = sb("scb", [128, 16, NT * 4], F32)
        junk = sb("junk", [128, 128], BF16)
        Rjunk = Res("junk")
        banks = [st.enter_context(nc.psum_tensor(f"bk{i}", [128, 512], F32)) for i in range(8)]
        RB = [Res(f"bk{i}", excl=True) for i in range(8)]

        def bkbf(i):
            return banks[i][:].bitcast(BF16)

        RX = [Res(f"X{t}") for t in range(NT)]
        RHa = [Res(f"hTa{t}") for t in range(NT)]
        RHb = [Res(f"hTb{t}") for t in range(NT)]
        RW = [Res(f"wr{i}") for i in range(NW)]
        Rc = Res("consts")
        Rp = Res("params")
        RS = [Res(f"S{h}") for h in range(4)]
        RSB = [Res(f"Sb{h}") for h in range(4)]
        Rtf = [Res(f"tmpf{i}") for i in range(2)]
        Rtb = [Res(f"tmpb{i}") for i in range(2)]
        Rxn = [Res(f"xnb{i}") for i in range(2)]
        Rnc = [Res(f"ncol{i}") for i in range(2)]
        Rsc = Res("sc")
        Rms = [Res("ms0"), Res("ms1")]
        Rtail = Res("tails")
        Rwlg = Res("wlg")

        o = 0
        def carve(n):
            nonlocal o
            v = arena1[:, o:o + n]
            o += n
            return v
        UW = 32 + TB
        u_v = carve(4 * UW).rearrange("p (c t) -> p c t", c=4)
        RAWW = 4 + TB
        raw_v = [carve(RAWW) for _ in range(2)]
        tmpg = [raw_v[i][:, 0:1024].bitcast(F32) for i in range(2)]
        qfm = carve(4 * TB).rearrange("p (c t) -> p c t", c=4)
        kfm = carve(4 * TB).rearrange("p (c t) -> p c t", c=4)
        ktm = carve(NT * 512).rearrange("p (t f) -> p t f", t=NT)
        vtm = carve(NT * 512).rearrange("p (t f) -> p t f", t=NT)
        assert o <= A1N, o
        actT = arena1[:, 0:22 * TB].rearrange("p (c t) -> p c t", c=22)
        RU = [Res(f"u{c}") for c in range(4)]
        RRAW = [Res(f"raw{i}") for i in range(2)]
        Rtg = RRAW
        RQ = [[Res(f"q{h}_{t}") for t in range(NT)] for h in range(4)]
        RK = [[Res(f"k{h}_{t}") for t in range(NT)] for h in range(4)]
        RKT = [Res(f"kt{t}") for t in range(NT)]
        RVT = [Res(f"vt{t}") for t in range(NT)]
        RACT = [Res(f"act{j}") for j in range(22)]

        o2 = 0
        def carve2(n):
            nonlocal o2
            v = region2[:, o2:o2 + n]
            o2 += n
            return v
        W4 = carve2(12 * 4 * 128).rearrange("p (f k c) -> p f k c", f=12, k=4)
        Wc = carve2(31 * 128).rearrange("p (k c) -> p k c", k=31)
        zs = carve2(NT * 512).rearrange("p (t f) -> p t f", t=NT)
        lg = carve2(NT * 8 * 2).bitcast(F32).rearrange("p (t f) -> p t f", t=NT)
        def c2h():
            return carve2(512).rearrange("p (h t) -> p h t", h=4)

        class Chain:
            pass
        CH = [Chain(), Chain()]
        oa = [0]
        def carveA(n):
            v = arena1[:, oa[0]:oa[0] + n]
            oa[0] += n
            return v
        ow = [0]
        def carveW(n):
            v = region2[:, ow[0]:ow[0] + n]
            ow[0] += n
            return v
        for ci, cv in enumerate([carve2, carveW]):
            B = CH[ci]
            B.gtri = cv(1024).bitcast(F32).rearrange("p (h t) -> p h t", h=4)
            for nm in ["dti", "egb", "A0", "A0T", "A1", "AT1", "qsq", "ksq", "Zb", "ZTb"]:
                setattr(B, nm, cv(512).rearrange("p (h t) -> p h t", h=4))
            B.R = {nm: Res(f"c{ci}{nm}") for nm in ["gtri", "dti", "egb", "A0", "A0T", "A1", "AT1", "qsq", "ksq", "Zb", "ZTb"]}
            B.bx, B.by = (0, 3) if ci == 0 else (4, 5)
        assert ow[0] <= 12 * 4 * 128
        Zfin = [c2h() for _ in range(4)]
        qkdt = [c2h() for _ in range(4)]
        RZF = [Res(f"zfin{i}") for i in range(4)]
        RQK = [Res(f"qkdt{i}") for i in range(4)]
        Vr = c2h()
        wz = c2h()
        rhs2 = c2h()
        Ybf = c2h()
        Ykbf = c2h()
        outb = c2h()
        assert o2 <= R2N, o2
        wfo = region2[:, 0:22 * 1024].rearrange("p (c n) -> p c n", c=22)
        RW4 = [[Res(f"W4_{f}_{k}") for k in range(4)] for f in range(12)]
        RWc = Res("Wc")
        RWck = [Res(f"Wc{k}") for k in range(31)]
        RZ = [Res(f"zs{t}") for t in range(NT)]
        RLG = [Res(f"lg{t}") for t in range(NT)]
        Rg = {k: Res(k) for k in ["Vr", "wz", "Y", "Yk", "outb"]}
        Rrhs2 = [Res(f"rhs2{h}") for h in range(4)]
        RWFO = [Res(f"wfo{i}") for i in range(6)]

        specs = []
        def win_group(g):
            src = win_d.rearrange("(c p) n -> p c n", p=128)[:, :, g * 512:(g + 1) * 512]
            return lambda slot: [(slot[:, 0:4096].rearrange("p (c n) -> p c n", c=8), src)]
        def wout_group(i):
            src = wout_d.rearrange("(c p) n -> p c n", p=128)[:, 4 * i:4 * i + 4, :]
            return lambda slot: [(slot[:, 0:4096].rearrange("p (c n) -> p c n", c=4), src)]
        def wfi_unit(u_):
            v = wfi_d.rearrange("(c p) n -> p c n", p=128)
            return lambda slot: [(slot[:, 0:2048].rearrange("p (c n) -> p c n", c=8), v[:, :, u_ * 256:(u_ + 1) * 256]),
                                 (slot[:, 2048:4096].rearrange("p (c n) -> p c n", c=8), v[:, :, D_FF + u_ * 256:D_FF + (u_ + 1) * 256])]
        in_order = [5, 0, 1, 2, 3, 4]
        def wada_group(pc, half):
            src = wada_d.rearrange("(c p) n -> p c n", p=128)[:, :, pc * D + half * 512:pc * D + (half + 1) * 512]
            return lambda slot: [(slot[:, 0:4096].rearrange("p (c n) -> p c n", c=8), src)]
        LATE = [(3, "col", 2, 24), (4, "col", 3, 32), (2, "gt", 0, 0), (5, "gt", 1, 0)]
        for blk in range(NB):
            for gi_, g in enumerate(in_order):
                specs.append(win_group(g))
                if blk == 0 and gi_ == 5:
                    for pc, _k, _i, _b in LATE:
                        for half in range(2):
                            specs.append(wada_group(pc, half))
            specs.append(wout_group(0))
            specs.append(wout_group(1))
            for u_ in range(11):
                specs.append(wfi_unit(u_))
        wstate = dict(issued=0, released=0, next=0)

        def wpump():
            while wstate["issued"] < len(specs) and wstate["issued"] - NW < wstate["released"]:
                j = wstate["issued"]
                slot = wring[j % NW]
                pairs = specs[j](slot)
                P.dma("pool", [L("dma_start", out=d_, in_=s_) for d_, s_ in pairs], RW[j % NW], writes=[RW[j % NW]])
                wstate["issued"] += 1

        def wnext():
            i = wstate["next"]
            wstate["next"] += 1
            wpump()
            assert wstate["issued"] > i
            return wring[i % NW], RW[i % NW]

        def wrel(n=1):
            wstate["released"] += n
            wpump()

        prow = arena1[:, 0:256].bitcast(F32)
        cw31r = arena1[:, 256:512].bitcast(F32)
        cw4r = arena1[:, 512:768].bitcast(F32)
        Rst = Res("stage")
        Rst2 = Res("stage2")
        Rst3 = Res("stage3")
        P.dma("sp", [L("dma_start", out=prow[0:84, :], in_=prow_d)], Rst, writes=[Rst])
        P.dma("sp", [L("dma_start", out=cw31r[0:124, :], in_=cw31_d)], Rst2, writes=[Rst2])
        P.dma("sp", [L("dma_start", out=cw4r[0:48, :], in_=cw4_d)], Rst3, writes=[Rst3])
        Rbc = Res("bcasts")
        Rgt = Res("gt")
        Rm2 = Res("mod2")
        P.dma("sp", [L("dma_start", out=nfw_bc[:], in_=nfw_d.partition_broadcast(128)),
                     L("dma_start", out=gnw_bc[:], in_=gnw_d.partition_broadcast(128)),
                     L("dma_start", out=abc[:], in_=alog_d.partition_broadcast(128)),
                     L("dma_start", out=dtb_bc[:], in_=dtb_d.partition_broadcast(128)),
                     L("dma_start", out=gt_bc[:, 0, :], in_=brow_d[:, 2 * D:3 * D].partition_broadcast(128)),
                     L("dma_start", out=gt_bc[:, 1, :], in_=brow_d[:, 5 * D:6 * D].partition_broadcast(128))],
              Rbc, writes=[Rbc])
        wad = [region2[:, 0:8192].rearrange("p (c n) -> p c n", c=8), region2[:, 8192:16384].rearrange("p (c n) -> p c n", c=8)]
        Rwad = [Res("wad0"), Res("wad1")]
        wada_v = wada_d.rearrange("(c p) n -> p c n", p=128)
        for i in range(2):
            P.dma("pool", [L("dma_start", out=wad[i], in_=wada_v[:, :, i * D:(i + 1) * D])], Rwad[i], writes=[Rwad[i]])
        for t in range(NT):
            P.dma("sp", [L("dma_start", out=X[:, t, :], in_=x_d[t * 128:(t + 1) * 128, :])], RX[t], reads=(Rwad if t >= 2 else []), writes=[RX[t]])
        def mk_mask(t, cmp, fill, base_val=1.0, step=-1, cm=1, eng="pool"):
            P.op(eng, L("memset", t[:], base_val), writes=[Rc])
            P.op(eng, L("affine_select", out=t[:], in_=t[:], pattern=[[step, 128]], compare_op=cmp, fill=fill, base=0,
                        channel_multiplier=cm), reads=[Rc], writes=[Rc])
        mk_mask(ident32, ALU.is_equal, 0.0)
        mk_mask(tri32, ALU.is_ge, 0.0, step=1, cm=-1)
        mk_mask(nm32, ALU.is_ge, -1e30, base_val=0.0, step=1, cm=-1)
        P.op("pool", L("memset", ones32[:], 1.0), writes=[Rc])
        P.op("pool", L("memset", nones32[:], -1.0), writes=[Rc])
        P.op("pool", L("memset", ones_bf[:], 1.0), writes=[Rc])
        P.op("pool", L("tensor_copy", ident_bf[:], ident32[:]), reads=[Rc], writes=[Rc])
        P.op("pool", L("tensor_scalar", nident_bf[:], ident32[:], -1.0, None, op0=ALU.mult), reads=[Rc], writes=[Rc])
        P.op("pool", L("tensor_tensor", out=su_bf[:], in0=tri32[:], in1=ident32[:], op=ALU.subtract), reads=[Rc], writes=[Rc])
        P.op("pool", L("memset", imp32[:], 0.0), writes=[Rc])
        P.op("pool", L("memset", imp32[0:64, 0:64], 1.0 / 64), writes=[Rc])
        P.op("pool", L("memset", imp32[64:128, 64:128], 1.0 / 64), writes=[Rc])
        P.op("pool", L("tensor_copy", gm_bf[:], imp32[:]), reads=[Rc], writes=[Rc])
        P.op("pool", L("tensor_tensor", out=imp32[:], in0=ident32[:], in1=imp32[:], op=ALU.subtract), reads=[Rc], writes=[Rc])
        P.op("pool", L("memset", S32[:], 0.0), writes=RS)
        P.op("pool", L("memset", Sbf[:], 0.0), writes=RSB)
        P.op("pool", L("memset", rawtail[:], 0.0), writes=[Rtail])
        P.op("pool", L("memset", utail[:], 0.0), writes=[Rtail])

        bdf = [arena1[:, 2048 + i * 256:2048 + (i + 1) * 256].bitcast(F32) for i in range(4)]
        for i, s_ in enumerate([8, 16, 32, 64]):
            v = bdf[i].rearrange("p (a b) -> p a b", b=s_)
            P.op("pool", L("memset", bdf[i], 1.0), writes=[Rc])
            P.op("pool", L("affine_select", out=v, in_=v, pattern=[[-s_, 128 // s_], [0, s_]], compare_op=ALU.is_ge, fill=0.0, base=0,
                           channel_multiplier=1), reads=[Rc], writes=[Rc])
            P.op("pool", L("affine_select", out=v, in_=v, pattern=[[s_, 128 // s_], [0, s_]], compare_op=ALU.is_ge, fill=0.0, base=s_ - 1,
                           channel_multiplier=-1), reads=[Rc], writes=[Rc])
        P.op("pool", L("tensor_copy", mk_bf[:, 0, :], bdf[0]), reads=[Rc], writes=[Rc])
        for i in range(3):
            P.op("pool", L("tensor_tensor", out=mk_bf[:, 1 + i, :], in0=bdf[i + 1], in1=bdf[i], op=ALU.subtract), reads=[Rc], writes=[Rc])
        P.op("pool", L("tensor_tensor", out=mk_bf[:, 4, :], in0=ones32[:], in1=bdf[3], op=ALU.subtract), reads=[Rc], writes=[Rc])
        P.dma("pool", [L("dma_start", out=wlg[:], in_=win_d.rearrange("(c p) n -> p c n", p=128)[:, :, 3072:3080])],
              Rwlg, writes=[Rwlg])
        wpump()
        P.op("pe", L("transpose", banks[0][:, 0:84], prow[0:84, :], ident32[0:84, 0:84]), reads=[Rst, Rc], writes=[RB[0]])
        P.op("pe", L("transpose", banks[0][:, 128:252], cw31r[0:124, :], ident32[0:124, 0:124]), reads=[Rst2, Rc], writes=[RB[0]])
        P.op("pe", L("transpose", banks[0][:, 256:304], cw4r[0:48, :], ident32[0:48, 0:48]), reads=[Rst3, Rc], writes=[RB[0]])
        P.op("dve", L("tensor_copy", pcol[:], banks[0][:, 0:84]), reads=[RB[0]], writes=[Rp])
        P.op("dve", L("tensor_scalar", cw31c[:], banks[0][:, 128:252], 0.5, None, op0=ALU.mult), reads=[RB[0]], writes=[Rp])
        P.op("dve", L("tensor_copy", cw4c[:], banks[0][:, 256:304]), reads=[RB[0]], writes=[Rp])
        P.op("act", L("activation", cact_bf[:], pcol[:, 48:56], AF.Silu), reads=[Rp], writes=[Rp])
        P.op("dve", L("tensor_copy", cactbc[:], cact_bf[:].unsqueeze(2).to_broadcast([128, 8, 128])), reads=[Rp], writes=[Rp])
        P.op("act", L("activation", abc[:], abc[:], AF.Exp), reads=[Rbc], writes=[Rbc])
        P.op("dve", L("tensor_scalar", abc[:], abc[:], -1.0, None, op0=ALU.mult), reads=[Rbc], writes=[Rbc])
        P.op("pe", L("matmul", banks[1][:, 0:4], lhsT=imp32[:], rhs=pcol[:, 72:76], start=True, stop=True), reads=[Rc, Rp], writes=[RB[1]])
        P.op("dve", L("tensor_copy", bcen[:, 0:4], banks[1][:, 0:4]), reads=[RB[1]], writes=[Rp])
        P.op("dve", L("tensor_scalar", bcen[:, 4:8], pcol[:, 80:84], -1.0, None, op0=ALU.mult), reads=[Rp], writes=[Rp])

        for i in range(2):
            for fb in range(8):
                for cch in range(8):
                    P.op("pe", L("matmul", banks[2][:, i * 8 + fb:i * 8 + fb + 1], lhsT=wad[i][:, cch, fb * 128:(fb + 1) * 128],
                                 rhs=cact_bf[:, cch:cch + 1], start=(cch == 0), stop=(cch == 7)),
                         reads=[Rwad[i], Rp], writes=[RB[2]])
        for i, boff in enumerate([0, 8]):
            P.op("dve", L("tensor_tensor", out=mcol[:, i * 8:(i + 1) * 8], in0=banks[2][:, i * 8:(i + 1) * 8],
                          in1=pcol[:, boff:boff + 8], op=ALU.add), reads=[RB[2], Rp], writes=[Rp])
        P.op("dve", L("scalar_tensor_tensor", out=mcol[:, 32:40], in0=mcol[:, 8:16], scalar=1.0, in1=pcol[:, 56:64],
                      op0=ALU.add, op1=ALU.mult), reads=[Rp], writes=[Rp])
        snap_setup = P.snapshot()

        dbg_list = []

        def norm_A1(t, par):
            P.op("act", L("activation", xnb[par][:], X[:, t, :], AF.Square, accum_out=ncol[par][:, 0:1]),
                 reads=[RX[t]], writes=[Rxn[par], Rnc[par]])
            P.op("act", L("activation", ncol[par][:, 1:2], ncol[par][:, 0:1], AF.Ln, bias=EPS, scale=1.0 / D),
                 reads=[Rnc[par]], writes=[Rnc[par]])
            P.op("act", L("activation", ncol[par][:, 2:3], ncol[par][:, 1:2], AF.Exp, scale=-0.5),
                 reads=[Rnc[par]], writes=[Rnc[par]])
            P.op("dve", L("tensor_scalar", xnb[par][:], X[:, t, :], ncol[par][:, 2:3], None, op0=ALU.mult),
                 reads=[RX[t], Rnc[par]], writes=[Rxn[par]])

        def norm_A2(t, par):
            bk = 6 + par
            pv = bkbf(bk)
            for cch in range(8):
                P.op("pe", L("transpose", pv[:, cch * 128:(cch + 1) * 128], xnb[par][:, cch * 128:(cch + 1) * 128], ident_bf[:]),
                     reads=[Rxn[par], Rc], writes=[RB[bk]])

        def norm_B(t, par, acol, shcol):
            bk = 6 + par
            pv = bkbf(bk)
            for cch in range(4):
                P.op("act", L("activation", hT[:, cch, t * 128:(t + 1) * 128], pv[:, cch * 128:(cch + 1) * 128], AF.Identity,
                              bias=mcol[:, shcol + cch:shcol + cch + 1], scale=mcol[:, acol + cch:acol + cch + 1]),
                     reads=[RB[bk], Rp, Rm2], writes=[RHa[t]])
            for cch in range(4, 8):
                P.op("dve", L("tensor_scalar", hT[:, cch, t * 128:(t + 1) * 128], pv[:, cch * 128:(cch + 1) * 128],
                              mcol[:, acol + cch:acol + cch + 1], mcol[:, shcol + cch:shcol + cch + 1], op0=ALU.mult, op1=ALU.add),
                     reads=[RB[bk], Rp, Rm2], writes=[RHb[t]])

        def norm_stream(acol, shcol):
            p1 = None
            p2 = None
            while True:
                t = yield
                if t is not None:
                    norm_A1(t, t % 2)
                if p1 is not None:
                    norm_A2(p1, p1 % 2)
                if p2 is not None:
                    norm_B(p2, p2 % 2, acol, shcol)
                p2 = p1
                p1 = t

        def norm_all(acol, shcol):
            ns = norm_stream(acol, shcol)
            next(ns)
            for t in range(NT):
                ns.send(t)
            ns.send(None)
            ns.send(None)

        bkrr = dict(i=0)

        def nbank(lst=(0, 1, 2, 3)):
            b = lst[bkrr["i"] % len(lst)]
            bkrr["i"] += 1
            return b

        evrr = dict(i=0)

        def chk(n):
            if stage == n:
                raise _Stop()

        carry = dict(ns=None, pend=[])

        def body():
          for blk in range(NB):
            tok0 = blk * TB
            chk(1)
            if blk == 0:
                norm_all(32, 0)
                P.apply(snap_setup)
            chk(2)
            wsl, wres = wnext()
            wv = wsl[:, 0:4096].rearrange("p (c n) -> p c n", c=8)
            for t in range(NT):
                if carry["pend"] and t >= 4:
                    carry["ns"].send(carry["pend"].pop(0))
                    if t == 4:
                        carry["ns"].send(carry["pend"].pop(0))
                bk = nbank()
                for cch in range(8):
                    P.op("pe", L("matmul", banks[bk][:, :], lhsT=hT[:, cch, t * 128:(t + 1) * 128], rhs=wv[:, cch, :],
                                 start=(cch == 0), stop=(cch == 7)), reads=[wres, RHa[t], RHb[t]], writes=[RB[bk]])
                P.op("act", L("activation", zs[:, t, :], banks[bk][:, :], AF.Silu), reads=[RB[bk]], writes=[RZ[t]])
                bk = nbank()
                for cch in range(8):
                    P.op("pe", L("matmul", banks[bk][:, 0:8], lhsT=hT[:, cch, t * 128:(t + 1) * 128], rhs=wlg[:, cch, :],
                                 start=(cch == 0), stop=(cch == 7)), reads=[Rwlg, RHa[t], RHb[t]], writes=[RB[bk]])
                P.op("dve", L("tensor_copy", lg[:, t, :], banks[bk][:, 0:8]), reads=[RB[bk]], writes=[RLG[t]])
            wrel(1)
            w4_list = [(f, k) for f in range(12) for k in range(4)]

            def w4_build(i0, i1):
                for (f, k) in w4_list[i0:i1]:
                    if (f * 4 + k) % 2 == 0:
                        P.op("dve", L("tensor_scalar", W4[:, f, k, :], ident32[:], cw4c[:, k * 12 + f:k * 12 + f + 1], None, op0=ALU.mult),
                             reads=[Rc, Rp], writes=[RW4[f][k]])
                    else:
                        P.op("act", L("activation", W4[:, f, k, :], ident32[:], AF.Copy, scale=cw4c[:, k * 12 + f:k * 12 + f + 1]),
                             reads=[Rc, Rp], writes=[RW4[f][k]])

            wa, wares = wnext()
            wg, wgres = wnext()
            wav = wa[:, 0:4096].rearrange("p (c n) -> p c n", c=8)
            wgv = wg[:, 0:4096].rearrange("p (c n) -> p c n", c=8)
            for cb in range(4):
                P.op("pool", L("tensor_copy", u_v[:, cb, 0:32], utail[:, cb, :]), reads=[Rtail], writes=[RU[cb]])
                for s in range(NS):
                    bka = nbank()
                    for cch in range(8):
                        P.op("pe", L("matmul", banks[bka][:, :], lhsT=wav[:, cch, cb * 128:(cb + 1) * 128],
                                     rhs=hT[:, cch, s * 512:(s + 1) * 512], start=(cch == 0), stop=(cch == 7)),
                             reads=[wares] + (RHa if cch < 4 else RHb)[4 * s:4 * s + 4], writes=[RB[bka]])
                    bkg = nbank()
                    for cch in range(8):
                        P.op("pe", L("matmul", banks[bkg][:, :], lhsT=wgv[:, cch, cb * 128:(cb + 1) * 128],
                                     rhs=hT[:, cch, s * 512:(s + 1) * 512], start=(cch == 0), stop=(cch == 7)),
                             reads=[wgres] + (RHa if cch < 4 else RHb)[4 * s:4 * s + 4], writes=[RB[bkg]])
                    par = s % 2
                    P.op("act", L("activation", tmpf[par][:], banks[bkg][:, :], AF.Tanh, scale=0.5), reads=[RB[bkg]], writes=[Rtf[par]])
                    P.op("dve", L("scalar_tensor_tensor", out=u_v[:, cb, 32 + s * 512:32 + (s + 1) * 512], in0=tmpf[par][:], scalar=1.0,
                                  in1=banks[bka][:, :], op0=ALU.add, op1=ALU.mult), reads=[Rtf[par], RB[bka]], writes=[RU[cb]])
                    w4_build((cb * NS + s) * 6, (cb * NS + s + 1) * 6)
                P.op("pool", L("tensor_copy", utail[:, cb, :], u_v[:, cb, TB:TB + 32]), reads=[RU[cb]], writes=[Rtail])
            wrel(2)
            chk(4)
            Q = lambda q: scb[:, q, :]
            Q3 = lambda q: scb[:, q, :].rearrange("p (t h) -> p t h", h=4)
            S1 = lambda q, t, h: scb[:, q, t * 4 + h:t * 4 + h + 1]
            S4 = lambda q, t: scb[:, q, t * 4:(t + 1) * 4]
            f4 = lambda v: v[:].rearrange("p h t -> p (h t)")
            bc4 = lambda m: m.unsqueeze(1).to_broadcast([128, 4, 128])

            def gdn_scalars():
                P.op("dve", L("tensor_tensor", out=Q3(0), in0=lg[:, :, 4:8], in1=dtb_bc[:].unsqueeze(1).to_broadcast([128, NT, 4]), op=ALU.add),
                     reads=RLG + [Rbc], writes=[Rsc])
                P.op("act", L("activation", Q(0), Q(0), AF.Exp), reads=[Rsc], writes=[Rsc])
                P.op("act", L("activation", Q(0), Q(0), AF.Ln, bias=1.0), reads=[Rsc], writes=[Rsc])
                P.op("dve", L("tensor_tensor", out=Q3(0), in0=Q3(0), in1=abc[:].unsqueeze(1).to_broadcast([128, NT, 4]), op=ALU.mult), reads=[Rsc, Rbc], writes=[Rsc])
                P.op("act", L("activation", Q3(1), lg[:, :, 0:4], AF.Exp, scale=-1.0), reads=RLG + [Rsc], writes=[Rsc])
                P.op("dve", L("tensor_scalar", Q(1), Q(1), 1.0, None, op0=ALU.add), reads=[Rsc], writes=[Rsc])
                P.op("dve", L("reciprocal", Q(1), Q(1)), reads=[Rsc], writes=[Rsc])
                for t in range(NT):
                    sq = tmpf[t % 2][:].bitcast(BF16)
                    qsq_v = sq[:, 0:512].rearrange("p (h t) -> p h t", h=4)
                    ksq_v = sq[:, 512:1024].rearrange("p (h t) -> p h t", h=4)
                    s_ = t // 4
                    tsl = slice(t * 128, (t + 1) * 128)
                    P.op("dve", L("tensor_tensor", out=qsq_v, in0=qfm[:, :, tsl], in1=qfm[:, :, tsl], op=ALU.mult),
                         reads=[RQ[h][t] for h in range(4)], writes=[Rtf[t % 2]])
                    P.op("dve", L("tensor_tensor", out=ksq_v, in0=kfm[:, :, tsl], in1=kfm[:, :, tsl], op=ALU.mult),
                         reads=[RK[h][t] for h in range(4)], writes=[Rtf[t % 2]])
                    for h in range(4):
                        P.op("pe", L("matmul", banks[4][:, t * 4 + h:t * 4 + h + 1], lhsT=qsq_v[:, h, :], rhs=ones_bf[:, 0:1], start=True, stop=True),
                             reads=[Rtf[t % 2], Rc], writes=[RB[4]])
                        P.op("pe", L("matmul", banks[4][:, 32 + t * 4 + h:32 + t * 4 + h + 1], lhsT=ksq_v[:, h, :], rhs=ones_bf[:, 0:1], start=True, stop=True),
                             reads=[Rtf[t % 2], Rc], writes=[RB[4]])
                P.op("pe", L("matmul", banks[4][:, 64:96], lhsT=tri32[:], rhs=Q(0), start=True, stop=True), reads=[Rc, Rsc], writes=[RB[4]])
                P.op("pe", L("matmul", banks[4][:, 96:128], lhsT=ones32[:], rhs=Q(0), start=True, stop=True), reads=[Rc, Rsc], writes=[RB[4]])
                P.op("act", L("activation", Q(13), banks[4][:, 32:64], AF.Ln, bias=EPS), reads=[RB[4], Rsc], writes=[Rsc])
                P.op("act", L("activation", Q(2), Q(13), AF.Exp, scale=-0.5), reads=[Rsc], writes=[Rsc])
                P.op("act", L("activation", Q(3), Q(13), AF.Exp, scale=0.5), reads=[Rsc], writes=[Rsc])
                P.op("dve", L("tensor_tensor", out=Q(4), in0=Q(2), in1=Q(2), op=ALU.mult), reads=[Rsc], writes=[Rsc])
                P.op("dve", L("tensor_tensor", out=Q(4), in0=Q(4), in1=Q(1), op=ALU.mult), reads=[Rsc], writes=[Rsc])
                P.op("dve", L("tensor_copy", scb[:, 5:7, :], banks[4][:, 64:128].rearrange("p (a b) -> p a b", a=2)), reads=[RB[4], Rsc], writes=[Rsc])
                P.op("act", L("activation", Q(7), Q(5), AF.Exp), reads=[Rsc], writes=[Rsc])
                P.op("dve", L("tensor_scalar", Q(8), Q(7), -1.0, None, op0=ALU.mult), reads=[Rsc], writes=[Rsc])
                P.op("dve", L("tensor_tensor", out=Q(9), in0=Q(6), in1=Q(5), op=ALU.subtract), reads=[Rsc], writes=[Rsc])
                P.op("act", L("activation", Q(9), Q(9), AF.Exp), reads=[Rsc], writes=[Rsc])
                P.op("dve", L("tensor_tensor", out=Q(10), in0=Q(9), in1=Q(4), op=ALU.mult), reads=[Rsc], writes=[Rsc])
                P.op("act", L("activation", Q(11), Q(6), AF.Exp), reads=[Rsc], writes=[Rsc])
                P.op("dve", L("tensor_scalar", Q(12), banks[4][:, 0:32], EPS, EPS * 128.0, op0=ALU.add, op1=ALU.mult), reads=[RB[4], Rsc], writes=[Rsc])

            kinds = ["q", "k", "v"]
            wcur = {}

            def qkv_A(fidx):
                gi, h = fidx // 4, fidx % 4
                if h == 0:
                    wsl, wres = wnext()
                    wcur["v"] = wsl[:, 0:4096].rearrange("p (c n) -> p c n", c=8)
                    wcur["r"] = wres
                wv, wres = wcur["v"], wcur["r"]
                rs = fidx % 2
                raw = raw_v[rs]
                P.op("pool", L("tensor_copy", raw[:, 0:4], rawtail[:, fidx, :]), reads=[Rtail], writes=[RRAW[rs]])
                for s in range(NS):
                    bk = nbank()
                    for cch in range(8):
                        P.op("pe", L("matmul", banks[bk][:, :], lhsT=wv[:, cch, h * 128:(h + 1) * 128],
                                     rhs=hT[:, cch, s * 512:(s + 1) * 512], start=(cch == 0), stop=(cch == 7)),
                             reads=[wres] + (RHa if cch < 4 else RHb)[4 * s:4 * s + 4], writes=[RB[bk]])
                    if evrr["i"] % 2 == 0:
                        P.op("act", L("copy", raw[:, 4 + s * 512:4 + (s + 1) * 512], banks[bk][:, :]), reads=[RB[bk]], writes=[RRAW[rs]])
                    else:
                        P.op("dve", L("tensor_copy", raw[:, 4 + s * 512:4 + (s + 1) * 512], banks[bk][:, :]), reads=[RB[bk]], writes=[RRAW[rs]])
                    evrr["i"] += 1
                P.op("pool", L("tensor_copy", rawtail[:, fidx, :], raw[:, TB:TB + 4]), reads=[RRAW[rs]], writes=[Rtail])
                if h == 3:
                    wrel(1)

            def qkv_B(fidx):
                gi, h = fidx // 4, fidx % 4
                kind = kinds[gi]
                rs = fidx % 2
                raw = raw_v[rs]
                if kind in ("q", "k"):
                    dst = qfm if kind == "q" else kfm
                    RR = RQ if kind == "q" else RK
                    for s in range(NS):
                        bk = nbank()
                        for k in range(4):
                            P.op("pe", L("matmul", banks[bk][:, :], lhsT=W4[:, fidx, k, :],
                                         rhs=raw[:, s * 512 + 1 + k:s * 512 + 1 + k + 512], start=(k == 0), stop=(k == 3)),
                                 reads=[RW4[fidx][k], RRAW[rs]], writes=[RB[bk]])
                        P.op("act", L("activation", dst[:, h, s * 512:(s + 1) * 512], banks[bk][:, :], AF.Silu),
                             reads=[RB[bk]], writes=RR[h][4 * s:4 * s + 4])
                if kind in ("k", "v"):
                    dst = ktm if kind == "k" else vtm
                    RR = RKT if kind == "k" else RVT
                    for t4 in range(NT // 4):
                        bk = nbank()
                        for tt in range(4):
                            t = t4 * 4 + tt
                            for k in range(4):
                                P.op("pe", L("matmul", banks[bk][:, tt * 128:(tt + 1) * 128],
                                             lhsT=raw[:, t * 128 + 1 + k:t * 128 + 1 + k + 128], rhs=W4[:, fidx, k, :],
                                             start=(k == 0), stop=(k == 3)), reads=[RW4[fidx][k], RRAW[rs]], writes=[RB[bk]])
                        P.op("act", L("activation", dst[:, t4 * 4:t4 * 4 + 4, h * 128:(h + 1) * 128],
                                      banks[bk][:, :].rearrange("p (t f) -> p t f", t=4), AF.Silu),
                             reads=[RB[bk]], writes=RR[t4 * 4:t4 * 4 + 4])
                if fidx == 7:
                    gdn_scalars()

            qkv_A(0)
            for fidx in range(12):
                if fidx + 1 < 12:
                    qkv_A(fidx + 1)
                qkv_B(fidx)
            snap_conv4 = P.snapshot()
            chk(3)
            def build_wc(cb, k0=0, k1=31):
                for k in range(k0, k1):
                    if k % 2 == 0:
                        P.op("dve", L("tensor_scalar", Wc[:, k, :], imp32[:], cw31c[:, k * 4 + cb:k * 4 + cb + 1], None, op0=ALU.mult),
                             reads=[Rc, Rp], writes=[RWck[k]])
                    else:
                        P.op("act", L("activation", Wc[:, k, :], imp32[:], AF.Copy, scale=cw31c[:, k * 4 + cb:k * 4 + cb + 1]),
                             reads=[Rc, Rp], writes=[RWck[k]])

            def gn(cb, s, bk, bk2):
                par = s % 2
                P.op("act", L("activation", tmpb[par][:], banks[bk][:, :], AF.Square, bias=bcen[:, cb:cb + 1]),
                     reads=[RB[bk], Rp], writes=[Rtb[par]])
                P.op("act", L("activation", tmpf[par][:], banks[bk][:, :], AF.Identity, bias=bcen[:, cb:cb + 1]),
                     reads=[RB[bk], Rp], writes=[Rtf[par]])
                yield
                P.op("pe", L("matmul", banks[bk2][:, :], lhsT=gm_bf[:], rhs=tmpb[par][:], start=True, stop=True),
                     reads=[Rc, Rtb[par]], writes=[RB[bk2]])
                yield
                P.op("act", L("activation", tmpg[par], banks[bk2][:, :], AF.Ln, bias=EPS), reads=[RB[bk2]], writes=[Rtg[par]])
                P.op("act", L("activation", tmpg[par], tmpg[par], AF.Exp, scale=-0.5), reads=[Rtg[par]], writes=[Rtg[par]])
                yield
                P.op("dve", L("scalar_tensor_tensor", out=hT[:, cb, s * 512:(s + 1) * 512], in0=tmpf[par][:], scalar=pcol[:, 76 + cb:77 + cb],
                              in1=tmpg[par], op0=ALU.mult, op1=ALU.mult), reads=[Rtf[par], Rtg[par], Rp], writes=RHa[4 * s:4 * s + 4])
                yield

            def conv31gen():
                cbanks = [1, 7]
                ci = 0
                for k0 in range(0, 31, 8):
                    build_wc(0, k0, min(31, k0 + 8))
                    yield
                for cb in range(4):
                    for s in range(NS):
                        bk = cbanks[ci % 2]
                        bk2 = cbanks[(ci + 1) % 2]
                        ci += 1
                        for k in range(31):
                            P.op("pe", L("matmul", banks[bk][:, :], lhsT=Wc[:, k, :], rhs=u_v[:, cb, s * 512 + 2 + k:s * 512 + 2 + k + 512],
                                         start=(k == 0), stop=(k == 30)), reads=[RWck[k], RU[cb]], writes=[RB[bk]])
                            if k % 4 == 3:
                                yield
                        yield
                        if s == NS - 1 and cb + 1 < 4:
                            for k0 in range(0, 31, 8):
                                build_wc(cb + 1, k0, min(31, k0 + 8))
                                yield
                        yield from gn(cb, s, bk, bk2)
            chk(5)
            P.apply(snap_conv4)
            chk(51)

            def prep(t):
                B = CH[t % 2]
                R = B.R
                o4 = t % 4
                s_ = t // 4
                tsl = slice(t * 128, (t + 1) * 128)
                bx, by = B.bx, B.by
                HS = [slice(h * 128, (h + 1) * 128) for h in range(4)]
                for h in range(4):
                    P.op("act", L("activation", B.gtri[:, h, :], tri32[:], AF.Copy, scale=S1(0, t, h)), reads=[Rc, Rsc], writes=[R["gtri"]])
                yield
                for h in range(4):
                    hs = HS[h]
                    P.op("pe", L("matmul", banks[bx][:, hs], lhsT=ones32[:], rhs=B.gtri[:, h, :], start=True, stop=False), reads=[Rc, R["gtri"]], writes=[RB[bx]])
                    P.op("pe", L("matmul", banks[bx][:, hs], lhsT=B.gtri[:, h, :], rhs=nones32[:], start=False, stop=False), reads=[Rc, R["gtri"]], writes=[RB[bx]])
                    P.op("pe", L("matmul", banks[bx][:, hs], lhsT=ident32[:], rhs=nm32[:], start=False, stop=True), reads=[Rc], writes=[RB[bx]])
                    P.op("pe", L("matmul", banks[by][:, hs], lhsT=ones32[:], rhs=B.gtri[:, h, :], start=True, stop=True), reads=[Rc, R["gtri"]], writes=[RB[by]])
                    if h < 3:
                        yield
                P.op("act", L("activation", f4(B.dti), banks[bx][:, :], AF.Exp), reads=[RB[bx]], writes=[R["dti"]])
                P.op("act", L("activation", f4(B.egb), banks[by][:, :], AF.Exp), reads=[RB[by]], writes=[R["egb"]])
                yield
                for h in range(4):
                    P.op("pe", L("matmul", banks[bx][:, HS[h]], lhsT=kfm[:, h, tsl], rhs=kfm[:, h, tsl], start=True, stop=True), reads=[RK[h][t]], writes=[RB[bx]])
                for h in range(4):
                    P.op("pe", L("matmul", banks[by][:, HS[h]], lhsT=kfm[:, h, tsl], rhs=qfm[:, h, tsl], start=True, stop=True), reads=[RK[h][t], RQ[h][t]], writes=[RB[by]])
                for h in range(4):
                    P.op("dve", L("scalar_tensor_tensor", out=B.A0[:, h, :], in0=banks[bx][:, HS[h]], scalar=S1(4, t, h), in1=B.dti[:, h, :],
                                  op0=ALU.mult, op1=ALU.mult), reads=[RB[bx], Rsc, R["dti"]], writes=[R["A0"]])
                P.op("dve", L("tensor_tensor", out=qkdt[o4][:].rearrange("p h t -> p (h t)"), in0=banks[by][:, :], in1=f4(B.dti), op=ALU.mult),
                     reads=[RB[by], R["dti"]], writes=[RQK[o4]])
                P.op("dve", L("tensor_tensor", out=B.A0[:], in0=B.A0[:], in1=bc4(su_bf[:]), op=ALU.mult), reads=[R["A0"], Rc], writes=[R["A0"]])
                P.op("dve", L("tensor_tensor", out=qfm[:, :, tsl], in0=qfm[:, :, tsl], in1=B.egb[:], op=ALU.mult),
                     reads=[RQ[h][t] for h in range(4)] + [R["egb"]], writes=[RQ[h][t] for h in range(4)])
                yield
                pvy = bkbf(by)
                for h in range(4):
                    P.op("pe", L("transpose", pvy[:, h * 128:(h + 1) * 128], B.A0[:, h, :], ident_bf[:]), reads=[R["A0"], Rc], writes=[RB[by]])
                P.op("act", L("copy", f4(B.A0T), pvy[:, 0:512]), reads=[RB[by]], writes=[R["A0T"]])
                yield
                P.op("dve", L("tensor_tensor", out=B.A1[:], in0=B.A0[:], in1=bc4(mk_bf[:, 0, :]), op=ALU.mult), reads=[R["A0"], Rc], writes=[R["A1"]])
                P.op("dve", L("tensor_tensor", out=B.AT1[:], in0=B.A0T[:], in1=bc4(mk_bf[:, 0, :]), op=ALU.mult), reads=[R["A0T"], Rc], writes=[R["AT1"]])
                P.op("dve", L("tensor_tensor", out=B.Zb[:], in0=bc4(ident_bf[:]), in1=B.A1[:], op=ALU.subtract), reads=[R["A1"], Rc], writes=[R["Zb"]])
                P.op("dve", L("tensor_tensor", out=B.ZTb[:], in0=bc4(ident_bf[:]), in1=B.AT1[:], op=ALU.subtract), reads=[R["AT1"], Rc], writes=[R["ZTb"]])
                yield
                for h in range(4):
                    P.op("pe", L("matmul", banks[bx][:, HS[h]], lhsT=B.AT1[:, h, :], rhs=B.A1[:, h, :], start=True, stop=True), reads=[R["A1"], R["AT1"]], writes=[RB[bx]])
                for h in range(4):
                    P.op("pe", L("matmul", banks[by][:, HS[h]], lhsT=B.A1[:, h, :], rhs=B.AT1[:, h, :], start=True, stop=True), reads=[R["A1"], R["AT1"]], writes=[RB[by]])
                P.op("dve", L("tensor_copy", f4(B.qsq), banks[bx][:, :]), reads=[RB[bx]], writes=[R["qsq"]])
                P.op("act", L("copy", f4(B.ksq), banks[by][:, :]), reads=[RB[by]], writes=[R["ksq"]])
                yield

                def zup(lz, rlz, rz, rrz, lzt, rlzt, rzt, rrzt, final=False):
                    for h in range(4):
                        P.op("pe", L("matmul", banks[bx][:, HS[h]], lhsT=ident_bf[:], rhs=B.Zb[:, h, :], start=True, stop=False), reads=[Rc, R["Zb"]], writes=[RB[bx]])
                        P.op("pe", L("matmul", banks[bx][:, HS[h]], lhsT=lz[:, h, :], rhs=rz[:, h, :], start=False, stop=True), reads=[rlz, rrz], writes=[RB[bx]])
                    if not final:
                        for h in range(4):
                            P.op("pe", L("matmul", banks[by][:, HS[h]], lhsT=ident_bf[:], rhs=B.ZTb[:, h, :], start=True, stop=False), reads=[Rc, R["ZTb"]], writes=[RB[by]])
                            P.op("pe", L("matmul", banks[by][:, HS[h]], lhsT=lzt[:, h, :], rhs=rzt[:, h, :], start=False, stop=True), reads=[rlzt, rrzt], writes=[RB[by]])
                        P.op("dve", L("tensor_copy", f4(B.Zb), banks[bx][:, :]), reads=[RB[bx]], writes=[R["Zb"]])
                        P.op("act", L("copy", f4(B.ZTb), banks[by][:, :]), reads=[RB[by]], writes=[R["ZTb"]])
                    else:
                        P.op("dve", L("tensor_copy", f4(Zfin[o4]), banks[bx][:, :]), reads=[RB[bx]], writes=[RZF[o4]])

                zup(B.ksq, R["ksq"], B.Zb, R["Zb"], B.Zb, R["Zb"], B.ksq, R["ksq"])
                yield
                for h in range(4):
                    P.op("pe", L("matmul", banks[by][:, HS[h]], lhsT=B.qsq[:, h, :], rhs=B.ksq[:, h, :], start=True, stop=True), reads=[R["qsq"], R["ksq"]], writes=[RB[by]])
                P.op("act", L("copy", f4(B.ksq), banks[by][:, :]), reads=[RB[by]], writes=[R["ksq"]])
                yield
                zup(B.ksq, R["ksq"], B.Zb, R["Zb"], B.Zb, R["Zb"], B.ksq, R["ksq"])

                def aoff(lvl):
                    P.op("pool", L("tensor_tensor", out=B.AT1[:], in0=B.A0T[:], in1=bc4(mk_bf[:, 1 + lvl, :]), op=ALU.mult), reads=[R["A0T"], Rc], writes=[R["AT1"]])
                aoff(0)
                yield
                for lvl in range(4):
                    for h in range(4):
                        P.op("pe", L("matmul", banks[bx][:, HS[h]], lhsT=B.AT1[:, h, :], rhs=B.Zb[:, h, :], start=True, stop=True), reads=[R["AT1"], R["Zb"]], writes=[RB[bx]])
                    P.op("act", L("activation", f4(B.A1), banks[bx][:, :], AF.Identity, scale=-1.0), reads=[RB[bx]], writes=[R["A1"]])
                    yield
                    zup(B.ZTb, R["ZTb"], B.A1, R["A1"], B.A1, R["A1"], B.ZTb, R["ZTb"], final=(lvl == 3))
                    if lvl < 3:
                        aoff(lvl + 1)
                    yield

            def rec(t):
                o4 = t % 4
                s_ = t // 4
                tsl = slice(t * 128, (t + 1) * 128)
                HS = [slice(h * 128, (h + 1) * 128) for h in range(4)]
                P.op("dve", L("tensor_tensor", out=Vr[:], in0=vtm[:, t, :].rearrange("p (h f) -> p h f", h=4),
                               in1=S4(3, t).unsqueeze(2).to_broadcast([128, 4, 128]), op=ALU.mult), reads=[RVT[t], Rsc], writes=[Rg["Vr"]])
                P.op("pool", L("tensor_tensor", out=wz[:], in0=zs[:, t, :].rearrange("p (h f) -> p h f", h=4),
                               in1=gnw_bc[:].unsqueeze(1).to_broadcast([128, 4, 128]), op=ALU.mult), reads=[RZ[t], Rbc], writes=[Rg["wz"]])
                for h in range(4):
                    P.op("pe", L("matmul", banks[2][:, HS[h]], lhsT=kfm[:, h, tsl], rhs=Sbf[:, h, :], start=True, stop=True),
                         reads=[RK[h][t], RSB[h]], writes=[RB[2]])
                yield
                for h in range(4):
                    P.op("dve", L("scalar_tensor_tensor", out=rhs2[:, h, :], in0=banks[2][:, HS[h]], scalar=S1(8, t, h), in1=Vr[:, h, :],
                                  op0=ALU.mult, op1=ALU.add), reads=[RB[2], Rsc, Rg["Vr"]], writes=[Rrhs2[h]])
                yield
                for h in range(4):
                    P.op("pe", L("matmul", banks[2][:, HS[h]], lhsT=Zfin[o4][:, h, :], rhs=rhs2[:, h, :], start=True, stop=True),
                         reads=[RZF[o4], Rrhs2[h]], writes=[RB[2]])
                yield
                yp3 = banks[2][:, :].rearrange("p (h t) -> p h t", h=4)
                P.op("dve", L("tensor_tensor", out=Ykbf[:], in0=yp3, in1=S4(10, t).unsqueeze(2).to_broadcast([128, 4, 128]), op=ALU.mult),
                     reads=[RB[2], Rsc], writes=[Rg["Yk"]])
                P.op("dve", L("tensor_tensor", out=Ybf[:], in0=yp3, in1=S4(4, t).unsqueeze(2).to_broadcast([128, 4, 128]), op=ALU.mult),
                     reads=[RB[2], Rsc], writes=[Rg["Y"]])
                yield
                for h in range(4):
                    P.op("pe", L("matmul", banks[2][:, HS[h]], lhsT=ktm[:, t, HS[h]], rhs=Ykbf[:, h, :], start=True, stop=True),
                         reads=[RKT[t], Rg["Yk"]], writes=[RB[2]])
                for h in range(4):
                    P.op("pe", L("matmul", banks[6][:, HS[h]], lhsT=qfm[:, h, tsl], rhs=Sbf[:, h, :], start=True, stop=False),
                         reads=[RQ[h][t], RSB[h]], writes=[RB[6]])
                    P.op("pe", L("matmul", banks[6][:, HS[h]], lhsT=qkdt[o4][:, h, :], rhs=Ybf[:, h, :], start=False, stop=True),
                         reads=[RQK[o4], Rg["Y"]], writes=[RB[6]])
                yield
                for h in range(4):
                    P.op("dve", L("scalar_tensor_tensor", out=S32[:, h, :], in0=S32[:, h, :], scalar=S1(11, t, h), in1=banks[2][:, HS[h]],
                                  op0=ALU.mult, op1=ALU.add), reads=[RB[2], Rsc, RS[h]], writes=[RS[h]])
                for h in range(4):
                    P.op("act", L("copy", Sbf[:, h, :], S32[:, h, :]), reads=[RS[h]], writes=[RSB[h]])
                for h in range(4):
                    P.op("act", L("activation", junk[:], banks[6][:, HS[h]], AF.Square, scale=128.0 ** -0.5, accum_out=S1(14, t, h)),
                         reads=[RB[6]], writes=[Rjunk, Rms[t % 2]])
                yield
                P.op("dve", L("tensor_tensor", out=S4(15, t), in0=S4(14, t), in1=S4(12, t), op=ALU.add), reads=[Rsc, Rms[t % 2]], writes=[Rms[t % 2]])
                yield
                P.op("act", L("activation", S4(15, t), S4(15, t), AF.Ln), reads=[Rms[t % 2]], writes=[Rms[t % 2]])
                P.op("act", L("activation", S4(15, t), S4(15, t), AF.Exp, scale=-0.5), reads=[Rms[t % 2]], writes=[Rms[t % 2]])
                yield
                for h in range(4):
                    P.op("dve", L("scalar_tensor_tensor", out=outb[:, h, :], in0=banks[6][:, HS[h]], scalar=S1(15, t, h), in1=wz[:, h, :],
                                  op0=ALU.mult, op1=ALU.mult), reads=[RB[6], Rms[t % 2], Rg["wz"]], writes=[Rg["outb"]])
                yield
                pv2 = bkbf(2)
                for h in range(4):
                    P.op("pe", L("transpose", pv2[:, h * 128:(h + 1) * 128], outb[:, h, :], ident_bf[:]), reads=[Rg["outb"], Rc], writes=[RB[2]])
                P.op("act", L("copy", hT[:, 4:8, tsl], pv2[:, 0:512].rearrange("p (h t) -> p h t", h=4)), reads=[RB[2]], writes=[RHb[t]])
                yield

            def late_gen(i):
                pc, kind_, idx, boff = LATE[i // 2]
                half = i % 2
                lsl, lres = wnext()
                lv = lsl[:, 0:4096].rearrange("p (c n) -> p c n", c=8)
                bk = 6
                if kind_ == "col":
                    for fb4 in range(4):
                        for cch in range(8):
                            P.op("pe", L("matmul", banks[bk][:, fb4:fb4 + 1], lhsT=lv[:, cch, fb4 * 128:(fb4 + 1) * 128],
                                         rhs=cact_bf[:, cch:cch + 1], start=(cch == 0), stop=(cch == 7)),
                                 reads=[lres, Rp], writes=[RB[bk]])
                    P.op("dve", L("tensor_tensor", out=mcol[:, idx * 8 + half * 4:idx * 8 + half * 4 + 4], in0=banks[bk][:, 0:4],
                                  in1=pcol[:, boff + half * 4:boff + half * 4 + 4], op=ALU.add), reads=[RB[bk], Rp, Rm2], writes=[Rm2])
                else:
                    for cch in range(8):
                        P.op("pe", L("matmul", banks[bk][:, :], lhsT=cactbc[:, cch, :], rhs=lv[:, cch, :],
                                     start=(cch == 0), stop=(cch == 7)), reads=[lres, Rp], writes=[RB[bk]])
                    P.op("dve", L("tensor_tensor", out=gt_bc[:, idx, half * 512:(half + 1) * 512], in0=banks[bk][:, :],
                                  in1=gt_bc[:, idx, half * 512:(half + 1) * 512], op=ALU.add), reads=[RB[bk], Rbc, Rgt], writes=[Rgt])
                wrel(1)
                if i == 3:
                    P.op("dve", L("scalar_tensor_tensor", out=mcol[:, 40:48], in0=mcol[:, 24:32], scalar=1.0, in1=pcol[:, 64:72],
                                  op0=ALU.add, op1=ALU.mult), reads=[Rp, Rm2], writes=[Rm2])
                yield

            rr_round = [0]

            def rr(gens, bg=None, period=2):
                gens = [g for g in gens if g is not None]
                while gens:
                    for g in list(gens):
                        try:
                            next(g)
                        except StopIteration:
                            gens.remove(g)
                    rr_round[0] += 1
                    if bg is not None:
                        try:
                            next(bg)
                        except StopIteration:
                            bg = None

            def seq(*gs):
                for g in gs:
                    yield from g

            def pair(g0, g1):
                gens = [g0, g1]
                while gens:
                    for g in list(gens):
                        try:
                            next(g)
                        except StopIteration:
                            gens.remove(g)
                    yield

            NU = NT // 2
            cg = conv31gen()
            rr([pair(prep(0), prep(1))], cg)
            for j in range(NU):
                nxt = pair(prep(2 * j + 2), prep(2 * j + 3)) if j + 1 < NU else None
                if blk == 0:
                    rr([seq(rec(2 * j), late_gen(2 * j), rec(2 * j + 1), late_gen(2 * j + 1)), nxt], cg)
                else:
                    rr([seq(rec(2 * j), rec(2 * j + 1)), nxt], cg)
            for _ in cg:
                pass
            for cb in range(4):
                P.op("act", L("activation", hT[:, cb, :], hT[:, cb, :], AF.Silu, bias=pcol[:, 80 + cb:81 + cb]), reads=RHa + [Rp], writes=RHa)
            snap_gdn = P.snapshot()
            chk(6)
            ns2 = norm_stream(40, 16)
            next(ns2)
            w0, w0r = wnext()
            w1, w1r = wnext()
            wo = [w0[:, 0:4096].rearrange("p (c n) -> p c n", c=4), w1[:, 0:4096].rearrange("p (c n) -> p c n", c=4)]
            wor = [w0r, w1r]
            for t in range(NT):
                if t >= 2:
                    ns2.send(t - 2)
                for nh in range(2):
                    bk = nbank()
                    for kc in range(8):
                        P.op("pe", L("matmul", banks[bk][:, :], lhsT=hT[:, kc, t * 128:(t + 1) * 128], rhs=wo[kc // 4][:, kc % 4, nh * 512:(nh + 1) * 512],
                                     start=(kc == 0), stop=(kc == 7)), reads=[(RHa if kc < 4 else RHb)[t], wor[kc // 4]], writes=[RB[bk]])
                    par = nh
                    P.op("dve", L("tensor_tensor", out=tmpf[par][:], in0=banks[bk][:, :], in1=gt_bc[:, 0, nh * 512:(nh + 1) * 512], op=ALU.mult),
                         reads=[RB[bk], Rbc, Rgt], writes=[Rtf[par]])
                    P.op("dve", L("tensor_tensor", out=X[:, t, nh * 512:(nh + 1) * 512], in0=X[:, t, nh * 512:(nh + 1) * 512], in1=tmpf[par][:], op=ALU.add),
                         reads=[Rtf[par], RX[t]], writes=[RX[t]])
            ns2.send(NT - 2)
            ns2.send(NT - 1)
            ns2.send(None)
            ns2.send(None)
            wrel(2)
            P.apply(snap_gdn)
            chk(7)
            wfo_v = wfo_d.rearrange("(c p) n -> p c n", p=128)
            def wfo_load(i):
                P.dma("pool", [L("dma_start", out=wfo[:, 4 * i:min(22, 4 * i + 4), :], in_=wfo_v[:, 4 * i:min(22, 4 * i + 4), :])],
                      RWFO[i], writes=[RWFO[i]])
            wfo_at = {0: 0, 2: 1, 4: 2, 6: 3, 8: 4, 9: 5}
            for u_ in range(11):
                wsl_, wr_ = wnext()
                wgv = wsl_[:, 0:2048].rearrange("p (c n) -> p c n", c=8)
                wuv = wsl_[:, 2048:4096].rearrange("p (c n) -> p c n", c=8)
                for fb in range(2):
                    j = u_ * 2 + fb
                    for s in range(NS):
                        bkg = nbank()
                        for cch in range(8):
                            P.op("pe", L("matmul", banks[bkg][:, :], lhsT=wgv[:, cch, fb * 128:(fb + 1) * 128], rhs=hT[:, cch, s * 512:(s + 1) * 512],
                                         start=(cch == 0), stop=(cch == 7)), reads=[wr_] + (RHa if cch < 4 else RHb)[4 * s:4 * s + 4], writes=[RB[bkg]])
                        bku = nbank()
                        for cch in range(8):
                            P.op("pe", L("matmul", banks[bku][:, :], lhsT=wuv[:, cch, fb * 128:(fb + 1) * 128], rhs=hT[:, cch, s * 512:(s + 1) * 512],
                                         start=(cch == 0), stop=(cch == 7)), reads=[wr_] + (RHa if cch < 4 else RHb)[4 * s:4 * s + 4], writes=[RB[bku]])
                        par = s % 2
                        P.op("act", L("activation", tmpb[par][:], banks[bkg][:, :], AF.Silu), reads=[RB[bkg]], writes=[Rtb[par]])
                        P.op("dve", L("tensor_tensor", out=actT[:, j, s * 512:(s + 1) * 512], in0=banks[bku][:, :], in1=tmpb[par][:], op=ALU.mult),
                             reads=[RB[bku], Rtb[par]], writes=[RACT[j]])
                wrel(1)
                if u_ in wfo_at:
                    wfo_load(wfo_at[u_])
            chk(8)
            ns1 = norm_stream(32, 0)
            next(ns1)
            for t in range(NT):
                if blk + 1 < NB and t >= 2:
                    ns1.send(t - 2)
                for nh in range(2):
                    bk = nbank()
                    for kc in range(22):
                        P.op("pe", L("matmul", banks[bk][:, :], lhsT=actT[:, kc, t * 128:(t + 1) * 128], rhs=wfo[:, kc, nh * 512:(nh + 1) * 512],
                                     start=(kc == 0), stop=(kc == 21)), reads=[RACT[kc], RWFO[kc // 4]], writes=[RB[bk]])
                    if t == NT - 1 and nh == 1:
                        snap_ffn = P.snapshot()
                    par = nh
                    P.op("dve", L("tensor_tensor", out=tmpf[par][:], in0=banks[bk][:, :], in1=gt_bc[:, 1, nh * 512:(nh + 1) * 512], op=ALU.mult),
                         reads=[RB[bk], Rbc, Rgt], writes=[Rtf[par]])
                    P.op("dve", L("tensor_tensor", out=X[:, t, nh * 512:(nh + 1) * 512], in0=X[:, t, nh * 512:(nh + 1) * 512], in1=tmpf[par][:], op=ALU.add),
                         reads=[Rtf[par], RX[t]], writes=[RX[t]])
                par = t % 2
                P.op("act", L("activation", tmpf[par][:].bitcast(BF16), X[:, t, :], AF.Square, accum_out=ncol[par][:, 0:1]), reads=[RX[t]], writes=[Rtf[par], Rnc[par]])
                P.op("act", L("activation", ncol[par][:, 1:2], ncol[par][:, 0:1], AF.Ln, bias=EPS, scale=1.0 / D), reads=[Rnc[par]], writes=[Rnc[par]])
                P.op("act", L("activation", ncol[par][:, 2:3], ncol[par][:, 1:2], AF.Exp, scale=-0.5), reads=[Rnc[par]], writes=[Rnc[par]])
                P.op("dve", L("scalar_tensor_tensor", out=X[:, t, :], in0=X[:, t, :], scalar=ncol[par][:, 2:3], in1=nfw_bc[:],
                              op0=ALU.mult, op1=ALU.mult), reads=[RX[t], Rnc[par], Rbc], writes=[RX[t]])
                P.dma("sp", [L("dma_start", out=y_d[tok0 + t * 128: tok0 + (t + 1) * 128, :], in_=X[:, t, :])], RX[t], reads=[RX[t]])
                if blk + 1 < NB:
                    ntok = tok0 + TB
                    P.dma("sp", [L("dma_start", out=X[:, t, :], in_=x_d[ntok + t * 128: ntok + (t + 1) * 128, :])], RX[t], writes=[RX[t]])
            if blk + 1 < NB:
                carry["ns"] = ns1
                carry["pend"] = [NT - 2, NT - 1, None, None]
            P.apply(snap_ffn)
        try:
            body()
        except _Stop:
            pass
        finals = [("d", RX[t], RX[t].cnt) for t in range(NT) if RX[t].cnt]
        if debug:
            P.barrier()
            dumps = dict(mk_bf=mk_bf, pcol=pcol, mcol=mcol, gt_bc=gt_bc, cw31c=cw31c, cw4c=cw4c, bcen=bcen, tri32=tri32, nm32=nm32, imp32=imp32,
                         su_bf=su_bf, gm_bf=gm_bf, cact_bf=cact_bf, abc=abc, hT=hT, qfm=qfm, kfm=kfm, ktm=ktm, vtm=vtm, zs=zs, lg=lg,
                         u_v=u_v, S32=S32, X=X, scb=scb, outb=outb, Ybf=Ybf)
            for name in debug:
                t_ = dumps[name]
                ap_ = t_ if isinstance(t_, bass.AP) else t_[:]
                dd = nc.dram_tensor("dbg_" + name, list(ap_.shape), ap_.dtype, kind="ExternalOutput").ap()
                rr = Res("dbg_" + name)
                finals.append(P.dma("sp", [L("dma_start", out=dd, in_=ap_)], rr))
        P.emit(finals)
    return nc


_NC_CACHE = {}


def kernel(x, c, w_ada, b_ada, norm_mix_w, w_in, conv_w, conv_b, conv_gn_w, conv_gn_b,
           gdn_conv_w, gdn_a_log, gdn_dt_bias, gdn_norm_w, w_out, norm_ffn_w,
           w_ffn_in, w_ffn_out, norm_final_w):
    f = lambda a: np.ascontiguousarray(np.asarray(a, dtype=np.float32))
    x = f(x); c = f(c)
    if "nc" not in _NC_CACHE:
        _NC_CACHE["nc"] = build()
    nc = _NC_CACHE["nc"]
    cw31 = f(conv_w)[0].reshape(31, 4, 128).reshape(124, 128)
    cw4 = f(gdn_conv_w)[0].reshape(4, 12, 128).reshape(48, 128)
    shared = {
        "cw31": cw31, "cw4": cw4,
        "brow": f(b_ada).reshape(1, 6144),
        "nfw": f(norm_final_w).reshape(1, D),
        "gnw": f(gdn_norm_w).reshape(1, 128),
        "alog": f(gdn_a_log).reshape(1, 4),
        "dtb": f(gdn_dt_bias).reshape(1, 4),
        "w_ada": f(w_ada)[0], "w_in": f(w_in)[0], "w_out": f(w_out)[0],
        "w_ffn_in": f(w_ffn_in)[0], "w_ffn_out": f(w_ffn_out)[0],
    }
    in_maps = []
    for b in range(8):
        prow = np.concatenate([f(b_ada).reshape(48, 128), c[b].reshape(8, 128), f(norm_mix_w).reshape(8, 128),
                               f(norm_ffn_w).reshape(8, 128), f(conv_b).reshape(4, 128), f(conv_gn_w).reshape(4, 128),
                               f(conv_gn_b).reshape(4, 128)], axis=0)
        m = dict(shared)
        m["x"] = x[b]
        m["prow"] = np.ascontiguousarray(prow)
        in_maps.append(m)
    res = run_bass_kernel_spmd(nc, in_maps, core_ids=list(range(8)))
    return np.stack([r["y"] for r in res.results], axis=0).astype(np.float32)
```

```python
import contextlib
import numpy as np
import concourse.bass as bass
import concourse.mybir as mybir
from concourse.bass_utils import run_bass_kernel_spmd

F32 = mybir.dt.float32
BF16 = mybir.dt.bfloat16
AF = mybir.ActivationFunctionType
ALU = mybir.AluOpType

D = 1024
S = 2048
TB = 1024
NB = S // TB
NT = TB // 128
NS = TB // 512
N_IN = 3080
D_FF = 2816
EPS = 1e-6
ENGS = ("pe", "act", "dve", "pool", "sp")


class Res:
    __slots__ = ("name", "w", "r", "sem", "cnt", "excl")

    def __init__(self, name, excl=False):
        self.name = name
        self.excl = excl
        self.w = None
        self.r = []
        self.sem = None
        self.cnt = 0


class Prog:
    def __init__(self, nc):
        self.nc = nc
        self.ops = {e: [] for e in ENGS}
        self.dma_res = []
        self.bar = {e: [] for e in ENGS}

    def _deps(self, eng, reads, writes):
        deps = []
        for r in reads:
            if r.w is not None:
                deps.append(r.w)
            if r.excl:
                deps.extend(x for x in r.r if x[1] != eng)
        for r in writes:
            if r.w is not None:
                deps.append(r.w)
            deps.extend(r.r)
        if self.bar[eng]:
            deps.extend(self.bar[eng])
            self.bar[eng] = []
        return [d for d in deps if not (d[0] == "e" and d[1] == "pe" and eng == "pe")]

    def op(self, eng, fn, reads=(), writes=()):
        deps = self._deps(eng, reads, writes)
        idx = len(self.ops[eng])
        self.ops[eng].append(dict(fn=fn, deps=deps, sig=False, dma=None))
        ref = ("e", eng, idx)
        for r in reads:
            r.r.append(ref)
        for r in writes:
            r.w = ref
            r.r = []
        return ref

    def dma(self, eng, fns, key, reads=(), writes=()):
        deps = self._deps(eng, reads, writes)
        if key.sem is None:
            self.dma_res.append(key)
            key.sem = True
        for i, fn in enumerate(fns):
            key.cnt += 16
            self.ops[eng].append(dict(fn=fn, deps=deps if i == 0 else [], sig=False, dma=key))
        ref = ("d", key, key.cnt)
        for r in reads:
            r.r.append(ref)
        for r in writes:
            r.w = ref
            r.r = []
        return ref

    def snapshot(self):
        last = []
        for e in ENGS:
            for i in range(len(self.ops[e]) - 1, -1, -1):
                if self.ops[e][i]["dma"] is None:
                    last.append(("e", e, i))
                    break
        return last

    def apply(self, snap):
        for e in ENGS:
            self.bar[e] = self.bar[e] + list(snap)

    def barrier(self):
        last = []
        for e in ENGS:
            for i in range(len(self.ops[e]) - 1, -1, -1):
                if self.ops[e][i]["dma"] is None:
                    last.append(("e", e, i))
                    break
        for k in self.dma_res:
            last.append(("d", k, k.cnt))
        for e in ENGS:
            self.bar[e] = list(last)

    def emit(self, final_refs):
        nc = self.nc
        allops = self.ops
        for e in ENGS:
            for o in allops[e]:
                for d in o["deps"]:
                    if d[0] == "e":
                        allops[d[1]][d[2]]["sig"] = True
        counts = {}
        for e in ENGS:
            c = 0
            lst = []
            for o in allops[e]:
                if o["sig"]:
                    c += 1
                lst.append(c)
            counts[e] = lst
        with contextlib.ExitStack() as st:
            esem = {e: st.enter_context(nc.semaphore("s_" + e)) for e in ENGS}
            for k in self.dma_res:
                k.sem = st.enter_context(nc.semaphore("d_" + k.name))
            block = st.enter_context(nc.Block())

            def run(e, engine):
                waited = {}

                def do_waits(deps):
                    need = {}
                    for d in deps:
                        if d[0] == "e":
                            s_, v = esem[d[1]], counts[d[1]][d[2]]
                        else:
                            s_, v = d[1].sem, d[2]
                        if v > need.get(s_, 0):
                            need[s_] = v
                    for s_, v in need.items():
                        if waited.get(s_, 0) >= v:
                            continue
                        waited[s_] = v
                        engine.wait_ge(s_, v)

                for o in allops[e]:
                    do_waits(o["deps"])
                    ins = o["fn"](engine)
                    if o["dma"] is not None:
                        ins.then_inc(o["dma"].sem, 16)
                    elif o["sig"]:
                        ins.then_inc(esem[e], 1)
                if e == "sp":
                    do_waits(final_refs)

            @block.tensor
            def _(eng):
                run("pe", eng)

            @block.scalar
            def _(eng):
                run("act", eng)

            @block.vector
            def _(eng):
                run("dve", eng)

            @block.gpsimd
            def _(eng):
                run("pool", eng)

            @block.sync
            def _(eng):
                run("sp", eng)


def L(name, *a, **k):
    return lambda e: getattr(e, name)(*a, **k)


class _Stop(Exception):
    pass


def build(debug=None, stage=99):
    nc = bass.Bass("TRN2", target_bir_lowering=False)
    dt_in = lambda name, shape: nc.dram_tensor(name, shape, F32, kind="ExternalInput").ap()
    x_d = dt_in("x", [S, D])
    prow_d = dt_in("prow", [84, 128])
    cw31_d = dt_in("cw31", [124, 128])
    cw4_d = dt_in("cw4", [48, 128])
    brow_d = dt_in("brow", [1, 6144])
    nfw_d = dt_in("nfw", [1, D])
    gnw_d = dt_in("gnw", [1, 128])
    alog_d = dt_in("alog", [1, 4])
    dtb_d = dt_in("dtb", [1, 4])
    wada_d = dt_in("w_ada", [D, 6 * D])
    win_d = dt_in("w_in", [D, N_IN])
    wout_d = dt_in("w_out", [D, D])
    wfi_d = dt_in("w_ffn_in", [D, 2 * D_FF])
    wfo_d = dt_in("w_ffn_out", [D_FF, D])
    y_d = nc.dram_tensor("y", [S, D], F32, kind="ExternalOutput").ap()
    P = Prog(nc)
    with contextlib.ExitStack() as st:
        def sb(name, shape, dt=F32):
            return st.enter_context(nc.sbuf_tensor(name, shape, dt))

        X = sb("X", [128, NT, D], F32)
        hT = sb("hT", [128, 8, TB], BF16)
        NW = 3
        wring = [sb(f"wr{i}", [128, 4096], BF16) for i in range(NW)]
        A1N = 22 * 1024 + 256
        arena1 = sb("arena1", [128, A1N], BF16)
        R2N = 12 * 4 * 128 + 31 * 128 + NT * 512 + NT * 16 + 1024 + 10 * 512 + 14 * 512
        region2 = sb("region2", [128, R2N], BF16)
        ident_bf = sb("ident_bf", [128, 128], BF16)
        nident_bf = sb("nident_bf", [128, 128], BF16)
        ones_bf = sb("ones_bf", [128, 128], BF16)
        su_bf = sb("su_bf", [128, 128], BF16)
        gm_bf = sb("gm_bf", [128, 128], BF16)
        mk_bf = sb("mk_bf", [128, 5, 128], BF16)
        ident32 = sb("ident32", [128, 128], F32)
        ones32 = sb("ones32", [128, 128], F32)
        nones32 = sb("nones32", [128, 128], F32)
        tri32 = sb("tri32", [128, 128], F32)
        nm32 = sb("nm32", [128, 128], F32)
        imp32 = sb("imp32", [128, 128], F32)
        pcol = sb("pcol", [128, 84], F32)
        cw31c = sb("cw31c", [128, 124], F32)
        cw4c = sb("cw4c", [128, 48], F32)
        mcol = sb("mcol", [128, 48], F32)
        bcen = sb("bcen", [128, 8], F32)
        cact_bf = sb("cact_bf", [128, 8], BF16)
        cactbc = sb("cactbc", [128, 8, 128], BF16)
        gt_bc = sb("gt_bc", [128, 2, D], F32)
        nfw_bc = sb("nfw_bc", [128, D], F32)
        gnw_bc = sb("gnw_bc", [128, 128], F32)
        abc = sb("abc", [128, 4], F32)
        dtb_bc = sb("dtb_bc", [128, 4], F32)
        S32 = sb("S32", [128, 4, 128], F32)
        Sbf = sb("Sbf", [128, 4, 128], BF16)
        rawtail = sb("rawtail", [128, 12, 4], BF16)
        utail = sb("utail", [128, 4, 32], BF16)
        wlg = sb("wlg", [128, 8, 8], BF16)
        tmpf = [sb(f"tmpf{i}", [128, 512], F32) for i in range(2)]
        tmpb = [sb(f"tmpb{i}", [128, 512], BF16) for i in range(2)]
        xnb = [sb(f"xnb{i}", [128, D], BF16) for i in range(2)]
        ncol = [sb(f"ncol{i}", [128, 4], F32) for i in range(2)]
        scb = sb("scb", [128, 16, NT * 4], F32)
        junk = sb("junk", [128, 128], BF16)
        Rjunk = Res("junk")
        banks = [st.enter_context(nc.psum_tensor(f"bk{i}", [128, 512], F32)) for i in range(8)]
        RB = [Res(f"bk{i}", excl=True) for i in range(8)]

        def bkbf(i):
            return banks[i][:].bitcast(BF16)

        RX = [Res(f"X{t}") for t in range(NT)]
        RHa = [Res(f"hTa{t}") for t in range(NT)]
        RHb = [Res(f"hTb{t}") for t in range(NT)]
        RW = [Res(f"wr{i}") for i in range(NW)]
        Rc = Res("consts")
        Rp = Res("params")
        RS = [Res(f"S{h}") for h in range(4)]
        RSB = [Res(f"Sb{h}") for h in range(4)]
        Rtf = [Res(f"tmpf{i}") for i in range(2)]
        Rtb = [Res(f"tmpb{i}") for i in range(2)]
        Rxn = [Res(f"xnb{i}") for i in range(2)]
        Rnc = [Res(f"ncol{i}") for i in range(2)]
        Rsc = Res("sc")
        Rms = [Res("ms0"), Res("ms1")]
        Rtail = Res("tails")
        Rwlg = Res("wlg")

        o = 0
        def carve(n):
            nonlocal o
            v = arena1[:, o:o + n]
            o += n
            return v
        UW = 32 + TB
        u_v = carve(4 * UW).rearrange("p (c t) -> p c t", c=4)
        RAWW = 4 + TB
        raw_v = [carve(RAWW) for _ in range(2)]
        tmpg = [raw_v[i][:, 0:1024].bitcast(F32) for i in range(2)]
        qfm = carve(4 * TB).rearrange("p (c t) -> p c t", c=4)
        kfm = carve(4 * TB).rearrange("p (c t) -> p c t", c=4)
        ktm = carve(NT * 512).rearrange("p (t f) -> p t f", t=NT)
        vtm = carve(NT * 512).rearrange("p (t f) -> p t f", t=NT)
        assert o <= A1N, o
        actT = arena1[:, 0:22 * TB].rearrange("p (c t) -> p c t", c=22)
        RU = [Res(f"u{c}") for c in range(4)]
        RRAW = [Res(f"raw{i}") for i in range(2)]
        Rtg = RRAW
        RQ = [[Res(f"q{h}_{t}") for t in range(NT)] for h in range(4)]
        RK = [[Res(f"k{h}_{t}") for t in range(NT)] for h in range(4)]
        RKT = [Res(f"kt{t}") for t in range(NT)]
        RVT = [Res(f"vt{t}") for t in range(NT)]
        RACT = [Res(f"act{j}") for j in range(22)]

        o2 = 0
        def carve2(n):
            nonlocal o2
            v = region2[:, o2:o2 + n]
            o2 += n
            return v
        W4 = carve2(12 * 4 * 128).rearrange("p (f k c) -> p f k c", f=12, k=4)
        Wc = carve2(31 * 128).rearrange("p (k c) -> p k c", k=31)
        zs = carve2(NT * 512).rearrange("p (t f) -> p t f", t=NT)
        lg = carve2(NT * 8 * 2).bitcast(F32).rearrange("p (t f) -> p t f", t=NT)
        def c2h():
            return carve2(512).rearrange("p (h t) -> p h t", h=4)

        class Chain:
            pass
        CH = [Chain(), Chain()]
        oa = [0]
        def carveA(n):
            v = arena1[:, oa[0]:oa[0] + n]
            oa[0] += n
            return v
        ow = [0]
        def carveW(n):
            v = region2[:, ow[0]:ow[0] + n]
            ow[0] += n
            return v
        for ci, cv in enumerate([carve2, carveW]):
            B = CH[ci]
            B.gtri = cv(1024).bitcast(F32).rearrange("p (h t) -> p h t", h=4)
            for nm in ["dti", "egb", "A0", "A0T", "A1", "AT1", "qsq", "ksq", "Zb", "ZTb"]:
                setattr(B, nm, cv(512).rearrange("p (h t) -> p h t", h=4))
            B.R = {nm: Res(f"c{ci}{nm}") for nm in ["gtri", "dti", "egb", "A0", "A0T", "A1", "AT1", "qsq", "ksq", "Zb", "ZTb"]}
            B.bx, B.by = (0, 3) if ci == 0 else (4, 5)
        assert ow[0] <= 12 * 4 * 128
        Zfin = [c2h() for _ in range(4)]
        qkdt = [c2h() for _ in range(4)]
        RZF = [Res(f"zfin{i}") for i in range(4)]
        RQK = [Res(f"qkdt{i}") for i in range(4)]
        Vr = c2h()
        wz = c2h()
        rhs2 = c2h()
        Ybf = c2h()
        Ykbf = c2h()
        outb = c2h()
        assert o2 <= R2N, o2
        wfo = region2[:, 0:22 * 1024].rearrange("p (c n) -> p c n", c=22)
        RW4 = [[Res(f"W4_{f}_{k}") for k in range(4)] for f in range(12)]
        RWc = Res("Wc")
        RWck = [Res(f"Wc{k}") for k in range(31)]
        RZ = [Res(f"zs{t}") for t in range(NT)]
        RLG = [Res(f"lg{t}") for t in range(NT)]
        Rg = {k: Res(k) for k in ["Vr", "wz", "Y", "Yk", "outb"]}
        Rrhs2 = [Res(f"rhs2{h}") for h in range(4)]
        RWFO = [Res(f"wfo{i}") for i in range(6)]

        specs = []
        def win_group(g):
            src = win_d.rearrange("(c p) n -> p c n", p=128)[:, :, g * 512:(g + 1) * 512]
            return lambda slot: [(slot[:, 0:4096].rearrange("p (c n) -> p c n", c=8), src)]
        def wout_group(i):
            src = wout_d.rearrange("(c p) n -> p c n", p=128)[:, 4 * i:4 * i + 4, :]
            return lambda slot: [(slot[:, 0:4096].rearrange("p (c n) -> p c n", c=4), src)]
        def wfi_unit(u_):
            v = wfi_d.rearrange("(c p) n -> p c n", p=128)
            return lambda slot: [(slot[:, 0:2048].rearrange("p (c n) -> p c n", c=8), v[:, :, u_ * 256:(u_ + 1) * 256]),
                                 (slot[:, 2048:4096].rearrange("p (c n) -> p c n", c=8), v[:, :, D_FF + u_ * 256:D_FF + (u_ + 1) * 256])]
        in_order = [5, 0, 1, 2, 3, 4]
        def wada_group(pc, half):
            src = wada_d.rearrange("(c p) n -> p c n", p=128)[:, :, pc * D + half * 512:pc * D + (half + 1) * 512]
            return lambda slot: [(slot[:, 0:4096].rearrange("p (c n) -> p c n", c=8), src)]
        LATE = [(3, "col", 2, 24), (4, "col", 3, 32), (2, "gt", 0, 0), (5, "gt", 1, 0)]
        for blk in range(NB):
            for gi_, g in enumerate(in_order):
                specs.append(win_group(g))
                if blk == 0 and gi_ == 5:
                    for pc, _k, _i, _b in LATE:
                        for half in range(2):
                            specs.append(wada_group(pc, half))
            specs.append(wout_group(0))
            specs.append(wout_group(1))
            for u_ in range(11):
                specs.append(wfi_unit(u_))
        wstate = dict(issued=0, released=0, next=0)

        def wpump():
            while wstate["issued"] < len(specs) and wstate["issued"] - NW < wstate["released"]:
                j = wstate["issued"]
                slot = wring[j % NW]
                pairs = specs[j](slot)
                P.dma("pool", [L("dma_start", out=d_, in_=s_) for d_, s_ in pairs], RW[j % NW], writes=[RW[j % NW]])
                wstate["issued"] += 1

        def wnext():
            i = wstate["next"]
            wstate["next"] += 1
            wpump()
            assert wstate["issued"] > i
            return wring[i % NW], RW[i % NW]

        def wrel(n=1):
            wstate["released"] += n
            wpump()

        prow = arena1[:, 0:256].bitcast(F32)
        cw31r = arena1[:, 256:512].bitcast(F32)
        cw4r = arena1[:, 512:768].bitcast(F32)
        Rst = Res("stage")
        Rst2 = Res("stage2")
        Rst3 = Res("stage3")
        P.dma("sp", [L("dma_start", out=prow[0:84, :], in_=prow_d)], Rst, writes=[Rst])
        P.dma("sp", [L("dma_start", out=cw31r[0:124, :], in_=cw31_d)], Rst2, writes=[Rst2])
        P.dma("sp", [L("dma_start", out=cw4r[0:48, :], in_=cw4_d)], Rst3, writes=[Rst3])
        Rbc = Res("bcasts")
        Rgt = Res("gt")
        Rm2 = Res("mod2")
        P.dma("sp", [L("dma_start", out=nfw_bc[:], in_=nfw_d.partition_broadcast(128)),
                     L("dma_start", out=gnw_bc[:], in_=gnw_d.partition_broadcast(128)),
                     L("dma_start", out=abc[:], in_=alog_d.partition_broadcast(128)),
                     L("dma_start", out=dtb_bc[:], in_=dtb_d.partition_broadcast(128)),
                     L("dma_start", out=gt_bc[:, 0, :], in_=brow_d[:, 2 * D:3 * D].partition_broadcast(128)),
                     L("dma_start", out=gt_bc[:, 1, :], in_=brow_d[:, 5 * D:6 * D].partition_broadcast(128))],
              Rbc, writes=[Rbc])
        wad = [region2[:, 0:8192].rearrange("p (c n) -> p c n", c=8), region2[:, 8192:16384].rearrange("p (c n) -> p c n", c=8)]
        Rwad = [Res("wad0"), Res("wad1")]
        wada_v = wada_d.rearrange("(c p) n -> p c n", p=128)
        for i in range(2):
            P.dma("pool", [L("dma_start", out=wad[i], in_=wada_v[:, :, i * D:(i + 1) * D])], Rwad[i], writes=[Rwad[i]])
        for t in range(NT):
            P.dma("sp", [L("dma_start", out=X[:, t, :], in_=x_d[t * 128:(t + 1) * 128, :])], RX[t], reads=(Rwad if t >= 2 else []), writes=[RX[t]])
        def mk_mask(t, cmp, fill, base_val=1.0, step=-1, cm=1, eng="pool"):
            P.op(eng, L("memset", t[:], base_val), writes=[Rc])
            P.op(eng, L("affine_select", out=t[:], in_=t[:], pattern=[[step, 128]], compare_op=cmp, fill=fill, base=0,
                        channel_multiplier=cm), reads=[Rc], writes=[Rc])
        mk_mask(ident32, ALU.is_equal, 0.0)
        mk_mask(tri32, ALU.is_ge, 0.0, step=1, cm=-1)
        mk_mask(nm32, ALU.is_ge, -1e30, base_val=0.0, step=1, cm=-1)
        P.op("pool", L("memset", ones32[:], 1.0), writes=[Rc])
        P.op("pool", L("memset", nones32[:], -1.0), writes=[Rc])
        P.op("pool", L("memset", ones_bf[:], 1.0), writes=[Rc])
        P.op("pool", L("tensor_copy", ident_bf[:], ident32[:]), reads=[Rc], writes=[Rc])
        P.op("pool", L("tensor_scalar", nident_bf[:], ident32[:], -1.0, None, op0=ALU.mult), reads=[Rc], writes=[Rc])
        P.op("pool", L("tensor_tensor", out=su_bf[:], in0=tri32[:], in1=ident32[:], op=ALU.subtract), reads=[Rc], writes=[Rc])
        P.op("pool", L("memset", imp32[:], 0.0), writes=[Rc])
        P.op("pool", L("memset", imp32[0:64, 0:64], 1.0 / 64), writes=[Rc])
        P.op("pool", L("memset", imp32[64:128, 64:128], 1.0 / 64), writes=[Rc])
        P.op("pool", L("tensor_copy", gm_bf[:], imp32[:]), reads=[Rc], writes=[Rc])
        P.op("pool", L("tensor_tensor", out=imp32[:], in0=ident32[:], in1=imp32[:], op=ALU.subtract), reads=[Rc], writes=[Rc])
        P.op("pool", L("memset", S32[:], 0.0), writes=RS)
        P.op("pool", L("memset", Sbf[:], 0.0), writes=RSB)
        P.op("pool", L("memset", rawtail[:], 0.0), writes=[Rtail])
        P.op("pool", L("memset", utail[:], 0.0), writes=[Rtail])

        bdf = [arena1[:, 2048 + i * 256:2048 + (i + 1) * 256].bitcast(F32) for i in range(4)]
        for i, s_ in enumerate([8, 16, 32, 64]):
            v = bdf[i].rearrange("p (a b) -> p a b", b=s_)
            P.op("pool", L("memset", bdf[i], 1.0), writes=[Rc])
            P.op("pool", L("affine_select", out=v, in_=v, pattern=[[-s_, 128 // s_], [0, s_]], compare_op=ALU.is_ge, fill=0.0, base=0,
                           channel_multiplier=1), reads=[Rc], writes=[Rc])
            P.op("pool", L("affine_select", out=v, in_=v, pattern=[[s_, 128 // s_], [0, s_]], compare_op=ALU.is_ge, fill=0.0, base=s_ - 1,
                           channel_multiplier=-1), reads=[Rc], writes=[Rc])
        P.op("pool", L("tensor_copy", mk_bf[:, 0, :], bdf[0]), reads=[Rc], writes=[Rc])
        for i in range(3):
            P.op("pool", L("tensor_tensor", out=mk_bf[:, 1 + i, :], in0=bdf[i + 1], in1=bdf[i], op=ALU.subtract), reads=[Rc], writes=[Rc])
        P.op("pool", L("tensor_tensor", out=mk_bf[:, 4, :], in0=ones32[:], in1=bdf[3], op=ALU.subtract), reads=[Rc], writes=[Rc])
        P.dma("pool", [L("dma_start", out=wlg[:], in_=win_d.rearrange("(c p) n -> p c n", p=128)[:, :, 3072:3080])],
              Rwlg, writes=[Rwlg])
        wpump()
        P.op("pe", L("transpose", banks[0][:, 0:84], prow[0:84, :], ident32[0:84, 0:84]), reads=[Rst, Rc], writes=[RB[0]])
        P.op("pe", L("transpose", banks[0][:, 128:252], cw31r[0:124, :], ident32[0:124, 0:124]), reads=[Rst2, Rc], writes=[RB[0]])
        P.op("pe", L("transpose", banks[0][:, 256:304], cw4r[0:48, :], ident32[0:48, 0:48]), reads=[Rst3, Rc], writes=[RB[0]])
        P.op("dve", L("tensor_copy", pcol[:], banks[0][:, 0:84]), reads=[RB[0]], writes=[Rp])
        P.op("dve", L("tensor_scalar", cw31c[:], banks[0][:, 128:252], 0.5, None, op0=ALU.mult), reads=[RB[0]], writes=[Rp])
        P.op("dve", L("tensor_copy", cw4c[:], banks[0][:, 256:304]), reads=[RB[0]], writes=[Rp])
        P.op("act", L("activation", cact_bf[:], pcol[:, 48:56], AF.Silu), reads=[Rp], writes=[Rp])
        P.op("dve", L("tensor_copy", cactbc[:], cact_bf[:].unsqueeze(2).to_broadcast([128, 8, 128])), reads=[Rp], writes=[Rp])
        P.op("act", L("activation", abc[:], abc[:], AF.Exp), reads=[Rbc], writes=[Rbc])
        P.op("dve", L("tensor_scalar", abc[:], abc[:], -1.0, None, op0=ALU.mult), reads=[Rbc], writes=[Rbc])
        P.op("pe", L("matmul", banks[1][:, 0:4], lhsT=imp32[:], rhs=pcol[:, 72:76], start=True, stop=True), reads=[Rc, Rp], writes=[RB[1]])
        P.op("dve", L("tensor_copy", bcen[:, 0:4], banks[1][:, 0:4]), reads=[RB[1]], writes=[Rp])
        P.op("dve", L("tensor_scalar", bcen[:, 4:8], pcol[:, 80:84], -1.0, None, op0=ALU.mult), reads=[Rp], writes=[Rp])

        for i in range(2):
            for fb in range(8):
                for cch in range(8):
                    P.op("pe", L("matmul", banks[2][:, i * 8 + fb:i * 8 + fb + 1], lhsT=wad[i][:, cch, fb * 128:(fb + 1) * 128],
                                 rhs=cact_bf[:, cch:cch + 1], start=(cch == 0), stop=(cch == 7)),
                         reads=[Rwad[i], Rp], writes=[RB[2]])
        for i, boff in enumerate([0, 8]):
            P.op("dve", L("tensor_tensor", out=mcol[:, i * 8:(i + 1) * 8], in0=banks[2][:, i * 8:(i + 1) * 8],
                          in1=pcol[:, boff:boff + 8], op=ALU.add), reads=[RB[2], Rp], writes=[Rp])
        P.op("dve", L("scalar_tensor_tensor", out=mcol[:, 32:40], in0=mcol[:, 8:16], scalar=1.0, in1=pcol[:, 56:64],
                      op0=ALU.add, op1=ALU.mult), reads=[Rp], writes=[Rp])
        snap_setup = P.snapshot()

        dbg_list = []

        def norm_A1(t, par):
            P.op("act", L("activation", xnb[par][:], X[:, t, :], AF.Square, accum_out=ncol[par][:, 0:1]),
                 reads=[RX[t]], writes=[Rxn[par], Rnc[par]])
            P.op("act", L("activation", ncol[par][:, 1:2], ncol[par][:, 0:1], AF.Ln, bias=EPS, scale=1.0 / D),
                 reads=[Rnc[par]], writes=[Rnc[par]])
            P.op("act", L("activation", ncol[par][:, 2:3], ncol[par][:, 1:2], AF.Exp, scale=-0.5),
                 reads=[Rnc[par]], writes=[Rnc[par]])
            P.op("dve", L("tensor_scalar", xnb[par][:], X[:, t, :], ncol[par][:, 2:3], None, op0=ALU.mult),
                 reads=[RX[t], Rnc[par]], writes=[Rxn[par]])

        def norm_A2(t, par):
            bk = 6 + par
            pv = bkbf(bk)
            for cch in range(8):
                P.op("pe", L("transpose", pv[:, cch * 128:(cch + 1) * 128], xnb[par][:, cch * 128:(cch + 1) * 128], ident_bf[:]),
                     reads=[Rxn[par], Rc], writes=[RB[bk]])

        def norm_B(t, par, acol, shcol):
            bk = 6 + par
            pv = bkbf(bk)
            for cch in range(4):
                P.op("act", L("activation", hT[:, cch, t * 128:(t + 1) * 128], pv[:, cch * 128:(cch + 1) * 128], AF.Identity,
                              bias=mcol[:, shcol + cch:shcol + cch + 1], scale=mcol[:, acol + cch:acol + cch + 1]),
                     reads=[RB[bk], Rp, Rm2], writes=[RHa[t]])
            for cch in range(4, 8):
                P.op("dve", L("tensor_scalar", hT[:, cch, t * 128:(t + 1) * 128], pv[:, cch * 128:(cch + 1) * 128],
                              mcol[:, acol + cch:acol + cch + 1], mcol[:, shcol + cch:shcol + cch + 1], op0=ALU.mult, op1=ALU.add),
                     reads=[RB[bk], Rp, Rm2], writes=[RHb[t]])

        def norm_stream(acol, shcol):
            p1 = None
            p2 = None
            while True:
                t = yield
                if t is not None:
                    norm_A1(t, t % 2)
                if p1 is not None:
                    norm_A2(p1, p1 % 2)
                if p2 is not None:
                    norm_B(p2, p2 % 2, acol, shcol)
                p2 = p1
                p1 = t

        def norm_all(acol, shcol):
            ns = norm_stream(acol, shcol)
            next(ns)
            for t in range(NT):
                ns.send(t)
            ns.send(None)
            ns.send(None)

        bkrr = dict(i=0)

        def nbank(lst=(0, 1, 2, 3)):
            b = lst[bkrr["i"] % len(lst)]
            bkrr["i"] += 1
            return b

        evrr = dict(i=0)

        def chk(n):
            if stage == n:
                raise _Stop()

        carry = dict(ns=None, pend=[])

        def body():
          for blk in range(NB):
            tok0 = blk * TB
            chk(1)
            if blk == 0:
                norm_all(32, 0)
                P.apply(snap_setup)
            chk(2)
            wsl, wres = wnext()
            wv = wsl[:, 0:4096].rearrange("p (c n) -> p c n", c=8)
            for t in range(NT):
                if carry["pend"] and t >= 4:
                    carry["ns"].send(carry["pend"].pop(0))
                    if t == 4:
                        carry["ns"].send(carry["pend"].pop(0))
                bk = nbank()
                for cch in range(8):
                    P.op("pe", L("matmul", banks[bk][:, :], lhsT=hT[:, cch, t * 128:(t + 1) * 128], rhs=wv[:, cch, :],
                                 start=(cch == 0), stop=(cch == 7)), reads=[wres, RHa[t], RHb[t]], writes=[RB[bk]])
                P.op("act", L("activation", zs[:, t, :], banks[bk][:, :], AF.Silu), reads=[RB[bk]], writes=[RZ[t]])
                bk = nbank()
                for cch in range(8):
                    P.op("pe", L("matmul", banks[bk][:, 0:8], lhsT=hT[:, cch, t * 128:(t + 1) * 128], rhs=wlg[:, cch, :],
                                 start=(cch == 0), stop=(cch == 7)), reads=[Rwlg, RHa[t], RHb[t]], writes=[RB[bk]])
                P.op("dve", L("tensor_copy", lg[:, t, :], banks[bk][:, 0:8]), reads=[RB[bk]], writes=[RLG[t]])
            wrel(1)
            w4_list = [(f, k) for f in range(12) for k in range(4)]

            def w4_build(i0, i1):
                for (f, k) in w4_list[i0:i1]:
                    if (f * 4 + k) % 2 == 0:
                        P.op("dve", L("tensor_scalar", W4[:, f, k, :], ident32[:], cw4c[:, k * 12 + f:k * 12 + f + 1], None, op0=ALU.mult),
                             reads=[Rc, Rp], writes=[RW4[f][k]])
                    else:
                        P.op("act", L("activation", W4[:, f, k, :], ident32[:], AF.Copy, scale=cw4c[:, k * 12 + f:k * 12 + f + 1]),
                             reads=[Rc, Rp], writes=[RW4[f][k]])

            wa, wares = wnext()
            wg, wgres = wnext()
            wav = wa[:, 0:4096].rearrange("p (c n) -> p c n", c=8)
            wgv = wg[:, 0:4096].rearrange("p (c n) -> p c n", c=8)
            for cb in range(4):
                P.op("pool", L("tensor_copy", u_v[:, cb, 0:32], utail[:, cb, :]), reads=[Rtail], writes=[RU[cb]])
                for s in range(NS):
                    bka = nbank()
                    for cch in range(8):
                        P.op("pe", L("matmul", banks[bka][:, :], lhsT=wav[:, cch, cb * 128:(cb + 1) * 128],
                                     rhs=hT[:, cch, s * 512:(s + 1) * 512], start=(cch == 0), stop=(cch == 7)),
                             reads=[wares] + (RHa if cch < 4 else RHb)[4 * s:4 * s + 4], writes=[RB[bka]])
                    bkg = nbank()
                    for cch in range(8):
                        P.op("pe", L("matmul", banks[bkg][:, :], lhsT=wgv[:, cch, cb * 128:(cb + 1) * 128],
                                     rhs=hT[:, cch, s * 512:(s + 1) * 512], start=(cch == 0), stop=(cch == 7)),
                             reads=[wgres] + (RHa if cch < 4 else RHb)[4 * s:4 * s + 4], writes=[RB[bkg]])
                    par = s % 2
                    P.op("act", L("activation", tmpf[par][:], banks[bkg][:, :], AF.Tanh, scale=0.5), reads=[RB[bkg]], writes=[Rtf[par]])
                    P.op("dve", L("scalar_tensor_tensor", out=u_v[:, cb, 32 + s * 512:32 + (s + 1) * 512], in0=tmpf[par][:], scalar=1.0,
                                  in1=banks[bka][:, :], op0=ALU.add, op1=ALU.mult), reads=[Rtf[par], RB[bka]], writes=[RU[cb]])
                    w4_build((cb * NS + s) * 6, (cb * NS + s + 1) * 6)
                P.op("pool", L("tensor_copy", utail[:, cb, :], u_v[:, cb, TB:TB + 32]), reads=[RU[cb]], writes=[Rtail])
            wrel(2)
            chk(4)
            Q = lambda q: scb[:, q, :]
            Q3 = lambda q: scb[:, q, :].rearrange("p (t h) -> p t h", h=4)
            S1 = lambda q, t, h: scb[:, q, t * 4 + h:t * 4 + h + 1]
            S4 = lambda q, t: scb[:, q, t * 4:(t + 1) * 4]
            f4 = lambda v: v[:].rearrange("p h t -> p (h t)")
            bc4 = lambda m: m.unsqueeze(1).to_broadcast([128, 4, 128])

            def gdn_scalars():
                P.op("dve", L("tensor_tensor", out=Q3(0), in0=lg[:, :, 4:8], in1=dtb_bc[:].unsqueeze(1).to_broadcast([128, NT, 4]), op=ALU.add),
                     reads=RLG + [Rbc], writes=[Rsc])
                P.op("act", L("activation", Q(0), Q(0), AF.Exp), reads=[Rsc], writes=[Rsc])
                P.op("act", L("activation", Q(0), Q(0), AF.Ln, bias=1.0), reads=[Rsc], writes=[Rsc])
                P.op("dve", L("tensor_tensor", out=Q3(0), in0=Q3(0), in1=abc[:].unsqueeze(1).to_broadcast([128, NT, 4]), op=ALU.mult), reads=[Rsc, Rbc], writes=[Rsc])
                P.op("act", L("activation", Q3(1), lg[:, :, 0:4], AF.Exp, scale=-1.0), reads=RLG + [Rsc], writes=[Rsc])
                P.op("dve", L("tensor_scalar", Q(1), Q(1), 1.0, None, op0=ALU.add), reads=[Rsc], writes=[Rsc])
                P.op("dve", L("reciprocal", Q(1), Q(1)), reads=[Rsc], writes=[Rsc])
                for t in range(NT):
                    sq = tmpf[t % 2][:].bitcast(BF16)
                    qsq_v = sq[:, 0:512].rearrange("p (h t) -> p h t", h=4)
                    ksq_v = sq[:, 512:1024].rearrange("p (h t) -> p h t", h=4)
                    s_ = t // 4
                    tsl = slice(t * 128, (t + 1) * 128)
                    P.op("dve", L("tensor_tensor", out=qsq_v, in0=qfm[:, :, tsl], in1=qfm[:, :, tsl], op=ALU.mult),
                         reads=[RQ[h][t] for h in range(4)], writes=[Rtf[t % 2]])
                    P.op("dve", L("tensor_tensor", out=ksq_v, in0=kfm[:, :, tsl], in1=kfm[:, :, tsl], op=ALU.mult),
                         reads=[RK[h][t] for h in range(4)], writes=[Rtf[t % 2]])
                    for h in range(4):
                        P.op("pe", L("matmul", banks[4][:, t * 4 + h:t * 4 + h + 1], lhsT=qsq_v[:, h, :], rhs=ones_bf[:, 0:1], start=True, stop=True),
                             reads=[Rtf[t % 2], Rc], writes=[RB[4]])
                        P.op("pe", L("matmul", banks[4][:, 32 + t * 4 + h:32 + t * 4 + h + 1], lhsT=ksq_v[:, h, :], rhs=ones_bf[:, 0:1], start=True, stop=True),
                             reads=[Rtf[t % 2], Rc], writes=[RB[4]])
                P.op("pe", L("matmul", banks[4][:, 64:96], lhsT=tri32[:], rhs=Q(0), start=True, stop=True), reads=[Rc, Rsc], writes=[RB[4]])
                P.op("pe", L("matmul", banks[4][:, 96:128], lhsT=ones32[:], rhs=Q(0), start=True, stop=True), reads=[Rc, Rsc], writes=[RB[4]])
                P.op("act", L("activation", Q(13), banks[4][:, 32:64], AF.Ln, bias=EPS), reads=[RB[4], Rsc], writes=[Rsc])
                P.op("act", L("activation", Q(2), Q(13), AF.Exp, scale=-0.5), reads=[Rsc], writes=[Rsc])
                P.op("act", L("activation", Q(3), Q(13), AF.Exp, scale=0.5), reads=[Rsc], writes=[Rsc])
                P.op("dve", L("tensor_tensor", out=Q(4), in0=Q(2), in1=Q(2), op=ALU.mult), reads=[Rsc], writes=[Rsc])
                P.op("dve", L("tensor_tensor", out=Q(4), in0=Q(4), in1=Q(1), op=ALU.mult), reads=[Rsc], writes=[Rsc])
                P.op("dve", L("tensor_copy", scb[:, 5:7, :], banks[4][:, 64:128].rearrange("p (a b) -> p a b", a=2)), reads=[RB[4], Rsc], writes=[Rsc])
                P.op("act", L("activation", Q(7), Q(5), AF.Exp), reads=[Rsc], writes=[Rsc])
                P.op("dve", L("tensor_scalar", Q(8), Q(7), -1.0, None, op0=ALU.mult), reads=[Rsc], writes=[Rsc])
                P.op("dve", L("tensor_tensor", out=Q(9), in0=Q(6), in1=Q(5), op=ALU.subtract), reads=[Rsc], writes=[Rsc])
                P.op("act", L("activation", Q(9), Q(9), AF.Exp), reads=[Rsc], writes=[Rsc])
                P.op("dve", L("tensor_tensor", out=Q(10), in0=Q(9), in1=Q(4), op=ALU.mult), reads=[Rsc], writes=[Rsc])
                P.op("act", L("activation", Q(11), Q(6), AF.Exp), reads=[Rsc], writes=[Rsc])
                P.op("dve", L("tensor_scalar", Q(12), banks[4][:, 0:32], EPS, EPS * 128.0, op0=ALU.add, op1=ALU.mult), reads=[RB[4], Rsc], writes=[Rsc])

            kinds = ["q", "k", "v"]
            wcur = {}

            def qkv_A(fidx):
                gi, h = fidx // 4, fidx % 4
                if h == 0:
                    wsl, wres = wnext()
                    wcur["v"] = wsl[:, 0:4096].rearrange("p (c n) -> p c n", c=8)
                    wcur["r"] = wres
                wv, wres = wcur["v"], wcur["r"]
                rs = fidx % 2
                raw = raw_v[rs]
                P.op("pool", L("tensor_copy", raw[:, 0:4], rawtail[:, fidx, :]), reads=[Rtail], writes=[RRAW[rs]])
                for s in range(NS):
                    bk = nbank()
                    for cch in range(8):
                        P.op("pe", L("matmul", banks[bk][:, :], lhsT=wv[:, cch, h * 128:(h + 1) * 128],
                                     rhs=hT[:, cch, s * 512:(s + 1) * 512], start=(cch == 0), stop=(cch == 7)),
                             reads=[wres] + (RHa if cch < 4 else RHb)[4 * s:4 * s + 4], writes=[RB[bk]])
                    if evrr["i"] % 2 == 0:
                        P.op("act", L("copy", raw[:, 4 + s * 512:4 + (s + 1) * 512], banks[bk][:, :]), reads=[RB[bk]], writes=[RRAW[rs]])
                    else:
                        P.op("dve", L("tensor_copy", raw[:, 4 + s * 512:4 + (s + 1) * 512], banks[bk][:, :]), reads=[RB[bk]], writes=[RRAW[rs]])
                    evrr["i"] += 1
                P.op("pool", L("tensor_copy", rawtail[:, fidx, :], raw[:, TB:TB + 4]), reads=[RRAW[rs]], writes=[Rtail])
                if h == 3:
                    wrel(1)

            def qkv_B(fidx):
                gi, h = fidx // 4, fidx % 4
                kind = kinds[gi]
                rs = fidx % 2
                raw = raw_v[rs]
                if kind in ("q", "k"):
                    dst = qfm if kind == "q" else kfm
                    RR = RQ if kind == "q" else RK
                    for s in range(NS):
                        bk = nbank()
                        for k in range(4):
                            P.op("pe", L("matmul", banks[bk][:, :], lhsT=W4[:, fidx, k, :],
                                         rhs=raw[:, s * 512 + 1 + k:s * 512 + 1 + k + 512], start=(k == 0), stop=(k == 3)),
                                 reads=[RW4[fidx][k], RRAW[rs]], writes=[RB[bk]])
                        P.op("act", L("activation", dst[:, h, s * 512:(s + 1) * 512], banks[bk][:, :], AF.Silu),
                             reads=[RB[bk]], writes=RR[h][4 * s:4 * s + 4])
                if kind in ("k", "v"):
                    dst = ktm if kind == "k" else vtm
                    RR = RKT if kind == "k" else RVT
                    for t4 in range(NT // 4):
                        bk = nbank()
                        for tt in range(4):
                            t = t4 * 4 + tt
                            for k in range(4):
                                P.op("pe", L("matmul", banks[bk][:, tt * 128:(tt + 1) * 128],
                                             lhsT=raw[:, t * 128 + 1 + k:t * 128 + 1 + k + 128], rhs=W4[:, fidx, k, :],
                                             start=(k == 0), stop=(k == 3)), reads=[RW4[fidx][k], RRAW[rs]], writes=[RB[bk]])
                        P.op("act", L("activation", dst[:, t4 * 4:t4 * 4 + 4, h * 128:(h + 1) * 128],
                                      banks[bk][:, :].rearrange("p (t f) -> p t f", t=4), AF.Silu),
                             reads=[RB[bk]], writes=RR[t4 * 4:t4 * 4 + 4])
                if fidx == 7:
                    gdn_scalars()

            qkv_A(0)
            for fidx in range(12):
                if fidx + 1 < 12:
                    qkv_A(fidx + 1)
                qkv_B(fidx)
            snap_conv4 = P.snapshot()
            chk(3)
            def build_wc(cb, k0=0, k1=31):
                for k in range(k0, k1):
                    if k % 2 == 0:
                        P.op("dve", L("tensor_scalar", Wc[:, k, :], imp32[:], cw31c[:, k * 4 + cb:k * 4 + cb + 1], None, op0=ALU.mult),
                             reads=[Rc, Rp], writes=[RWck[k]])
                    else:
                        P.op("act", L("activation", Wc[:, k, :], imp32[:], AF.Copy, scale=cw31c[:, k * 4 + cb:k * 4 + cb + 1]),
                             reads=[Rc, Rp], writes=[RWck[k]])

            def gn(cb, s, bk, bk2):
                par = s % 2
                P.op("act", L("activation", tmpb[par][:], banks[bk][:, :], AF.Square, bias=bcen[:, cb:cb + 1]),
                     reads=[RB[bk], Rp], writes=[Rtb[par]])
                P.op("act", L("activation", tmpf[par][:], banks[bk][:, :], AF.Identity, bias=bcen[:, cb:cb + 1]),
                     reads=[RB[bk], Rp], writes=[Rtf[par]])
                yield
                P.op("pe", L("matmul", banks[bk2][:, :], lhsT=gm_bf[:], rhs=tmpb[par][:], start=True, stop=True),
                     reads=[Rc, Rtb[par]], writes=[RB[bk2]])
                yield
                P.op("act", L("activation", tmpg[par], banks[bk2][:, :], AF.Ln, bias=EPS), reads=[RB[bk2]], writes=[Rtg[par]])
                P.op("act", L("activation", tmpg[par], tmpg[par], AF.Exp, scale=-0.5), reads=[Rtg[par]], writes=[Rtg[par]])
                yield
                P.op("dve", L("scalar_tensor_tensor", out=hT[:, cb, s * 512:(s + 1) * 512], in0=tmpf[par][:], scalar=pcol[:, 76 + cb:77 + cb],
                              in1=tmpg[par], op0=ALU.mult, op1=ALU.mult), reads=[Rtf[par], Rtg[par], Rp], writes=RHa[4 * s:4 * s + 4])
                yield

            def conv31gen():
                cbanks = [1, 7]
                ci = 0
                for k0 in range(0, 31, 8):
                    build_wc(0, k0, min(31, k0 + 8))
                    yield
                for cb in range(4):
                    for s in range(NS):
                        bk = cbanks[ci % 2]
                        bk2 = cbanks[(ci + 1) % 2]
                        ci += 1
                        for k in range(31):
                            P.op("pe", L("matmul", banks[bk][:, :], lhsT=Wc[:, k, :], rhs=u_v[:, cb, s * 512 + 2 + k:s * 512 + 2 + k + 512],
                                         start=(k == 0), stop=(k == 30)), reads=[RWck[k], RU[cb]], writes=[RB[bk]])
                            if k % 4 == 3:
                                yield
                        yield
                        if s == NS - 1 and cb + 1 < 4:
                            for k0 in range(0, 31, 8):
                                build_wc(cb + 1, k0, min(31, k0 + 8))
                                yield
                        yield from gn(cb, s, bk, bk2)
            chk(5)
            P.apply(snap_conv4)
            chk(51)

            def prep(t):
                B = CH[t % 2]
                R = B.R
                o4 = t % 4
                s_ = t // 4
                tsl = slice(t * 128, (t + 1) * 128)
                bx, by = B.bx, B.by
                HS = [slice(h * 128, (h + 1) * 128) for h in range(4)]
                for h in range(4):
                    P.op("act", L("activation", B.gtri[:, h, :], tri32[:], AF.Copy, scale=S1(0, t, h)), reads=[Rc, Rsc], writes=[R["gtri"]])
                yield
                for h in range(4):
                    hs = HS[h]
                    P.op("pe", L("matmul", banks[bx][:, hs], lhsT=ones32[:], rhs=B.gtri[:, h, :], start=True, stop=False), reads=[Rc, R["gtri"]], writes=[RB[bx]])
                    P.op("pe", L("matmul", banks[bx][:, hs], lhsT=B.gtri[:, h, :], rhs=nones32[:], start=False, stop=False), reads=[Rc, R["gtri"]], writes=[RB[bx]])
                    P.op("pe", L("matmul", banks[bx][:, hs], lhsT=ident32[:], rhs=nm32[:], start=False, stop=True), reads=[Rc], writes=[RB[bx]])
                    P.op("pe", L("matmul", banks[by][:, hs], lhsT=ones32[:], rhs=B.gtri[:, h, :], start=True, stop=True), reads=[Rc, R["gtri"]], writes=[RB[by]])
                    if h < 3:
                        yield
                P.op("act", L("activation", f4(B.dti), banks[bx][:, :], AF.Exp), reads=[RB[bx]], writes=[R["dti"]])
                P.op("act", L("activation", f4(B.egb), banks[by][:, :], AF.Exp), reads=[RB[by]], writes=[R["egb"]])
                yield
                for h in range(4):
                    P.op("pe", L("matmul", banks[bx][:, HS[h]], lhsT=kfm[:, h, tsl], rhs=kfm[:, h, tsl], start=True, stop=True), reads=[RK[h][t]], writes=[RB[bx]])
                for h in range(4):
                    P.op("pe", L("matmul", banks[by][:, HS[h]], lhsT=kfm[:, h, tsl], rhs=qfm[:, h, tsl], start=True, stop=True), reads=[RK[h][t], RQ[h][t]], writes=[RB[by]])
                for h in range(4):
                    P.op("dve", L("scalar_tensor_tensor", out=B.A0[:, h, :], in0=banks[bx][:, HS[h]], scalar=S1(4, t, h), in1=B.dti[:, h, :],
                                  op0=ALU.mult, op1=ALU.mult), reads=[RB[bx], Rsc, R["dti"]], writes=[R["A0"]])
                P.op("dve", L("tensor_tensor", out=qkdt[o4][:].rearrange("p h t -> p (h t)"), in0=banks[by][:, :], in1=f4(B.dti), op=ALU.mult),
                     reads=[RB[by], R["dti"]], writes=[RQK[o4]])
                P.op("dve", L("tensor_tensor", out=B.A0[:], in0=B.A0[:], in1=bc4(su_bf[:]), op=ALU.mult), reads=[R["A0"], Rc], writes=[R["A0"]])
                P.op("dve", L("tensor_tensor", out=qfm[:, :, tsl], in0=qfm[:, :, tsl], in1=B.egb[:], op=ALU.mult),
                     reads=[RQ[h][t] for h in range(4)] + [R["egb"]], writes=[RQ[h][t] for h in range(4)])
                yield
                pvy = bkbf(by)
                for h in range(4):
                    P.op("pe", L("transpose", pvy[:, h * 128:(h + 1) * 128], B.A0[:, h, :], ident_bf[:]), reads=[R["A0"], Rc], writes=[RB[by]])
                P.op("act", L("copy", f4(B.A0T), pvy[:, 0:512]), reads=[RB[by]], writes=[R["A0T"]])
                yield
                P.op("dve", L("tensor_tensor", out=B.A1[:], in0=B.A0[:], in1=bc4(mk_bf[:, 0, :]), op=ALU.mult), reads=[R["A0"], Rc], writes=[R["A1"]])
                P.op("dve", L("tensor_tensor", out=B.AT1[:], in0=B.A0T[:], in1=bc4(mk_bf[:, 0, :]), op=ALU.mult), reads=[R["A0T"], Rc], writes=[R["AT1"]])
                P.op("dve", L("tensor_tensor", out=B.Zb[:], in0=bc4(ident_bf[:]), in1=B.A1[:], op=ALU.subtract), reads=[R["A1"], Rc], writes=[R["Zb"]])
                P.op("dve", L("tensor_tensor", out=B.ZTb[:], in0=bc4(ident_bf[:]), in1=B.AT1[:], op=ALU.subtract), reads=[R["AT1"], Rc], writes=[R["ZTb"]])
                yield
                for h in range(4):
                    P.op("pe", L("matmul", banks[bx][:, HS[h]], lhsT=B.AT1[:, h, :], rhs=B.A1[:, h, :], start=True, stop=True), reads=[R["A1"], R["AT1"]], writes=[RB[bx]])
                for h in range(4):
                    P.op("pe", L("matmul", banks[by][:, HS[h]], lhsT=B.A1[:, h, :], rhs=B.AT1[:, h, :], start=True, stop=True), reads=[R["A1"], R["AT1"]], writes=[RB[by]])
                P.op("dve", L("tensor_copy", f4(B.qsq), banks[bx][:, :]), reads=[RB[bx]], writes=[R["qsq"]])
                P.op("act", L("copy", f4(B.ksq), banks[by][:, :]), reads=[RB[by]], writes=[R["ksq"]])
                yield

                def zup(lz, rlz, rz, rrz, lzt, rlzt, rzt, rrzt, final=False):
                    for h in range(4):
                        P.op("pe", L("matmul", banks[bx][:, HS[h]], lhsT=ident_bf[:], rhs=B.Zb[:, h, :], start=True, stop=False), reads=[Rc, R["Zb"]], writes=[RB[bx]])
                        P.op("pe", L("matmul", banks[bx][:, HS[h]], lhsT=lz[:, h, :], rhs=rz[:, h, :], start=False, stop=True), reads=[rlz, rrz], writes=[RB[bx]])
                    if not final:
                        for h in range(4):
                            P.op("pe", L("matmul", banks[by][:, HS[h]], lhsT=ident_bf[:], rhs=B.ZTb[:, h, :], start=True, stop=False), reads=[Rc, R["ZTb"]], writes=[RB[by]])
                            P.op("pe", L("matmul", banks[by][:, HS[h]], lhsT=lzt[:, h, :], rhs=rzt[:, h, :], start=False, stop=True), reads=[rlzt, rrzt], writes=[RB[by]])
                        P.op("dve", L("tensor_copy", f4(B.Zb), banks[bx][:, :]), reads=[RB[bx]], writes=[R["Zb"]])
                        P.op("act", L("copy", f4(B.ZTb), banks[by][:, :]), reads=[RB[by]], writes=[R["ZTb"]])
                    else:
                        P.op("dve", L("tensor_copy", f4(Zfin[o4]), banks[bx][:, :]), reads=[RB[bx]], writes=[RZF[o4]])

                zup(B.ksq, R["ksq"], B.Zb, R["Zb"], B.Zb, R["Zb"], B.ksq, R["ksq"])
                yield
                for h in range(4):
                    P.op("pe", L("matmul", banks[by][:, HS[h]], lhsT=B.qsq[:, h, :], rhs=B.ksq[:, h, :], start=True, stop=True), reads=[R["qsq"], R["ksq"]], writes=[RB[by]])
                P.op("act", L("copy", f4(B.ksq), banks[by][:, :]), reads=[RB[by]], writes=[R["ksq"]])
                yield
                zup(B.ksq, R["ksq"], B.Zb, R["Zb"], B.Zb, R["Zb"], B.ksq, R["ksq"])

                def aoff(lvl):
                    P.op("pool", L("tensor_tensor", out=B.AT1[:], in0=B.A0T[:], in1=bc4(mk_bf[:, 1 + lvl, :]), op=ALU.mult), reads=[R["A0T"], Rc], writes=[R["AT1"]])
                aoff(0)
                yield
                for lvl in range(4):
                    for h in range(4):
                        P.op("pe", L("matmul", banks[bx][:, HS[h]], lhsT=B.AT1[:, h, :], rhs=B.Zb[:, h, :], start=True, stop=True), reads=[R["AT1"], R["Zb"]], writes=[RB[bx]])
                    P.op("act", L("activation", f4(B.A1), banks[bx][:, :], AF.Identity, scale=-1.0), reads=[RB[bx]], writes=[R["A1"]])
                    yield
                    zup(B.ZTb, R["ZTb"], B.A1, R["A1"], B.A1, R["A1"], B.ZTb, R["ZTb"], final=(lvl == 3))
                    if lvl < 3:
                        aoff(lvl + 1)
                    yield

            def rec(t):
                o4 = t % 4
                s_ = t // 4
                tsl = slice(t * 128, (t + 1) * 128)
                HS = [slice(h * 128, (h + 1) * 128) for h in range(4)]
                P.op("dve", L("tensor_tensor", out=Vr[:], in0=vtm[:, t, :].rearrange("p (h f) -> p h f", h=4),
                               in1=S4(3, t).unsqueeze(2).to_broadcast([128, 4, 128]), op=ALU.mult), reads=[RVT[t], Rsc], writes=[Rg["Vr"]])
                P.op("pool", L("tensor_tensor", out=wz[:], in0=zs[:, t, :].rearrange("p (h f) -> p h f", h=4),
                               in1=gnw_bc[:].unsqueeze(1).to_broadcast([128, 4, 128]), op=ALU.mult), reads=[RZ[t], Rbc], writes=[Rg["wz"]])
                for h in range(4):
                    P.op("pe", L("matmul", banks[2][:, HS[h]], lhsT=kfm[:, h, tsl], rhs=Sbf[:, h, :], start=True, stop=True),
                         reads=[RK[h][t], RSB[h]], writes=[RB[2]])
                yield
                for h in range(4):
                    P.op("dve", L("scalar_tensor_tensor", out=rhs2[:, h, :], in0=banks[2][:, HS[h]], scalar=S1(8, t, h), in1=Vr[:, h, :],
                                  op0=ALU.mult, op1=ALU.add), reads=[RB[2], Rsc, Rg["Vr"]], writes=[Rrhs2[h]])
                yield
                for h in range(4):
                    P.op("pe", L("matmul", banks[2][:, HS[h]], lhsT=Zfin[o4][:, h, :], rhs=rhs2[:, h, :], start=True, stop=True),
                         reads=[RZF[o4], Rrhs2[h]], writes=[RB[2]])
                yield
                yp3 = banks[2][:, :].rearrange("p (h t) -> p h t", h=4)
                P.op("dve", L("tensor_tensor", out=Ykbf[:], in0=yp3, in1=S4(10, t).unsqueeze(2).to_broadcast([128, 4, 128]), op=ALU.mult),
                     reads=[RB[2], Rsc], writes=[Rg["Yk"]])
                P.op("dve", L("tensor_tensor", out=Ybf[:], in0=yp3, in1=S4(4, t).unsqueeze(2).to_broadcast([128, 4, 128]), op=ALU.mult),
                     reads=[RB[2], Rsc], writes=[Rg["Y"]])
                yield
                for h in range(4):
                    P.op("pe", L("matmul", banks[2][:, HS[h]], lhsT=ktm[:, t, HS[h]], rhs=Ykbf[:, h, :], start=True, stop=True),
                         reads=[RKT[t], Rg["Yk"]], writes=[RB[2]])
                for h in range(4):
                    P.op("pe", L("matmul", banks[6][:, HS[h]], lhsT=qfm[:, h, tsl], rhs=Sbf[:, h, :], start=True, stop=False),
                         reads=[RQ[h][t], RSB[h]], writes=[RB[6]])
                    P.op("pe", L("matmul", banks[6][:, HS[h]], lhsT=qkdt[o4][:, h, :], rhs=Ybf[:, h, :], start=False, stop=True),
                         reads=[RQK[o4], Rg["Y"]], writes=[RB[6]])
                yield
                for h in range(4):
                    P.op("dve", L("scalar_tensor_tensor", out=Sbf[:, h, :], in0=S32[:, h, :], scalar=S1(11, t, h), in1=banks[2][:, HS[h]],
                                  op0=ALU.mult, op1=ALU.add), reads=[RB[2], Rsc, RS[h]], writes=[RSB[h]])
                for h in range(4):
                    P.op("dve", L("scalar_tensor_tensor", out=S32[:, h, :], in0=S32[:, h, :], scalar=S1(11, t, h), in1=banks[2][:, HS[h]],
                                  op0=ALU.mult, op1=ALU.add), reads=[RB[2], Rsc, RS[h]], writes=[RS[h]])
                for h in range(4):
                    P.op("act", L("activation", junk[:], banks[6][:, HS[h]], AF.Square, scale=128.0 ** -0.5, accum_out=S1(14, t, h)),
                         reads=[RB[6]], writes=[Rjunk, Rms[t % 2]])
                yield
                P.op("dve", L("tensor_tensor", out=S4(15, t), in0=S4(14, t), in1=S4(12, t), op=ALU.add), reads=[Rsc, Rms[t % 2]], writes=[Rms[t % 2]])
                yield
                P.op("act", L("activation", S4(15, t), S4(15, t), AF.Ln), reads=[Rms[t % 2]], writes=[Rms[t % 2]])
                P.op("act", L("activation", S4(15, t), S4(15, t), AF.Exp, scale=-0.5), reads=[Rms[t % 2]], writes=[Rms[t % 2]])
                yield
                for h in range(4):
                    P.op("dve", L("scalar_tensor_tensor", out=outb[:, h, :], in0=banks[6][:, HS[h]], scalar=S1(15, t, h), in1=wz[:, h, :],
                                  op0=ALU.mult, op1=ALU.mult), reads=[RB[6], Rms[t % 2], Rg["wz"]], writes=[Rg["outb"]])
                yield
                pv2 = bkbf(2)
                for h in range(4):
                    P.op("pe", L("transpose", pv2[:, h * 128:(h + 1) * 128], outb[:, h, :], ident_bf[:]), reads=[Rg["outb"], Rc], writes=[RB[2]])
                P.op("act", L("copy", hT[:, 4:8, tsl], pv2[:, 0:512].rearrange("p (h t) -> p h t", h=4)), reads=[RB[2]], writes=[RHb[t]])
                yield

            def late_gen(i):
                pc, kind_, idx, boff = LATE[i // 2]
                half = i % 2
                lsl, lres = wnext()
                lv = lsl[:, 0:4096].rearrange("p (c n) -> p c n", c=8)
                bk = 6
                if kind_ == "col":
                    for fb4 in range(4):
                        for cch in range(8):
                            P.op("pe", L("matmul", banks[bk][:, fb4:fb4 + 1], lhsT=lv[:, cch, fb4 * 128:(fb4 + 1) * 128],
                                         rhs=cact_bf[:, cch:cch + 1], start=(cch == 0), stop=(cch == 7)),
                                 reads=[lres, Rp], writes=[RB[bk]])
                    P.op("dve", L("tensor_tensor", out=mcol[:, idx * 8 + half * 4:idx * 8 + half * 4 + 4], in0=banks[bk][:, 0:4],
                                  in1=pcol[:, boff + half * 4:boff + half * 4 + 4], op=ALU.add), reads=[RB[bk], Rp, Rm2], writes=[Rm2])
                else:
                    for cch in range(8):
                        P.op("pe", L("matmul", banks[bk][:, :], lhsT=cactbc[:, cch, :], rhs=lv[:, cch, :],
                                     start=(cch == 0), stop=(cch == 7)), reads=[lres, Rp], writes=[RB[bk]])
                    P.op("dve", L("tensor_tensor", out=gt_bc[:, idx, half * 512:(half + 1) * 512], in0=banks[bk][:, :],
                                  in1=gt_bc[:, idx, half * 512:(half + 1) * 512], op=ALU.add), reads=[RB[bk], Rbc, Rgt], writes=[Rgt])
                wrel(1)
                if i == 3:
                    P.op("dve", L("scalar_tensor_tensor", out=mcol[:, 40:48], in0=mcol[:, 24:32], scalar=1.0, in1=pcol[:, 64:72],
                                  op0=ALU.add, op1=ALU.mult), reads=[Rp, Rm2], writes=[Rm2])
                yield

            rr_round = [0]

            def rr(gens, bg=None, period=2):
                gens = [g for g in gens if g is not None]
                while gens:
                    for g in list(gens):
                        try:
                            next(g)
                        except StopIteration:
                            gens.remove(g)
                    rr_round[0] += 1
                    if bg is not None:
                        try:
                            next(bg)
                        except StopIteration:
                            bg = None

            def seq(*gs):
                for g in gs:
                    yield from g

            def pair(g0, g1):
                gens = [g0, g1]
                while gens:
                    for g in list(gens):
                        try:
                            next(g)
                        except StopIteration:
                            gens.remove(g)
                    yield

            NU = NT // 2
            cg = conv31gen()
            rr([pair(prep(0), prep(1))], cg)
            for j in range(NU):
                nxt = pair(prep(2 * j + 2), prep(2 * j + 3)) if j + 1 < NU else None
                if blk == 0:
                    rr([seq(rec(2 * j), late_gen(2 * j), rec(2 * j + 1), late_gen(2 * j + 1)), nxt], cg)
                else:
                    rr([seq(rec(2 * j), rec(2 * j + 1)), nxt], cg)
            for _ in cg:
                pass
            for cb in range(4):
                P.op("act", L("activation", hT[:, cb, :], hT[:, cb, :], AF.Silu, bias=pcol[:, 80 + cb:81 + cb]), reads=RHa + [Rp], writes=RHa)
            snap_gdn = P.snapshot()
            chk(6)
            ns2 = norm_stream(40, 16)
            next(ns2)
            w0, w0r = wnext()
            w1, w1r = wnext()
            wo = [w0[:, 0:4096].rearrange("p (c n) -> p c n", c=4), w1[:, 0:4096].rearrange("p (c n) -> p c n", c=4)]
            wor = [w0r, w1r]
            for t in range(NT):
                if t >= 2:
                    ns2.send(t - 2)
                for nh in range(2):
                    bk = nbank()
                    for kc in range(8):
                        P.op("pe", L("matmul", banks[bk][:, :], lhsT=hT[:, kc, t * 128:(t + 1) * 128], rhs=wo[kc // 4][:, kc % 4, nh * 512:(nh + 1) * 512],
                                     start=(kc == 0), stop=(kc == 7)), reads=[(RHa if kc < 4 else RHb)[t], wor[kc // 4]], writes=[RB[bk]])
                    par = nh
                    P.op("dve", L("tensor_tensor", out=tmpf[par][:], in0=banks[bk][:, :], in1=gt_bc[:, 0, nh * 512:(nh + 1) * 512], op=ALU.mult),
                         reads=[RB[bk], Rbc, Rgt], writes=[Rtf[par]])
                    P.op("dve", L("tensor_tensor", out=X[:, t, nh * 512:(nh + 1) * 512], in0=X[:, t, nh * 512:(nh + 1) * 512], in1=tmpf[par][:], op=ALU.add),
                         reads=[Rtf[par], RX[t]], writes=[RX[t]])
            ns2.send(NT - 2)
            ns2.send(NT - 1)
            ns2.send(None)
            ns2.send(None)
            wrel(2)
            P.apply(snap_gdn)
            chk(7)
            wfo_v = wfo_d.rearrange("(c p) n -> p c n", p=128)
            def wfo_load(i):
                P.dma("pool", [L("dma_start", out=wfo[:, 4 * i:min(22, 4 * i + 4), :], in_=wfo_v[:, 4 * i:min(22, 4 * i + 4), :])],
                      RWFO[i], writes=[RWFO[i]])
            wfo_at = {0: 0, 2: 1, 4: 2, 6: 3, 8: 4, 9: 5}
            for u_ in range(11):
                wsl_, wr_ = wnext()
                wgv = wsl_[:, 0:2048].rearrange("p (c n) -> p c n", c=8)
                wuv = wsl_[:, 2048:4096].rearrange("p (c n) -> p c n", c=8)
                for fb in range(2):
                    j = u_ * 2 + fb
                    for s in range(NS):
                        bkg = nbank()
                        for cch in range(8):
                            P.op("pe", L("matmul", banks[bkg][:, :], lhsT=wgv[:, cch, fb * 128:(fb + 1) * 128], rhs=hT[:, cch, s * 512:(s + 1) * 512],
                                         start=(cch == 0), stop=(cch == 7)), reads=[wr_] + (RHa if cch < 4 else RHb)[4 * s:4 * s + 4], writes=[RB[bkg]])
                        bku = nbank()
                        for cch in range(8):
                            P.op("pe", L("matmul", banks[bku][:, :], lhsT=wuv[:, cch, fb * 128:(fb + 1) * 128], rhs=hT[:, cch, s * 512:(s + 1) * 512],
                                         start=(cch == 0), stop=(cch == 7)), reads=[wr_] + (RHa if cch < 4 else RHb)[4 * s:4 * s + 4], writes=[RB[bku]])
                        par = s % 2
                        P.op("act", L("activation", tmpb[par][:], banks[bkg][:, :], AF.Silu), reads=[RB[bkg]], writes=[Rtb[par]])
                        P.op("dve", L("tensor_tensor", out=actT[:, j, s * 512:(s + 1) * 512], in0=banks[bku][:, :], in1=tmpb[par][:], op=ALU.mult),
                             reads=[RB[bku], Rtb[par]], writes=[RACT[j]])
                wrel(1)
                if u_ in wfo_at:
                    wfo_load(wfo_at[u_])
            chk(8)
            ns1 = norm_stream(32, 0)
            next(ns1)
            for t in range(NT):
                if blk + 1 < NB and t >= 2:
                    ns1.send(t - 2)
                for nh in range(2):
                    bk = nbank()
                    for kc in range(22):
                        P.op("pe", L("matmul", banks[bk][:, :], lhsT=actT[:, kc, t * 128:(t + 1) * 128], rhs=wfo[:, kc, nh * 512:(nh + 1) * 512],
                                     start=(kc == 0), stop=(kc == 21)), reads=[RACT[kc], RWFO[kc // 4]], writes=[RB[bk]])
                    if t == NT - 1 and nh == 1:
                        snap_ffn = P.snapshot()
                    par = nh
                    P.op("dve", L("tensor_tensor", out=tmpf[par][:], in0=banks[bk][:, :], in1=gt_bc[:, 1, nh * 512:(nh + 1) * 512], op=ALU.mult),
                         reads=[RB[bk], Rbc, Rgt], writes=[Rtf[par]])
                    P.op("dve", L("tensor_tensor", out=X[:, t, nh * 512:(nh + 1) * 512], in0=X[:, t, nh * 512:(nh + 1) * 512], in1=tmpf[par][:], op=ALU.add),
                         reads=[Rtf[par], RX[t]], writes=[RX[t]])
                par = t % 2
                P.op("act", L("activation", tmpf[par][:].bitcast(BF16), X[:, t, :], AF.Square, accum_out=ncol[par][:, 0:1]), reads=[RX[t]], writes=[Rtf[par], Rnc[par]])
                P.op("act", L("activation", ncol[par][:, 1:2], ncol[par][:, 0:1], AF.Ln, bias=EPS, scale=1.0 / D), reads=[Rnc[par]], writes=[Rnc[par]])
                P.op("act", L("activation", ncol[par][:, 2:3], ncol[par][:, 1:2], AF.Exp, scale=-0.5), reads=[Rnc[par]], writes=[Rnc[par]])
                P.op("dve", L("scalar_tensor_tensor", out=X[:, t, :], in0=X[:, t, :], scalar=ncol[par][:, 2:3], in1=nfw_bc[:],
                              op0=ALU.mult, op1=ALU.mult), reads=[RX[t], Rnc[par], Rbc], writes=[RX[t]])
                P.dma("sp", [L("dma_start", out=y_d[tok0 + t * 128: tok0 + (t + 1) * 128, :], in_=X[:, t, :])], RX[t], reads=[RX[t]])
                if blk + 1 < NB:
                    ntok = tok0 + TB
                    P.dma("sp", [L("dma_start", out=X[:, t, :], in_=x_d[ntok + t * 128: ntok + (t + 1) * 128, :])], RX[t], writes=[RX[t]])
            if blk + 1 < NB:
                carry["ns"] = ns1
                carry["pend"] = [NT - 2, NT - 1, None, None]
            P.apply(snap_ffn)
        try:
            body()
        except _Stop:
            pass
        finals = [("d", RX[t], RX[t].cnt) for t in range(NT) if RX[t].cnt]
        if debug:
            P.barrier()
            dumps = dict(mk_bf=mk_bf, pcol=pcol, mcol=mcol, gt_bc=gt_bc, cw31c=cw31c, cw4c=cw4c, bcen=bcen, tri32=tri32, nm32=nm32, imp32=imp32,
                         su_bf=su_bf, gm_bf=gm_bf, cact_bf=cact_bf, abc=abc, hT=hT, qfm=qfm, kfm=kfm, ktm=ktm, vtm=vtm, zs=zs, lg=lg,
                         u_v=u_v, S32=S32, X=X, scb=scb, outb=outb, Ybf=Ybf)
            for name in debug:
                t_ = dumps[name]
                ap_ = t_ if isinstance(t_, bass.AP) else t_[:]
                dd = nc.dram_tensor("dbg_" + name, list(ap_.shape), ap_.dtype, kind="ExternalOutput").ap()
                rr = Res("dbg_" + name)
                finals.append(P.dma("sp", [L("dma_start", out=dd, in_=ap_)], rr))
        P.emit(finals)
    return nc


_NC_CACHE = {}


def kernel(x, c, w_ada, b_ada, norm_mix_w, w_in, conv_w, conv_b, conv_gn_w, conv_gn_b,
           gdn_conv_w, gdn_a_log, gdn_dt_bias, gdn_norm_w, w_out, norm_ffn_w,
           w_ffn_in, w_ffn_out, norm_final_w):
    f = lambda a: np.ascontiguousarray(np.asarray(a, dtype=np.float32))
    x = f(x); c = f(c)
    if "nc" not in _NC_CACHE:
        _NC_CACHE["nc"] = build()
    nc = _NC_CACHE["nc"]
    cw31 = f(conv_w)[0].reshape(31, 4, 128).reshape(124, 128)
    cw4 = f(gdn_conv_w)[0].reshape(4, 12, 128).reshape(48, 128)
    shared = {
        "cw31": cw31, "cw4": cw4,
        "brow": f(b_ada).reshape(1, 6144),
        "nfw": f(norm_final_w).reshape(1, D),
        "gnw": f(gdn_norm_w).reshape(1, 128),
        "alog": f(gdn_a_log).reshape(1, 4),
        "dtb": f(gdn_dt_bias).reshape(1, 4),
        "w_ada": f(w_ada)[0], "w_in": f(w_in)[0], "w_out": f(w_out)[0],
        "w_ffn_in": f(w_ffn_in)[0], "w_ffn_out": f(w_ffn_out)[0],
    }
    in_maps = []
    for b in range(8):
        prow = np.concatenate([f(b_ada).reshape(48, 128), c[b].reshape(8, 128), f(norm_mix_w).reshape(8, 128),
                               f(norm_ffn_w).reshape(8, 128), f(conv_b).reshape(4, 128), f(conv_gn_w).reshape(4, 128),
                               f(conv_gn_b).reshape(4, 128)], axis=0)
        m = dict(shared)
        m["x"] = x[b]
        m["prow"] = np.ascontiguousarray(prow)
        in_maps.append(m)
    res = run_bass_kernel_spmd(nc, in_maps, core_ids=list(range(8)))
    return np.stack([r["y"] for r in res.results], axis=0).astype(np.float32)
```
